# Optimizing a Trainium2 kernel written in Bass

```python
import jax, jax.numpy as jnp
from jax import lax
import numpy as np

D_MODEL = 1024
BATCH = 32
SEQ = 256
DEPTH = 2
DEC_BATCH = 8
DEC_SEQ = 1024
PAST_LEN = 256

GRID_W = 64
N_DIR = 2
H_GLA = 4
DK_GLA = 128
DV_GLA = 256
GLA_RANK = 16
GLA_TAU = 16.0
GLA_CHUNK = 16
H_MLSTM = 4
DH_MLSTM = 256
MLSTM_CHUNK = 64
CONV_W = 3
D_FF = 2816
N_MOD = 9
LN_EPS = 1e-5
NORM_EPS = 1e-6
QK_GLA = H_GLA * DK_GLA
V_GLA = H_GLA * DV_GLA
W_MLSTM = H_MLSTM * DH_MLSTM
IN_SIZES = (QK_GLA, QK_GLA, V_GLA, V_GLA, GLA_RANK, GLA_RANK,
            2 * W_MLSTM, W_MLSTM, W_MLSTM, H_MLSTM, H_MLSTM, H_MLSTM, H_MLSTM,
            D_MODEL, D_MODEL)
D_IN = sum(IN_SIZES)

kernel_name = 'hybrid_gla_mlstm_diffusion_step'


def _layer_norm(x, g, b):
    xf = x.astype(jnp.float32)
    mu = xf.mean(-1, keepdims=True)
    var = jnp.square(xf - mu).mean(-1, keepdims=True)
    return ((xf - mu) * lax.rsqrt(var + LN_EPS) * g + b).astype(x.dtype)


def _head_rms(x, g):
    xf = x.astype(jnp.float32)
    return xf * lax.rsqrt(jnp.mean(jnp.square(xf), -1, keepdims=True) + NORM_EPS) * g


def _heads(t, n_heads):
    b, t_len, ch = t.shape
    return t.reshape(b, t_len, n_heads, ch // n_heads).transpose(0, 2, 1, 3)


def _merge_heads(t):
    b, h, t_len, d = t.shape
    return t.transpose(0, 2, 1, 3).reshape(b, t_len, h * d)


def _to_chunks(t, chunk):
    b, h, t_len = t.shape[:3]
    t = t.reshape((b, h, t_len // chunk, chunk) + t.shape[3:])
    return jnp.moveaxis(t, 2, 0)


def _from_chunks(t):
    t = jnp.moveaxis(t, 0, 2)
    b, h, n, c = t.shape[:4]
    return t.reshape((b, h, n * c) + t.shape[4:])


def _gla_scan(q, k, v, log_a, s0):
    f32 = jnp.float32
    mask = jnp.tril(jnp.ones((GLA_CHUNK, GLA_CHUNK), bool))

    def step(s, inp):
        qc, kc, vc, ac = inp
        cum = jnp.cumsum(ac, axis=-2)
        qd = qc * jnp.exp(cum)
        kd = kc * jnp.exp(-cum)
        att = jnp.where(mask, jnp.einsum('bhtd,bhsd->bhts', qd, kd), 0.0)
        o = jnp.einsum('bhts,bhsv->bhtv', att, vc) + jnp.einsum('bhtd,bhdv->bhtv', qd, s)
        cum_end = cum[..., -1:, :]
        s_new = jnp.exp(cum_end[..., 0, :])[..., None] * s + jnp.einsum('bhsd,bhsv->bhdv', kc * jnp.exp(cum_end - cum), vc)
        return s_new, o

    xs = tuple(_to_chunks(t.astype(f32), GLA_CHUNK) for t in (q, k, v, log_a))
    s_fin, o = lax.scan(step, s0.astype(f32), xs)
    return _from_chunks(o), s_fin


def _mlstm_scan(q, k, v, i_pre, log_f, c0, n0, m0):
    f32 = jnp.float32
    mask = jnp.tril(jnp.ones((MLSTM_CHUNK, MLSTM_CHUNK), bool))

    def step(carry, inp):
        c, n, m = carry
        qc, kc, vc, ic, fc = inp
        cum = jnp.cumsum(fc, axis=-1)
        log_d = jnp.where(mask, cum[..., :, None] - cum[..., None, :] + ic[..., None, :], -jnp.inf)
        log_inter = cum + m[..., None]
        m_t = jnp.maximum(log_inter, log_d.max(-1))
        d = jnp.exp(log_d - m_t[..., None])
        inter = jnp.exp(log_inter - m_t)
        s = jnp.einsum('bhtd,bhsd->bhts', qc, kc) * d
        num = jnp.einsum('bhts,bhsv->bhtv', s, vc) + inter[..., None] * jnp.einsum('bhtd,bhdv->bhtv', qc, c)
        den = s.sum(-1) + inter * jnp.einsum('bhtd,bhd->bht', qc, n)
        h = num / jnp.maximum(jnp.abs(den), jnp.exp(-m_t))[..., None]
        cum_end = cum[..., -1]
        log_w = cum_end[..., None] - cum + ic
        m_new = jnp.maximum(cum_end + m, log_w.max(-1))
        w = jnp.exp(log_w - m_new[..., None])
        decay = jnp.exp(cum_end + m - m_new)
        c_new = decay[..., None, None] * c + jnp.einsum('bhsd,bhsv->bhdv', kc * w[..., None], vc)
        n_new = decay[..., None] * n + jnp.einsum('bhs,bhsd->bhd', w, kc)
        return (c_new, n_new, m_new), h

    xs = tuple(_to_chunks(t.astype(f32), MLSTM_CHUNK) for t in (q, k, v, i_pre, log_f))
    carry, h = lax.scan(step, (c0.astype(f32), n0.astype(f32), m0.astype(f32)), xs)
    return _from_chunks(h), carry


def _centred_conv(x, w, b):
    t_len = x.shape[1]
    pad = CONV_W // 2
    xp = jnp.pad(x, ((0, 0), (pad, pad), (0, 0)))
    return sum(xp[:, j:j + t_len] * w[j] for j in range(CONV_W)) + b


def _swiglu(h, w_gate, w_up, w_down):
    return (jax.nn.silu(h @ w_gate) * (h @ w_up)) @ w_down


def _grid_pos_embed(t_len, dtype):
    rows = t_len // GRID_W
    r = jnp.repeat(jnp.arange(rows), GRID_W).astype(jnp.float32)
    col = jnp.tile(jnp.arange(GRID_W), rows).astype(jnp.float32)
    nf = D_MODEL // 4
    omega = 1.0 / (10000.0 ** (jnp.arange(nf, dtype=jnp.float32) / nf))
    er = r[:, None] * omega
    ec = col[:, None] * omega
    return jnp.concatenate([jnp.sin(er), jnp.cos(er), jnp.sin(ec), jnp.cos(ec)], axis=-1).astype(dtype)


def _mixer(h, st, l, p):
    s_gla, c_m, n_m, m_m = st
    z = h @ p['w_in'][l]
    points = [int(v) for v in np.cumsum(IN_SIZES)[:-1]]
    (q_g, k_g, v_g, r_g, a_f, a_b, qk_m, v_m, o_m,
     i_f, f_f, i_b, f_b, g_gla, g_mlstm) = jnp.split(z, points, axis=-1)
    flip = lambda t: jnp.flip(t, axis=2)
    to_bht = lambda t: jnp.moveaxis(t, -1, 1)

    q_g = _heads(q_g, H_GLA) * DK_GLA ** -0.5
    k_g = _heads(k_g, H_GLA)
    v_g = _heads(v_g, H_GLA)
    la_f = _heads(jax.nn.log_sigmoid(a_f @ p['w_decay'][l, 0] + p['b_decay'][l, 0]) / GLA_TAU, H_GLA)
    la_b = _heads(jax.nn.log_sigmoid(a_b @ p['w_decay'][l, 1] + p['b_decay'][l, 1]) / GLA_TAU, H_GLA)
    o_f, s_f = _gla_scan(q_g, k_g, v_g, la_f, s_gla[:, 0])
    o_b, s_b = _gla_scan(flip(q_g), flip(k_g), flip(v_g), flip(la_b), s_gla[:, 1])
    o_g = _merge_heads(_head_rms(o_f + flip(o_b), p['gla_norm_g'][l])).astype(h.dtype)
    y_g = (o_g * jax.nn.silu(r_g)) @ p['w_br_gla'][l]

    qk_m = jax.nn.silu(_centred_conv(qk_m, p['w_conv'][l], p['b_conv'][l]))
    q_m, k_m = jnp.split(qk_m, 2, axis=-1)
    q_m = _heads(q_m, H_MLSTM) * DH_MLSTM ** -0.5
    k_m = _heads(k_m, H_MLSTM)
    v_m = _heads(v_m, H_MLSTM)
    lf_f = to_bht(jax.nn.log_sigmoid(f_f + p['f_bias'][l, 0]))
    lf_b = to_bht(jax.nn.log_sigmoid(f_b + p['f_bias'][l, 1]))
    h_f, (c_f, n_f, m_f) = _mlstm_scan(q_m, k_m, v_m, to_bht(i_f), lf_f, c_m[:, 0], n_m[:, 0], m_m[:, 0])
    h_b, (c_b, n_b, m_b) = _mlstm_scan(flip(q_m), flip(k_m), flip(v_m), flip(to_bht(i_b)), flip(lf_b),
                                       c_m[:, 1], n_m[:, 1], m_m[:, 1])
    h_m = _merge_heads(_head_rms(h_f + flip(h_b), p['mlstm_norm_g'][l])).astype(h.dtype)
    y_m = (jax.nn.sigmoid(o_m) * h_m) @ p['w_br_mlstm'][l]

    y = (jax.nn.sigmoid(g_gla) * y_g + jax.nn.sigmoid(g_mlstm) * y_m) @ p['w_out'][l]
    new_st = (jnp.stack([s_f, s_b], axis=1), jnp.stack([c_f, c_b], axis=1),
              jnp.stack([n_f, n_b], axis=1), jnp.stack([m_f, m_b], axis=1))
    return y, new_st


def _layer(x, cond, st, l, p, alpha):
    ada = (jax.nn.silu(cond) @ p['w_ada'][l] + p['b_ada'][l]).reshape(cond.shape[0], N_MOD, 1, D_MODEL)
    sh1, sc1, g1, sh2, sc2, g2, sh3, sc3, g3 = [ada[:, j] for j in range(N_MOD)]
    f1 = _swiglu(x * (1.0 + sc1) + sh1, p['ffn1_w_gate'][l], p['ffn1_w_up'][l], p['ffn1_w_down'][l])
    x = _layer_norm(alpha * x + 0.5 * g1 * f1, p['ln_g'][l, 0], p['ln_b'][l, 0])
    y, new_st = _mixer(x * (1.0 + sc2) + sh2, st, l, p)
    x = _layer_norm(alpha * x + g2 * y, p['ln_g'][l, 1], p['ln_b'][l, 1])
    f2 = _swiglu(x * (1.0 + sc3) + sh3, p['ffn2_w_gate'][l], p['ffn2_w_up'][l], p['ffn2_w_down'][l])
    x = _layer_norm(alpha * x + 0.5 * g3 * f2, p['ln_g'][l, 2], p['ln_b'][l, 2])
    return x, new_st


def setup_inputs(seed: int = 0) -> dict:
    keys = iter(jax.random.split(jax.random.key(seed), 40))

    def nrm(shape, scale):
        return jax.random.normal(next(keys), shape, jnp.float32) * scale

    beta = (8.0 * DEPTH) ** -0.25
    d_in_scale = D_MODEL ** -0.5
    return {
        'x_prompt': nrm((BATCH, SEQ, D_MODEL), 1.0),
        'x_sample': nrm((DEC_BATCH, DEC_SEQ, D_MODEL), 1.0),
        'c': nrm((DEC_BATCH, D_MODEL), 1.0),
        'state_gla_s': nrm((DEC_BATCH, DEPTH, N_DIR, H_GLA, DK_GLA, DV_GLA), 0.5),
        'state_mlstm_c': nrm((DEC_BATCH, DEPTH, N_DIR, H_MLSTM, DH_MLSTM, DH_MLSTM), 0.1),
        'state_mlstm_n': nrm((DEC_BATCH, DEPTH, N_DIR, H_MLSTM, DH_MLSTM), 0.1),
        'state_mlstm_m': nrm((DEC_BATCH, DEPTH, N_DIR, H_MLSTM), 1.0),
        'c_ctx': nrm((D_MODEL,), 1.0),
        'w_ada': nrm((DEPTH, D_MODEL, N_MOD * D_MODEL), 0.5 * d_in_scale),
        'b_ada': nrm((DEPTH, N_MOD * D_MODEL), 0.02),
        'ffn1_w_gate': nrm((DEPTH, D_MODEL, D_FF), d_in_scale),
        'ffn1_w_up': nrm((DEPTH, D_MODEL, D_FF), d_in_scale),
        'ffn1_w_down': nrm((DEPTH, D_FF, D_MODEL), beta * D_FF ** -0.5),
        'w_in': nrm((DEPTH, D_MODEL, D_IN), d_in_scale),
        'w_decay': nrm((DEPTH, N_DIR, GLA_RANK, QK_GLA), GLA_RANK ** -0.5),
        'b_decay': nrm((DEPTH, N_DIR, QK_GLA), 0.1),
        'w_conv': nrm((DEPTH, CONV_W, 2 * W_MLSTM), CONV_W ** -0.5),
        'b_conv': nrm((DEPTH, 2 * W_MLSTM), 0.02),
        'f_bias': 3.0 + nrm((DEPTH, N_DIR, H_MLSTM), 0.1),
        'gla_norm_g': 1.0 + nrm((DEPTH, DV_GLA), 0.05),
        'mlstm_norm_g': 1.0 + nrm((DEPTH, DH_MLSTM), 0.05),
        'w_br_gla': nrm((DEPTH, V_GLA, D_MODEL), V_GLA ** -0.5),
        'w_br_mlstm': nrm((DEPTH, W_MLSTM, D_MODEL), W_MLSTM ** -0.5),
        'w_out': nrm((DEPTH, D_MODEL, D_MODEL), beta * d_in_scale),
        'ffn2_w_gate': nrm((DEPTH, D_MODEL, D_FF), d_in_scale),
        'ffn2_w_up': nrm((DEPTH, D_MODEL, D_FF), d_in_scale),
        'ffn2_w_down': nrm((DEPTH, D_FF, D_MODEL), beta * D_FF ** -0.5),
        'ln_g': 1.0 + nrm((DEPTH, 3, D_MODEL), 0.05),
        'ln_b': nrm((DEPTH, 3, D_MODEL), 0.02),
    }


def reference(x_prompt, x_sample, c, state_gla_s, state_mlstm_c, state_mlstm_n, state_mlstm_m, c_ctx,
              w_ada, b_ada, ffn1_w_gate, ffn1_w_up, ffn1_w_down, w_in, w_decay, b_decay, w_conv, b_conv,
              f_bias, gla_norm_g, mlstm_norm_g, w_br_gla, w_br_mlstm, w_out,
              ffn2_w_gate, ffn2_w_up, ffn2_w_down, ln_g, ln_b):
    p = dict(w_ada=w_ada, b_ada=b_ada, ffn1_w_gate=ffn1_w_gate, ffn1_w_up=ffn1_w_up,
             ffn1_w_down=ffn1_w_down, w_in=w_in, w_decay=w_decay, b_decay=b_decay,
             w_conv=w_conv, b_conv=b_conv, f_bias=f_bias, gla_norm_g=gla_norm_g,
             mlstm_norm_g=mlstm_norm_g, w_br_gla=w_br_gla, w_br_mlstm=w_br_mlstm, w_out=w_out,
             ffn2_w_gate=ffn2_w_gate, ffn2_w_up=ffn2_w_up, ffn2_w_down=ffn2_w_down,
             ln_g=ln_g, ln_b=ln_b)
    alpha = (2.0 * DEPTH) ** 0.25
    f32 = jnp.float32

    bp = x_prompt.shape[0]
    zero_st = (jnp.zeros((bp, N_DIR, H_GLA, DK_GLA, DV_GLA), f32),
               jnp.zeros((bp, N_DIR, H_MLSTM, DH_MLSTM, DH_MLSTM), f32),
               jnp.zeros((bp, N_DIR, H_MLSTM, DH_MLSTM), f32),
               jnp.zeros((bp, N_DIR, H_MLSTM), f32))
    h = x_prompt
    gla_s, mc, mn, mm = [], [], [], []
    for l in range(DEPTH):
        h, (s_l, c_l, n_l, m_l) = _layer(h, c_ctx[None, :], zero_st, l, p, alpha)
        gla_s.append(s_l)
        mc.append(c_l)
        mn.append(n_l)
        mm.append(m_l)
    y_prompt = h

    h = x_sample + _grid_pos_embed(x_sample.shape[1], x_sample.dtype)[None]
    for l in range(DEPTH):
        st = (state_gla_s[:, l], state_mlstm_c[:, l], state_mlstm_n[:, l], state_mlstm_m[:, l])
        h, _ = _layer(h, c, st, l, p, alpha)
    y_sample = h

    new_gla_s = jnp.stack(gla_s, axis=1)
    new_mlstm_c = jnp.stack(mc, axis=1)
    new_mlstm_n = jnp.stack(mn, axis=1)
    new_mlstm_m = jnp.stack(mm, axis=1)
    return (y_prompt, y_sample, new_gla_s, new_mlstm_c, new_mlstm_n, new_mlstm_m)
```

```python
import contextlib
import math
import numpy as np
import concourse.bass as bass
import concourse.mybir as mybir
from concourse.bass_utils import run_bass_kernel_spmd

F32 = mybir.dt.float32
BF16 = mybir.dt.bfloat16
I32 = mybir.dt.int32
ALU = mybir.AluOpType
AF = mybir.ActivationFunctionType

ENGS = ("pe", "act", "dve", "pool", "sp")
SAME_ENG_SYNC = True
SAME_ENG_DIST = 2
N_DMA_SEMS = 40

D = 1024
T = 1024
DFF = 2816
DIN = 9264
ALPHA = 4.0 ** 0.25
LN_EPS = 1e-5
NORM_EPS = 1e-6
O_QG, O_KG, O_VG, O_RG, O_A = 0, 512, 1024, 2048, 3072
O_QM, O_KM, O_VM, O_OM = 3104, 4128, 5152, 6176
O_IF, O_FF, O_IB, O_FB = 7200, 7204, 7208, 7212
O_GG, O_GM = 7216, 8240


class Sched:
    def __init__(self, nc):
        self.nc = nc
        self.prog = {e: [] for e in ENGS}
        self.cnt = {e: 0 for e in ENGS}
        self.seen = {e: {} for e in ENGS}
        self.lastw = {}
        self.readers = {}
        self.dma_i = 0
        self.dma_cnt = [0] * N_DMA_SEMS
        self.n_inst = 0
        self.idx = {e: 0 for e in ENGS}
        self.idx_of = {e: {} for e in ENGS}

    def _need(self, eng, reads, writes):
        need = {}

        def add(prod, c):
            if need.get(prod, 0) < c:
                need[prod] = c

        for k in reads:
            lw = self.lastw.get(k)
            if lw:
                add(*lw)
        for k in writes:
            lw = self.lastw.get(k)
            if lw:
                add(*lw)
            for p, c in self.readers.get(k, {}).items():
                add(p, c)
        out = []
        for p, c in need.items():
            if p == eng and (eng == "pe" or not SAME_ENG_SYNC):
                continue
            if p == eng and c > self.cnt[eng]:
                continue
            if p == eng and self.idx[eng] - self.idx_of[eng].get(c, -10 ** 9) > SAME_ENG_DIST:
                continue
            if self.seen[eng].get(p, 0) >= c:
                continue
            self.seen[eng][p] = c
            out.append((p, c))
        return out

    def _record(self, prod, c, reads, writes):
        for k in reads:
            d = self.readers.setdefault(k, {})
            if d.get(prod, 0) < c:
                d[prod] = c
        for k in writes:
            self.lastw[k] = (prod, c)
            self.readers[k] = {}

    def op(self, eng, fn, reads=(), writes=(), inc=True):
        for p, c in self._need(eng, reads, writes):
            self.prog[eng].append(("wait", p, c))
        c = self.cnt[eng] + 1
        if inc:
            self.cnt[eng] = c
        self.idx_of[eng][c] = self.idx[eng]
        self.idx[eng] += 1
        self.prog[eng].append(("op", fn, inc))
        self._record(eng, c, reads, writes)
        self.n_inst += 1

    def dma(self, eng, fn, reads=(), writes=()):
        s = self.dma_i % N_DMA_SEMS
        self.dma_i += 1
        prod = ("dma", s)
        waits = self._need(eng, reads, writes)
        prev = self.dma_cnt[s]
        if prev and self.seen[eng].get(prod, 0) < prev:
            self.seen[eng][prod] = prev
            waits.append((prod, prev))
        for p, c in waits:
            self.prog[eng].append(("wait", p, c))
        c = prev + 16
        self.dma_cnt[s] = c
        self.prog[eng].append(("dma", fn, s))
        self._record(prod, c, reads, writes)
        self.n_inst += 1

    def barrier(self):
        for e in ENGS:
            for p in ENGS:
                if p != e and self.cnt[p] > self.seen[e].get(p, 0):
                    self.seen[e][p] = self.cnt[p]
                    self.prog[e].append(("wait", p, self.cnt[p]))
            for s in range(N_DMA_SEMS):
                prod = ("dma", s)
                if self.dma_cnt[s] > self.seen[e].get(prod, 0):
                    self.seen[e][prod] = self.dma_cnt[s]
                    self.prog[e].append(("wait", prod, self.dma_cnt[s]))

    def finish(self):
        self.barrier()

    def run(self):
        nc = self.nc
        with contextlib.ExitStack() as st:
            esem = {e: st.enter_context(nc.semaphore("s_" + e)) for e in ENGS}
            dsem = [st.enter_context(nc.semaphore("d_%d" % i)) for i in range(N_DMA_SEMS)]
            block = st.enter_context(nc.Block())

            def semof(p):
                return dsem[p[1]] if isinstance(p, tuple) else esem[p]

            def replay(e, engobj):
                for it in self.prog[e]:
                    if it[0] == "wait":
                        engobj.wait_ge(semof(it[1]), it[2])
                    elif it[0] == "op":
                        ins = it[1](engobj)
                        if it[2]:
                            ins.then_inc(esem[e], 1)
                    else:
                        ins = it[1](engobj)
                        ins.then_inc(dsem[it[2]], 16)

            @block.sync
            def _(eng):
                replay("sp", eng)

            @block.scalar
            def _(eng):
                replay("act", eng)

            @block.vector
            def _(eng):
                replay("dve", eng)

            @block.gpsimd
            def _(eng):
                replay("pool", eng)

            @block.tensor
            def _(eng):
                replay("pe", eng)


class Rot:
    def __init__(self, tiles, name):
        self.tiles = tiles
        self.name = name
        self.i = 0

    def get(self):
        j = self.i % len(self.tiles)
        self.i += 1
        return self.tiles[j], "%s%d" % (self.name, j)


class _Stop(Exception):
    pass


def build_program(stop_at=None, dumps=()):
    nc = bass.Bass("TRN2", target_bir_lowering=False)
    dump_aps = {}

    def ck(name, env=None):
        for dn, (cname, fn) in dict(dumps).items():
            if cname == name:
                ap, keys = fn(env)
                dt = ap.dtype
                d = nc.dram_tensor("dbg_" + dn, list(ap.shape), dt, kind="ExternalOutput").ap()
                dump_aps[dn] = d
                S.dma("sp", lambda e, d=d, ap=ap: e.dma_start(out=d, in_=ap), keys, [])
        if stop_at == name:
            raise _Stop()

    def din(name, shape):
        return nc.dram_tensor(name, list(shape), F32, kind="ExternalInput").ap()

    def dout(name, shape):
        return nc.dram_tensor(name, list(shape), F32, kind="ExternalOutput").ap()

    x_in = [din("xp", [T, D]), din("xs", [T, D])]
    cond = din("cond", [16, 128])
    gs0 = din("gs0", [2, 2, 4, 128, 256])
    mc0 = din("mc0", [2, 2, 4, 256, 256])
    mn0 = din("mn0", [2, 2, 4, 256])
    mm0 = din("mm0", [2, 2, 4])
    w_ada = din("w_ada", [2, D, 9 * D])
    b_ada = din("b_ada", [2, 72, 128])
    w_g = [din("ffn1_w_gate", [2, D, DFF]), din("ffn2_w_gate", [2, D, DFF])]
    w_u = [din("ffn1_w_up", [2, D, DFF]), din("ffn2_w_up", [2, D, DFF])]
    w_d = [din("ffn1_w_down", [2, DFF, D]), din("ffn2_w_down", [2, DFF, D])]
    w_in = din("w_in", [2, D, DIN])
    w_decay = din("w_decay", [2, 2, 16, 512])
    b_decay = din("b_decay", [2, 2, 512])
    w_conv = din("w_conv", [96, 128])
    b_conv = din("b_conv", [32, 128])
    f_bias = din("f_bias", [2, 2, 4])
    gla_ng = din("gla_norm_g", [2, 256])
    ml_ng = din("mlstm_norm_g", [2, 256])
    w_brg = din("w_br_gla", [2, D, D])
    w_brm = din("w_br_mlstm", [2, D, D])
    w_out = din("w_out", [2, D, D])
    ln_g = din("ln_g", [48, 128])
    ln_b = din("ln_b", [48, 128])

    y_out = [dout("yp", [T, D]), dout("ys", [T, D])]
    ogs = dout("ogs", [4, 2, 2, 4, 128, 256])
    omc = dout("omc", [4, 2, 2, 4, 256, 256])
    omn = dout("omn", [4, 2, 2, 4, 256])
    omm = dout("omm", [4, 2, 2, 4])

    S = Sched(nc)
    st = contextlib.ExitStack()

    def sb(name, shape, dt=F32):
        return st.enter_context(nc.sbuf_tensor(name, list(shape), dt))

    def psm(name, shape, dt=F32):
        return st.enter_context(nc.psum_tensor(name, list(shape), dt))

    def act(out, in_, func, r, w, **kw):
        S.op("act", lambda e: e.activation(out=out, in_=in_, func=func, **kw), r, w)

    def acp(out, in_, r, w):
        S.op("act", lambda e: e.copy(out=out, in_=in_), r, w)

    def vcp(out, in_, r, w):
        S.op("dve", lambda e: e.tensor_copy(out=out, in_=in_), r, w)

    def tt(out, in0, in1, op, r, w):
        S.op("dve", lambda e: e.tensor_tensor(out=out, in0=in0, in1=in1, op=op), r, w)

    def ts(out, in0, s1, s2, op0, op1, r, w):
        if s2 is None:
            S.op("dve", lambda e: e.tensor_scalar(out=out, in0=in0, scalar1=s1, scalar2=None, op0=op0), r, w)
        else:
            S.op("dve", lambda e: e.tensor_scalar(out=out, in0=in0, scalar1=s1, scalar2=s2, op0=op0, op1=op1), r, w)

    def stt(out, in0, scalar, in1, op0, op1, r, w):
        S.op("dve", lambda e: e.scalar_tensor_tensor(out=out, in0=in0, scalar=scalar, in1=in1, op0=op0, op1=op1), r, w)

    def recip(out, in_, r, w):
        S.op("dve", lambda e: e.reciprocal(out=out, in_=in_), r, w)

    def scan(out, d0, d1, init, op0, op1, r, w):
        S.op("dve", lambda e: e.tensor_tensor_scan(out=out, data0=d0, data1=d1, initial=init, op0=op0, op1=op1), r, w)

    def mm(out, lhsT, rhs, start, stop, r, w, inc):
        S.op("pe", lambda e: e.matmul(out, lhsT=lhsT, rhs=rhs, start=start, stop=stop), r, w, inc=inc)

    def tr(out, in_, ident, r, w, inc):
        S.op("pe", lambda e: e.transpose(out=out, in_=in_, identity=ident), r, w, inc=inc)

    def pmemset(ap, val, w):
        S.op("pool", lambda e: e.memset(ap, val), (), w)

    def vmemset(ap, val, w):
        S.op("dve", lambda e: e.memset(ap, val), (), w)

    def dma_sp(out, in_, r, w):
        S.dma("sp", lambda e: e.dma_start(out=out, in_=in_), r, w)

    def dma_cast(out, in_, r, w):
        S.dma("pool", lambda e: e.dma_start(out=out, in_=in_), r, w)

    banks = [psm("bank%d" % i, [128, 512]) for i in range(7)]
    bank_rot = Rot(banks, "B")

    def bank():
        return bank_rot.get()

    identf = sb("identf", [128, 128])
    ones = sb("ones", [128, 128])
    onesm = sb("onesm", [128, 128])
    neg16 = sb("neg16", [128, 128])
    tric = [sb("tric0", [128, 128]), sb("tric1", [128, 128])]
    trir = [sb("trir0", [128, 128]), sb("trir1", [128, 128])]
    mask = [sb("mask0", [128, 128]), sb("mask1", [128, 128])]
    sel = sb("sel", [64, 4, 128])
    rows_stage = sb("rows_stage", [128, 128])

    pmemset(ones[:], 1.0, ["ones"])
    pmemset(onesm[:], 1.0 / 1024.0, ["onesm"])
    pmemset(neg16[:], -1.0 / 16.0, ["neg16"])

    def asel(out, in_, pattern, op, cm, r, w):
        S.op("pool", lambda e: e.affine_select(out=out, in_=in_, pattern=pattern, compare_op=op, fill=0.0,
                                               base=0, channel_multiplier=cm), r, w)

    asel(identf[:], ones[:], [[-1, 128]], ALU.is_equal, 1, ["ones"], ["identf"])
    asel(tric[0][:], neg16[:], [[1, 128]], ALU.is_ge, -1, ["neg16"], ["tric0"])
    asel(tric[1][:], neg16[:], [[-1, 128]], ALU.is_ge, 1, ["neg16"], ["tric1"])
    asel(trir[0][:], neg16[:], [[-1, 128]], ALU.is_gt, 1, ["neg16"], ["trir0"])
    asel(trir[1][:], neg16[:], [[1, 128]], ALU.is_gt, -1, ["neg16"], ["trir1"])
    asel(mask[0][:], ones[:], [[1, 128]], ALU.is_ge, -1, ["ones"], ["mask0"])
    asel(mask[1][:], ones[:], [[-1, 128]], ALU.is_ge, 1, ["ones"], ["mask1"])

    xT = sb("xT", [128, 8, T])
    hT = sb("hT", [128, 8, T], BF16)
    slab_bufs = [sb("slab%d" % i, [128, 2816], BF16) for i in range(3)]
    slab_rot = Rot(slab_bufs, "slab")
    t5_rot = Rot([sb("t5_%d" % i, [128, 512]) for i in range(3)], "t5")
    TAB = sb("TAB", [128, 2048])
    TA = TAB[:, 0:1024]
    TB = TAB[:, 1024:2048]
    ones4 = TAB[0:64, 1024:1536].rearrange("p (a b) -> p a b", a=4)
    pmemset(ones4, 1.0, ["TB"])
    for p0 in (0, 32):
        asel(sel[p0:p0 + 32], ones4[p0:p0 + 32], [[-1, 4], [0, 128]], ALU.is_equal, 1, ["TB"], ["sel%d" % p0])
    UNI = sb("UNI", [128, 32768], BF16)

    ADA = sb("ADA", [128, 2, 72, 2])
    ONEP = sb("ONEP", [128, 2, 72, 2])
    GH = sb("GH", [128, 2, 72, 2])
    BADA = sb("BADA", [128, 72])
    CONDT = sb("CONDT", [128, 16])
    SCB = sb("SCB", [128, 8, 2], BF16)
    LNG = sb("LNG", [128, 48])
    LNB = sb("LNB", [128, 48])
    WC = sb("WC", [128, 96])
    BC = sb("BC", [128, 32])
    S4 = sb("S4", [128, 4, 64])

    def uni_bf(off, n):
        return UNI[:, off:off + n]

    def uni_f32(off, n):
        return UNI[:, off:off + 2 * n].bitcast(F32)

    def xk(dc, th):
        return "xT%d_%d" % (dc, th)

    def hk(kc, th):
        return "hT%d_%d" % (kc, th)

    def T5():
        return t5_rot.get()

    def load_cols(dst, dkey, src, R):
        dma_sp(rows_stage[0:R, :], src, [], ["rows_stage"])
        b, bk = bank()
        tr(b[:, 0:R], rows_stage[0:R, :], identf[0:R, 0:R], ["rows_stage", "identf"], [bk], True)
        vcp(dst, b[:, 0:R], [bk], [dkey])

    load_cols(LNG[:], "LNG", ln_g, 48)
    load_cols(LNB[:], "LNB", ln_b, 48)
    load_cols(WC[:], "WC", w_conv, 96)
    load_cols(BC[:], "BC", b_conv, 32)
    load_cols(CONDT[:], "CONDT", cond, 16)
    act(SCB[:], CONDT[:].rearrange("p (a k) -> p k a", a=2), AF.Silu, ["CONDT"], ["SCB"])

    def slab(parts, KC, C):
        buf, key = slab_rot.get()
        v = buf[:, 0:KC * C].rearrange("p (k c) -> p k c", k=KC)
        for src, off in parts:
            c = src.shape[1]
            dma_cast(v[:, :, off:off + c], src.rearrange("(k p) c -> p k c", p=128), [], [key])
        return v, key

    pidx_i = sb("pidx_i", [128, 1], I32)
    pidx = sb("pidx", [128, 1])
    OM = sb("OM", [128, 2])
    nidx_i = TAB[:, 0:64].bitcast(I32)
    nidx = TAB[:, 64:128]
    U4 = TAB[:, 128:384].rearrange("p (a b) -> p a b", a=4)
    K4i = TAB[:, 384:640].bitcast(I32).rearrange("p (a b) -> p a b", a=4)
    K4 = TAB[:, 640:896].rearrange("p (a b) -> p a b", a=4)
    S.op("pool", lambda e: e.iota(pidx_i[:], pattern=[[0, 1]], base=0, channel_multiplier=1), (), ["pidx_i"])
    S.op("pool", lambda e: e.iota(nidx_i, pattern=[[1, 64]], base=0, channel_multiplier=0), (), ["nidx_i"])
    vcp(pidx[:], pidx_i[:], ["pidx_i"], ["pidx"])
    vcp(nidx, nidx_i, ["nidx_i"], ["nidx"])
    lk = math.log(10000.0) / 256.0
    for jc in range(2):
        act(OM[:, jc:jc + 1], pidx[:], AF.Exp, ["pidx"], ["OM"], scale=-lk, bias=-lk * 128.0 * jc)
    ts(OM[:], OM[:], 1.0 / (2.0 * math.pi), None, ALU.mult, None, ["OM"], ["OM"])
    for v in range(4):
        jc = v % 2
        ts(U4[:, v, :], nidx, OM[:, jc:jc + 1], None, ALU.mult, None, ["nidx", "OM"], ["U4"])
        if v >= 2:
            ts(U4[:, v, :], U4[:, v, :], 0.25, None, ALU.add, None, ["U4"], ["U4"])
    vcp(K4i, U4, ["U4"], ["K4i"])
    vcp(K4, K4i, ["K4i"], ["K4"])
    tt(U4, U4, K4, ALU.subtract, ["U4", "K4"], ["U4"])
    ts(K4, U4, 0.5, None, ALU.is_gt, None, ["U4"], ["K4"])
    tt(U4, U4, K4, ALU.subtract, ["U4", "K4"], ["U4"])
    ts(K4, U4, -0.5, None, ALU.is_lt, None, ["U4"], ["K4"])
    tt(U4, U4, K4, ALU.add, ["U4", "K4"], ["U4"])
    act(S4[:], U4, AF.Sin, ["U4"], ["S4"], scale=6.283185)

    BADA2 = [BADA, sb("BADA1", [128, 72])]
    for l in range(2):
        load_cols(BADA2[l][:], "BADA%d" % l, b_ada[l], 72)
    ada_pending = [(l, j) for l in range(2) for j in range(9)]

    def ada_chunk():
        if not ada_pending:
            return
        l, j = ada_pending.pop(0)
        ab, abk = bank()
        for sl in range(4):
            c0 = j * 1024 + sl * 256
            sv, sk = slab([(w_ada[l][:, c0:c0 + 256], 0)], 8, 256)
            for cg in range(2):
                col = sl * 2 + cg
                for kc in range(8):
                    mm(ab[:, col * 2:col * 2 + 2], sv[:, kc, cg * 128:(cg + 1) * 128], SCB[:, kc, :],
                       kc == 0, kc == 7, [sk, "SCB"], [abk], inc=(kc == 7 and cg == 1))
        js = slice(j * 8, (j + 1) * 8)
        tt(ADA[:, l, js, :], ab[:, 0:16].rearrange("p (c a) -> p c a", a=2),
           BADA2[l][:, js].unsqueeze(2).to_broadcast([128, 8, 2]), ALU.add, [abk, "BADA%d" % l], ["ADA%d_%d" % (l, j)])
        ts(ONEP[:, l, js, :], ADA[:, l, js, :], 1.0, None, ALU.add, None, ["ADA%d_%d" % (l, j)], ["ONEP%d_%d" % (l, j)])
        ts(GH[:, l, js, :], ADA[:, l, js, :], 0.5, None, ALU.mult, None, ["ADA%d_%d" % (l, j)], ["GH%d_%d" % (l, j)])

    for _ in range(3):
        ada_chunk()

    def adac(arr, l, j, dc, p):
        return arr[:, l, j * 8 + dc, p:p + 1]

    actT = uni_bf(0, 22 * T).rearrange("p (j t) -> p j t", j=22)
    ogT = uni_bf(0, 8 * T).rearrange("p (c t) -> p c t", c=8)
    hmT = uni_bf(8 * T, 8 * T).rearrange("p (c t) -> p c t", c=8)
    HB = UNI[:, 16384:32768]

    def hb_bf(off, n):
        return HB[:, off:off + n]

    def hb_f32(off, n):
        return HB[:, off:off + 2 * n].bitcast(F32)

    g_SP = [sb("GSP0", [128, 1024]), sb("GSP1", [128, 1024])]
    g_QD = [hb_bf(4096, 1024), hb_bf(5120, 1024)]
    g_KD = [hb_bf(6144, 1024), hb_bf(7168, 1024)]
    g_KW = [hb_bf(8192, 1024), hb_bf(9216, 1024)]
    g_V = hb_bf(10240, 2048).rearrange("p (t v) -> p t v", t=8)
    g_RG = hb_bf(12288, 2048).rearrange("p (t v) -> p t v", t=8)
    g_SST2 = [[sb("GSST%d_%d" % (q, d), [128, 256]) for d in range(2)] for q in range(2)]
    g_SBF = [hb_bf(15360, 256), hb_bf(15616, 256)]
    g_ATT = Rot([hb_bf(15872 + i * 128, 128) for i in range(4)], "gatt")
    m_QK = [hb_bf(0, 2048).rearrange("p (c t) -> p c t", c=2), hb_bf(2048, 2048).rearrange("p (c t) -> p c t", c=2)]
    m_KTOK = hb_bf(4096, 2048).rearrange("p (t c) -> p t c", t=8)
    m_OG = hb_bf(6144, 2048).rearrange("p (t v) -> p t v", t=8)
    m_QS = [hb_bf(8192, 2048).rearrange("p (c t) -> p c t", c=2), hb_bf(10240, 2048).rearrange("p (c t) -> p c t", c=2)]
    m_DTM = [hb_bf(12288, 1024), hb_bf(13312, 1024)]
    m_PT = Rot([hb_bf(14336 + i * 128, 128) for i in range(4)], "mpt")
    m_KWT = Rot([hb_bf(14848 + i * 256, 256) for i in range(4)], "mkw")
    MT = hb_bf(0, 8 * T).rearrange("p (c t) -> p c t", c=8)

    VAUG = sb("VAUG", [128, 8, 257], BF16)
    CST2 = [[sb("CST%d_%d" % (q, d), [128, 2, 257]) for d in range(2)] for q in range(2)]
    CBF = [sb("CBF0", [128, 2, 257], BF16), sb("CBF1", [128, 2, 257], BF16)]
    OGTMP = Rot([sb("ogtmp%d" % i, [128, 256]) for i in range(2)], "ogtmp")
    SSQ = sb("SSQ", [128, 8])
    RS = sb("RS", [128, 8])
    DEC = [sb("DEC0", [128, 8]), sb("DEC1", [128, 8])]
    SM = Rot([sb("sm%d" % i, [128, 1]) for i in range(4)], "sm")
    SLA = sb("SLA", [128, 8, 32], BF16)
    SLF = sb("SLF", [128, 8, 64], BF16)
    SLI = sb("SLI", [128, 8, 64], BF16)
    GNB = sb("GNB", [128, 256])
    MNB = sb("MNB", [128, 256])
    R1 = sb("R1", [64, T])
    R2 = sb("R2", [64, T])
    R3 = sb("R3", [64, T])
    R4 = sb("R4", [64, T])
    AT = R1[0:33, :]
    WDEC = R2[0:33, :].rearrange("p (d c) -> p d c", d=2)
    TOT = sb("TOT", [64, 8])
    MM = sb("MM", [64, 16])
    FB = sb("FB", [64, 1])
    NFB = sb("NFB", [64, 1])
    UCOL = sb("UCOL", [128, 8, 36])
    ECOL = sb("ECOL", [128, 8, 36])

    OSUM = TAB[:].rearrange("p (t v) -> p t v", t=8)

    def osk(tile):
        return "TA" if tile < 4 else "TB"

    pmemset(SLF[:], 0.0, ["SLF"])
    pmemset(SLI[:], 0.0, ["SLI"])
    pmemset(FB[:], 0.0, ["FB"])
    pmemset(VAUG[:, :, 256:257], 1.0, ["VAUGo"])

    LNT = Rot([uni_f32(24576 + i * 1024, 512) for i in range(4)], "lnt")
    lnm1 = uni_f32(24576 + 4 * 1024, 512)
    lnr1 = uni_f32(24576 + 5 * 1024, 512)
    lnm2 = uni_f32(24576 + 6 * 1024, 512)
    lnr2 = uni_f32(24576 + 7 * 1024, 512)
    LNG2 = sb("LNG2", [128, 8])
    LNB2 = sb("LNB2", [128, 8])

    def layernorm(l, i, mod):
        c0 = l * 24 + i * 8
        if mod is not None:
            l2, jsc, jsh, p = mod
            tt(LNG2[:], LNG[:, c0:c0 + 8], ONEP[:, l2, jsc * 8:(jsc + 1) * 8, p], ALU.mult, ["LNG", "ONEP%d_%d" % (l2, jsc)], ["LNG2"])
            tt(LNB2[:], LNB[:, c0:c0 + 8], ONEP[:, l2, jsc * 8:(jsc + 1) * 8, p], ALU.mult, ["LNB", "ONEP%d_%d" % (l2, jsc)], ["LNB2"])
            tt(LNB2[:], LNB2[:], ADA[:, l2, jsh * 8:(jsh + 1) * 8, p], ALU.add, ["LNB2", "ADA%d_%d" % (l2, jsh)], ["LNB2"])
        bms, bqs = [], []
        for th in range(2):
            ts_ = slice(th * 512, (th + 1) * 512)
            bm, bmk = bank()
            bq, bqk = bank()
            bms.append((bm, bmk))
            bqs.append((bq, bqk))
            for dc in range(8):
                t, tk = LNT.get()
                act(t, xT[:, dc, ts_], AF.Square, [xk(dc, th)], [tk])
                mm(bm[:], onesm[:], xT[:, dc, ts_], dc == 0, dc == 7, ["onesm", xk(dc, th)], [bmk], inc=(dc == 7))
                mm(bq[:], onesm[:], t, dc == 0, dc == 7, ["onesm", tk], [bqk], inc=True)
        mean = [lnm1, lnm2]
        rstd = [lnr1, lnr2]
        mk = ["lnm1", "lnm2"]
        rk = ["lnr1", "lnr2"]
        for th in range(2):
            acp(mean[th], bms[th][0][:], [bms[th][1]], [mk[th]])
        for th in range(2):
            act(rstd[th], bms[th][0][:], AF.Square, [bms[th][1]], [rk[th]])
        for th in range(2):
            tt(rstd[th], bqs[th][0][:], rstd[th], ALU.subtract, [bqs[th][1], rk[th]], [rk[th]])
        for th in range(2):
            ts(rstd[th], rstd[th], 0.0, None, ALU.max, None, [rk[th]], [rk[th]])
        for th in range(2):
            act(rstd[th], rstd[th], AF.Sqrt, [rk[th]], [rk[th]], bias=LN_EPS)
        for th in range(2):
            recip(rstd[th], rstd[th], [rk[th]], [rk[th]])
        items = [(dc, th) for th in range(2) for dc in range(8)]
        tmp = {}

        def st1(k):
            dc, th = items[k]
            ts_ = slice(th * 512, (th + 1) * 512)
            t, tk = LNT.get()
            tmp[k] = (t, tk)
            tt(t, xT[:, dc, ts_], mean[th], ALU.subtract, [xk(dc, th), mk[th]], [tk])

        def st2(k):
            dc, th = items[k]
            t, tk = tmp[k]
            tt(t, t, rstd[th], ALU.mult, [tk, rk[th]], [tk])

        def st3(k):
            dc, th = items[k]
            ts_ = slice(th * 512, (th + 1) * 512)
            t, tk = tmp[k]
            c = c0 + dc
            act(xT[:, dc, ts_], t, AF.Identity, [tk, "LNG", "LNB"], [xk(dc, th)], scale=LNG[:, c:c + 1], bias=LNB[:, c:c + 1])
            if mod is not None:
                act(hT[:, dc, ts_], t, AF.Identity, [tk, "LNG2", "LNB2"], [hk(dc, th)], scale=LNG2[:, dc:dc + 1],
                    bias=LNB2[:, dc:dc + 1])

        n = len(items)
        for k in range(n + 2):
            if k < n:
                st1(k)
            if 1 <= k <= n:
                st2(k - 1)
            if 2 <= k <= n + 1:
                st3(k - 2)

    def resid_update(b, bk, dc, th, gate_ap, gkey):
        ts_ = slice(th * 512, (th + 1) * 512)
        t, tk = T5()
        act(t[:], b[:], AF.Identity, [bk, gkey], [tk], scale=gate_ap)
        stt(xT[:, dc, ts_], xT[:, dc, ts_], ALPHA, t[:], ALU.mult, ALU.add, [xk(dc, th), tk], [xk(dc, th)])

    def ffn(l, which, p):
        wg, wu, wd = w_g[which][l], w_u[which][l], w_d[which][l]
        jg = 2 if which == 0 else 8
        for js in range(11):
            if js < 6:
                ada_chunk()
            sg, sgk = slab([(wg[:, js * 256:(js + 1) * 256], 0)], 8, 256)
            su, suk = slab([(wu[:, js * 256:(js + 1) * 256], 0)], 8, 256)
            for jj in range(2):
                jc = js * 2 + jj
                for th in range(2):
                    ts_ = slice(th * 512, (th + 1) * 512)
                    bg, bgk = bank()
                    bu, buk = bank()
                    for kc in range(8):
                        mm(bg[:], sg[:, kc, jj * 128:(jj + 1) * 128], hT[:, kc, ts_], kc == 0, kc == 7,
                           [sgk, hk(kc, th)], [bgk], inc=(kc == 7))
                    for kc in range(8):
                        mm(bu[:], su[:, kc, jj * 128:(jj + 1) * 128], hT[:, kc, ts_], kc == 0, kc == 7,
                           [suk, hk(kc, th)], [buk], inc=(kc == 7))
                    t, tk = T5()
                    act(t[:], bg[:], AF.Silu, [bgk], [tk])
                    tt(actT[:, jc, ts_], bu[:], t[:], ALU.mult, [buk, tk], ["act%d_%d" % (jc, th)])
        for ds in range(8):
            sd, sdk = slab([(wd[:, ds * 128:(ds + 1) * 128], 0)], 22, 128)
            for th in range(2):
                ts_ = slice(th * 512, (th + 1) * 512)
                b, bk = bank()
                for jc in range(22):
                    mm(b[:], sd[:, jc, :], actT[:, jc, ts_], jc == 0, jc == 21,
                       [sdk, "act%d_%d" % (jc, th)], [bk], inc=(jc == 21))
                resid_update(b, bk, ds, th, adac(GH, l, jg, ds, p), "GH%d_%d" % (l, jg))

    def head_epilogue(gate, dstT, h, ngkey):
        for tile in range(8):
            jt, jtk = T5()
            act(jt[:, 0:256], OSUM[:, tile, :], AF.Square, [osk(tile)], [jtk, "SSQ"], accum_out=SSQ[:, tile:tile + 1])
        ck("epA")
        ts(RS[:], SSQ[:], 1.0 / 256.0, None, ALU.mult, None, ["SSQ"], ["RS"])
        act(RS[:], RS[:], AF.Sqrt, ["RS"], ["RS"], bias=NORM_EPS)
        recip(RS[:], RS[:], ["RS"], ["RS"])
        ck("epB")
        for tile in range(8):
            og, ogk = OGTMP.get()
            stt(og[:], OSUM[:, tile, :], RS[:, tile:tile + 1], gate[:, tile, :], ALU.mult, ALU.mult,
                [osk(tile), "RS", ngkey], [ogk])
            ck("epC")
            pt, ptk = bank()
            for vc in range(2):
                tr(pt[:, vc * 128:(vc + 1) * 128], og[:, vc * 128:(vc + 1) * 128], identf[:], [ogk, "identf"], [ptk],
                   inc=(vc == 1))
            acp(dstT[:, h * 2:(h + 1) * 2, tile * 128:(tile + 1) * 128],
                pt[:, 0:256].rearrange("p (a b) -> p a b", a=2), [ptk], ["mixT"])
            ck("epD%d" % tile)

    def tok_proj(l, col0, dst, dkey, post):
        sv, sk = slab([(w_in[l][:, col0:col0 + 256], 0)], 8, 256)
        for pair in range(4):
            b, bk = bank()
            for j in range(2):
                tile = pair * 2 + j
                th = tile // 4
                for kc in range(8):
                    mm(b[:, j * 256:(j + 1) * 256], hT[:, kc, tile * 128:(tile + 1) * 128], sv[:, kc, :],
                       kc == 0, kc == 7, [sk, hk(kc, th)], [bk], inc=(kc == 7 and j == 1))
            post(b[:].rearrange("p (a v) -> p a v", a=2), bk, dst[:, pair * 2:(pair + 1) * 2, 0:256], dkey)

    def feat_proj(l, col0, ncols, dst_fn):
        sv, sk = slab([(w_in[l][:, col0:col0 + ncols], 0)], 8, ncols)
        for cc in range(ncols // 128):
            for th in range(2):
                b, bk = bank()
                for kc in range(8):
                    mm(b[:], sv[:, kc, cc * 128:(cc + 1) * 128], hT[:, kc, th * 512:(th + 1) * 512], kc == 0, kc == 7,
                       [sk, hk(kc, th)], [bk], inc=(kc == 7))
                dst_fn(cc, th, b, bk)

    def gla(l, pcfg):
        is_prompt, seqs, p = pcfg
        pmemset(AT[32:33, :], 1.0, ["AT1"])
        pmemset(WDEC, 0.0, ["WDEC"])
        dma_cast(SLA[:], w_in[l][:, O_A:O_A + 32].rearrange("(k p) c -> p k c", p=128), [], ["SLA"])
        for th in range(2):
            b, bk = bank()
            for kc in range(8):
                mm(b[0:32, :], SLA[:, kc, :], hT[:, kc, th * 512:(th + 1) * 512], kc == 0, kc == 7,
                   ["SLA", hk(kc, th)], [bk], inc=(kc == 7))
            acp(AT[0:32, th * 512:(th + 1) * 512], b[0:32, :], [bk], ["AT%d" % th])
        dma_sp(WDEC[0:16, 0, :], w_decay[l, 0], [], ["WDEC"])
        dma_sp(WDEC[16:32, 1, :], w_decay[l, 1], [], ["WDEC"])
        for d in range(2):
            dma_sp(WDEC[32:33, d, :], b_decay[l, d:d + 1, :], [], ["WDEC"])
        dma_sp(GNB[:], gla_ng[l].partition_broadcast(128), [], ["GNB"])
        ck("glaA")

        for h in range(4):
            ada_chunk()
            def put_q(cc, th, b, bk):
                acp(TA[:, th * 512:(th + 1) * 512], b[:], [bk], ["TA"])

            def put_k(cc, th, b, bk):
                acp(TB[:, th * 512:(th + 1) * 512], b[:], [bk], ["TB"])

            feat_proj(l, O_QG + h * 128, 128, put_q)
            feat_proj(l, O_KG + h * 128, 128, put_k)

            def post_v(b3, bk, out3, dkey):
                vcp(out3, b3, [bk], [dkey])

            def post_r(b3, bk, out3, dkey):
                t, tk = T5()
                act(t[:].rearrange("p (a v) -> p a v", a=2), b3, AF.Silu, [bk], [tk])
                tt(out3, t[:].rearrange("p (a v) -> p a v", a=2), GNB[:].unsqueeze(1).to_broadcast([128, 2, 256]),
                   ALU.mult, [tk, "GNB"], [dkey])

            tok_proj(l, O_VG + h * 256, g_V, "gV", post_v)
            tok_proj(l, O_RG + h * 256, g_RG, "gRG", post_r)
            ck("glaB")

            for d in range(2):
                for half in range(2):
                    b, bk = bank()
                    for j in range(4):
                        tile = half * 4 + j
                        mm(b[:, j * 128:(j + 1) * 128], AT[0:33, tile * 128:(tile + 1) * 128],
                           WDEC[0:33, d, h * 128:(h + 1) * 128], True, True,
                           ["AT%d" % half, "AT1", "WDEC"], [bk], inc=(j == 3))
                    t, tk = T5()
                    act(t[:], b[:], AF.Exp, [bk], [tk], scale=-1.0)
                    act(g_SP[d][:, half * 512:(half + 1) * 512], t[:], AF.Ln, [tk], ["gSP%d" % d], bias=1.0)
            for d in range(2):
                lastc = 127 if d == 0 else 0
                for half in range(2):
                    hs = slice(half * 512, (half + 1) * 512)
                    b, bk = bank()
                    for j in range(4):
                        tile = half * 4 + j
                        mm(b[:, j * 128:(j + 1) * 128], g_SP[d][:, tile * 128:(tile + 1) * 128], tric[d][:], True, True,
                           ["gSP%d" % d, "tric%d" % d], [bk], inc=(j == 3))
                    t, tk = T5()
                    act(t[:], b[:], AF.Exp, [bk], [tk])
                    vcp(DEC[d][:, half * 4:(half + 1) * 4], t[:, lastc::128], [tk], ["DEC%d" % d])
                    stt(g_QD[d][:, hs], TA[:, hs], 128.0 ** -0.5, t[:], ALU.mult, ALU.mult, ["TA", tk], ["gQD%d" % d])
                    t2, t2k = T5()
                    act(t2[:], b[:], AF.Exp, [bk], [t2k], scale=-1.0)
                    tt(g_KD[d][:, hs], TB[:, hs], t2[:], ALU.mult, ["TB", t2k], ["gKD%d" % d])
            for half in range(2):
                hs = slice(half * 512, (half + 1) * 512)
                bkT, bkTk = bank()
                for j in range(4):
                    tile = half * 4 + j
                    tr(bkT[:, j * 128:(j + 1) * 128], TB[:, tile * 128:(tile + 1) * 128], identf[:], ["TB", "identf"],
                       [bkTk], inc=(j == 3))
                for d in range(2):
                    b, bk = bank()
                    for j in range(4):
                        tile = half * 4 + j
                        mm(b[:, j * 128:(j + 1) * 128], trir[d][:], g_SP[d][:, tile * 128:(tile + 1) * 128], True, True,
                           ["gSP%d" % d, "trir%d" % d], [bk], inc=(j == 3))
                    t, tk = T5()
                    act(t[:], b[:], AF.Exp, [bk], [tk])
                    tt(g_KW[d][:, hs], bkT[:], t[:], ALU.mult, [bkTk, tk], ["gKW%d" % d])

            ck("glaC")
            written = set()
            for si, (t0, n) in enumerate(seqs):
                have = [False, False]
                g_SST = g_SST2[si % 2]
                sq_ = "q%d" % (si % 2)
                if not is_prompt:
                    for d in range(2):
                        dma_sp(g_SST[d][:], gs0[l, d, h], [], ["gSST%d" % d + sq_])
                        acp(g_SBF[d][:], g_SST[d][:], ["gSST%d" % d + sq_], ["gSBF%d" % d])
                        have[d] = True
                tl = lambda i, d: (t0 + i) if d == 0 else (t0 + n - 1 - i)
                need_upd = lambda i: (i < n - 1) or is_prompt
                stA, stB, stK, stO = {}, {}, {}, {}

                def A(i):
                    b, bk = bank()
                    for d in range(2):
                        tsl = slice(tl(i, d) * 128, (tl(i, d) + 1) * 128)
                        mm(b[:, d * 128:(d + 1) * 128], g_KD[d][:, tsl], g_QD[d][:, tsl], True, True,
                           ["gKD%d" % d, "gQD%d" % d], [bk], inc=(d == 1))
                    stA[i] = (b, bk)

                def B(i):
                    b, bk = stA[i]
                    for d in range(2):
                        am, amk = g_ATT.get()
                        tt(am[:], b[:, d * 128:(d + 1) * 128], mask[d][:], ALU.mult, [bk, "mask%d" % d], [amk])
                        stB[(i, d)] = (am, amk)

                def K(i):
                    if not need_upd(i):
                        return
                    b, bk = bank()
                    for d in range(2):
                        tile = tl(i, d)
                        tsl = slice(tile * 128, (tile + 1) * 128)
                        mm(b[:, d * 256:(d + 1) * 256], g_KW[d][:, tsl], g_V[:, tile, :], True, True,
                           ["gKW%d" % d, "gV"], [bk], inc=(d == 1))
                    stK[i] = (b, bk)

                def O(i):
                    b, bk = bank()
                    for d in range(2):
                        tile = tl(i, d)
                        tsl = slice(tile * 128, (tile + 1) * 128)
                        first = (i == 0 and not have[d])
                        am, amk = stB[(i, d)]
                        mm(b[:, d * 256:(d + 1) * 256], am[:], g_V[:, tile, :], True, first, [amk, "gV"], [bk],
                           inc=(first and d == 1))
                        if not first:
                            mm(b[:, d * 256:(d + 1) * 256], g_QD[d][:, tsl], g_SBF[d][:], False, True,
                               ["gQD%d" % d, "gSBF%d" % d], [bk], inc=(d == 1))
                    stO[i] = (b, bk)

                def E(i):
                    if not need_upd(i):
                        return
                    b, bk = stK[i]
                    for d in range(2):
                        tile = tl(i, d)
                        first = (i == 0 and not have[d])
                        if first:
                            acp(g_SST[d][:], b[:, d * 256:(d + 1) * 256], [bk], ["gSST%d" % d + sq_])
                        else:
                            stt(g_SST[d][:], g_SST[d][:], DEC[d][:, tile:tile + 1], b[:, d * 256:(d + 1) * 256],
                                ALU.mult, ALU.add, [bk, "gSST%d" % d + sq_, "DEC%d" % d], ["gSST%d" % d + sq_])
                    if i < n - 1:
                        for d in range(2):
                            acp(g_SBF[d][:], g_SST[d][:], ["gSST%d" % d + sq_], ["gSBF%d" % d])
                    if i == n - 1 and is_prompt:
                        for d in range(2):
                            dma_sp(ogs[si, l, d, h], g_SST[d][:], ["gSST%d" % d + sq_], [])

                def F(i):
                    b, bk = stO[i]
                    for d in range(2):
                        tile = tl(i, d)
                        if tile not in written:
                            written.add(tile)
                            acp(OSUM[:, tile, :], b[:, d * 256:(d + 1) * 256], [bk], [osk(tile)])
                        else:
                            tt(OSUM[:, tile, :], OSUM[:, tile, :], b[:, d * 256:(d + 1) * 256], ALU.add,
                               [bk, osk(tile)], [osk(tile)])

                A(0)
                B(0)
                K(0)
                for i in range(n):
                    if i + 1 < n:
                        A(i + 1)
                    O(i)
                    E(i)
                    if i + 1 < n:
                        B(i + 1)
                        K(i + 1)
                    F(i)
            ck("glaD")
            head_epilogue(g_RG, ogT, h, "gRG")
            ck("glaE")

    def mlstm(l, pcfg):
        is_prompt, seqs, p = pcfg
        L = seqs[0][1] * 128
        NS = len(seqs)
        dma_cast(SLA[:, :, 0:16], w_in[l][:, O_IF:O_IF + 16].rearrange("(k p) c -> p k c", p=128), [], ["SLA"])
        for (dst, dk_, c0, o) in ((SLF, "SLF", 0, 4), (SLF, "SLF", 32, 12), (SLI, "SLI", 0, 0), (SLI, "SLI", 32, 8)):
            vcp(dst[:, :, c0:c0 + 4], SLA[:, :, o:o + 4], ["SLA"], [dk_])
        for d in range(2):
            dma_sp(FB[d * 32:d * 32 + 4, 0:1], f_bias[l, d].rearrange("(p o) -> p o", o=1), [], ["FB"])
        ts(NFB[:], FB[:], -1.0, None, ALU.mult, None, ["FB"], ["NFB"])
        dma_sp(MNB[:], ml_ng[l].partition_broadcast(128), [], ["MNB"])
        for th in range(2):
            hs = slice(th * 512, (th + 1) * 512)
            bF, bFk = bank()
            for kc in range(8):
                mm(bF[0:64, :], SLF[:, kc, :], hT[:, kc, hs], kc == 0, kc == 7, ["SLF", hk(kc, th)], [bFk], inc=(kc == 7))
            act(R1[:, hs], bF[0:64, :], AF.Exp, [bFk, "NFB"], ["R1"], scale=-1.0, bias=NFB[:, 0:1])
            act(R1[:, hs], R1[:, hs], AF.Ln, ["R1"], ["R1"], bias=1.0)
            bI, bIk = bank()
            for kc in range(8):
                mm(bI[0:64, :], SLI[:, kc, :], hT[:, kc, hs], kc == 0, kc == 7, ["SLI", hk(kc, th)], [bIk], inc=(kc == 7))
            vcp(R3[:, hs], bI[0:64, :], [bIk], ["R3"])
        for tile in range(8):
            tsl = slice(tile * 128, (tile + 1) * 128)
            scan(R2[:, tsl], ones[0:64, :], R1[:, tsl], 0.0, ALU.mult, ALU.add, ["R1", "ones"], ["R2"])
        vcp(TOT[32:64, :], R2[32:64, 127::128], ["R2"], ["TOT"])
        tt(R4[32:64, :], R1[32:64, :], R2[32:64, :], ALU.subtract, ["R1", "R2"], ["R4"])
        tt(R2[32:64, :].rearrange("p (t s) -> p t s", t=8), R4[32:64, :].rearrange("p (t s) -> p t s", t=8),
           TOT[32:64, :].unsqueeze(2).to_broadcast([32, 8, 128]), ALU.add, ["R4", "TOT"], ["R2"])
        tt(R3[:], R3[:], R2[:], ALU.add, ["R3", "R2"], ["R3"])
        for tile in range(8):
            tsl = slice(tile * 128, (tile + 1) * 128)
            scan(R4[0:32, tsl], ones[0:32, :], R3[0:32, tsl], -1e30, ALU.mult, ALU.max, ["R3", "ones"], ["R4"])
            rsl = slice(tile * 128 + 127, tile * 128 - 1 if tile > 0 else None, -1)
            scan(R4[32:64, rsl], ones[32:64, :], R3[32:64, rsl], -1e30, ALU.mult, ALU.max, ["R3", "ones"], ["R4"])
        vmemset(MM[:], 0.0, ["MM"])
        col = 0
        for si, (t0, n) in enumerate(seqs):
            if not is_prompt:
                for d in range(2):
                    dma_sp(MM[d * 32:d * 32 + 4, col:col + 1], mm0[l, d].rearrange("(p o) -> p o", o=1), [], ["MM"])
            for d in range(2):
                rs = slice(d * 32, d * 32 + 32)
                c = col
                for i in range(n):
                    tile = t0 + i if d == 0 else t0 + n - 1 - i
                    tsl = slice(tile * 128, (tile + 1) * 128)
                    lc = tile * 128 + (127 if d == 0 else 0)
                    ts(R4[rs, tsl], R4[rs, tsl], MM[rs, c:c + 1], None, ALU.max, None, ["R4", "MM"], ["R4"])
                    act(R1[rs, tsl], R4[rs, tsl], AF.Exp, ["R4", "MM"], ["R1"], scale=-1.0, bias=MM[rs, c:c + 1])
                    tt(MM[rs, c + 1:c + 2], R4[rs, lc:lc + 1], R2[rs, lc:lc + 1], ALU.subtract, ["R4", "R2"], ["MM"])
                    c += 1
                if is_prompt:
                    dma_sp(omm[si, l, d].rearrange("(p o) -> p o", o=1), MM[d * 32:d * 32 + 4, c:c + 1], ["MM"], [])
            col += n + 1
        ck("mlA")
        tt(R2[:], R4[:], R2[:], ALU.subtract, ["R4", "R2"], ["R2"])
        act(R2[:], R2[:], AF.Exp, ["R2"], ["R2"], scale=-1.0)
        for (src, skey, dst, dkey) in ((R3, "R3", UCOL, "UCOL"), (R2, "R2", ECOL, "ECOL")):
            for half in range(2):
                b, bk = bank()
                for j in range(4):
                    tile = half * 4 + j
                    tr(b[:, j * 64:(j + 1) * 64], src[0:64, tile * 128:(tile + 1) * 128], identf[0:64, 0:64],
                       [skey, "identf"], [bk], inc=(j == 3))
                vcp(dst[:, half * 4:(half + 1) * 4, :], b[:, 0:256].rearrange("p (a c) -> p a c", a=4)[:, :, 0:36],
                    [bk], [dkey])

        ck("mlB")
        RAWv = TA.rearrange("p (s l) -> p s l", s=NS)
        ACCv = TB.rearrange("p (s l) -> p s l", s=NS)
        for h in range(4):
            ada_chunk()
            for d in range(2):
                r = d * 32 + h
                rs4 = slice(d * 32, d * 32 + 4)
                for th in range(2):
                    hs = slice(th * 512, (th + 1) * 512)
                    bg, bgk = bank()
                    mm(bg[:], sel[rs4, h, :], R4[rs4, hs], True, True, ["sel%d" % (d * 32), "R4"], [bgk], inc=True)
                    t, tk = T5()
                    for j in range(4):
                        tile = th * 4 + j
                        act(t[:, j * 128:(j + 1) * 128], bg[:, j * 128:(j + 1) * 128], AF.Exp, [bgk, "UCOL"], [tk],
                            scale=-1.0, bias=UCOL[:, tile, r:r + 1])
                    tt(m_DTM[d][:, hs].rearrange("p (a s) -> p a s", a=4), t[:].rearrange("p (a s) -> p a s", a=4),
                       mask[d][:].unsqueeze(1).to_broadcast([128, 4, 128]), ALU.mult, [tk, "mask%d" % d], ["mDTM%d" % d])
            scr = [(TA, TB, "TA", "TB"),
                   (m_QS[0][:].rearrange("p c t -> p (c t)").bitcast(F32), m_QS[1][:].rearrange("p c t -> p (c t)").bitcast(F32),
                    "mQS0", "mQS1")]
            units = [(which, cc) for which in range(2) for cc in range(2)]
            slabs_qk = {}

            def s1(u):
                which, cc = units[u]
                RAW, ACC, rk_, ak_ = scr[u % 2]
                if which not in slabs_qk:
                    slabs_qk[which] = slab([(w_in[l][:, O_QM + which * 1024 + h * 256:O_QM + which * 1024 + (h + 1) * 256], 0)],
                                           8, 256)
                sv, sk = slabs_qk[which]
                for th in range(2):
                    b, bk = bank()
                    for kc in range(8):
                        mm(b[:], sv[:, kc, cc * 128:(cc + 1) * 128], hT[:, kc, th * 512:(th + 1) * 512], kc == 0,
                           kc == 7, [sk, hk(kc, th)], [bk], inc=(kc == 7))
                    acp(RAW[:, th * 512:(th + 1) * 512], b[:], [bk], [rk_])

            def s2(u):
                which, cc = units[u]
                RAW, ACC, rk_, ak_ = scr[u % 2]
                RAWv = RAW.rearrange("p (s l) -> p s l", s=NS)
                ACCv = ACC.rearrange("p (s l) -> p s l", s=NS)
                ch = l * 48 + which * 8 + h * 2 + cc
                bch = l * 16 + which * 8 + h * 2 + cc
                ts(ACC, RAW, WC[:, ch + 16:ch + 17], BC[:, bch:bch + 1], ALU.mult, ALU.add, [rk_, "WC", "BC"], [ak_])
                stt(ACCv[:, :, 1:L], RAWv[:, :, 0:L - 1], WC[:, ch:ch + 1], ACCv[:, :, 1:L], ALU.mult, ALU.add,
                    [rk_, ak_, "WC"], [ak_])
                stt(ACCv[:, :, 0:L - 1], RAWv[:, :, 1:L], WC[:, ch + 32:ch + 33], ACCv[:, :, 0:L - 1], ALU.mult, ALU.add,
                    [rk_, ak_, "WC"], [ak_])

            def s3(u):
                which, cc = units[u]
                RAW, ACC, rk_, ak_ = scr[u % 2]
                act(m_QK[which][:, cc, :], ACC, AF.Silu, [ak_], ["mQK%d" % which])
                if which == 1:
                    act(RAW, ACC, AF.Silu, [ak_], [rk_])
                    for half in range(2):
                        b, bk = bank()
                        for j in range(4):
                            tile = half * 4 + j
                            tr(b[:, j * 128:(j + 1) * 128], RAW[:, tile * 128:(tile + 1) * 128], identf[:],
                               [rk_, "identf"], [bk], inc=(j == 3))
                        vcp(m_KTOK[:, half * 4:(half + 1) * 4, cc * 128:(cc + 1) * 128],
                            b[:].rearrange("p (a c) -> p a c", a=4), [bk], ["mKTOK"])

            def post_v(b3, bk, out3, dkey):
                vcp(out3, b3, [bk], [dkey])

            def post_o(b3, bk, out3, dkey):
                t, tk = T5()
                act(t[:].rearrange("p (a v) -> p a v", a=2), b3, AF.Sigmoid, [bk], [tk])
                tt(out3, t[:].rearrange("p (a v) -> p a v", a=2), MNB[:].unsqueeze(1).to_broadcast([128, 2, 256]),
                   ALU.mult, [tk, "MNB"], [dkey])

            s1(0)
            s1(1)
            s2(0)
            s2(1)
            tok_proj(l, O_VM + h * 256, VAUG, "VAUG", post_v)
            s3(0)
            s1(2)
            s3(1)
            s1(3)
            s2(2)
            s2(3)
            tok_proj(l, O_OM + h * 256, m_OG, "mOG", post_o)
            s3(2)
            s3(3)

            for d in range(2):
                rs4 = slice(d * 32, d * 32 + 4)
                lastc = 127 if d == 0 else 0
                for th in range(2):
                    hs = slice(th * 512, (th + 1) * 512)
                    bi, bik = bank()
                    mm(bi[:], sel[rs4, h, :], R1[rs4, hs], True, True, ["sel%d" % (d * 32), "R1"], [bik], inc=True)
                    vcp(DEC[d][:, th * 4:(th + 1) * 4], bi[:, lastc::128], [bik], ["DEC%d" % d])
                    for cc in range(2):
                        stt(m_QS[d][:, cc, hs], m_QK[0][:, cc, hs], 1.0 / 16.0, bi[:], ALU.mult, ALU.mult,
                            ["mQK0", bik], ["mQS%d" % d])
            ck("mlD")
            written = set()
            for si, (t0, n) in enumerate(seqs):
                have = [False, False]
                CST = CST2[si % 2]
                sq_ = "q%d" % (si % 2)
                if not is_prompt:
                    for d in range(2):
                        dma_sp(CST[d][:, :, 0:256], mc0[l, d, h].rearrange("(c p) v -> p c v", p=128), [], ["CST%d" % d + sq_])
                        for cc in range(2):
                            dma_sp(CST[d][:, cc, 256:257],
                                   mn0[l, d, h, cc * 128:(cc + 1) * 128].rearrange("(p o) -> p o", o=1), [], ["CST%d" % d + sq_])
                        acp(CBF[d][:], CST[d][:], ["CST%d" % d + sq_], ["CBF%d" % d])
                        have[d] = True
                tl = lambda i, d: (t0 + i) if d == 0 else (t0 + n - 1 - i)
                need_upd = lambda i: (i < n - 1) or is_prompt
                stA, stB, stC = {}, {}, {}

                def A(i):
                    b, bk = bank()
                    for d in range(2):
                        tsl = slice(tl(i, d) * 128, (tl(i, d) + 1) * 128)
                        for cc in range(2):
                            mm(b[:, d * 128:(d + 1) * 128], m_QK[1][:, cc, tsl], m_QK[0][:, cc, tsl], cc == 0, cc == 1,
                               ["mQK0", "mQK1"], [bk], inc=(cc == 1 and d == 1))
                    stA[i] = (b, bk)

                def B(i):
                    b, bk = stA[i]
                    for d in range(2):
                        tile = tl(i, d)
                        tsl = slice(tile * 128, (tile + 1) * 128)
                        pt, ptk = m_PT.get()
                        stt(pt[:], b[:, d * 128:(d + 1) * 128], 1.0 / 16.0, m_DTM[d][:, tsl], ALU.mult, ALU.mult,
                            [bk, "mDTM%d" % d], [ptk])
                        stB[(i, d)] = (pt, ptk)
                    if need_upd(i):
                        for d in range(2):
                            tile = tl(i, d)
                            lc = tile * 128 + (127 if d == 0 else 0)
                            kw, kwk = m_KWT.get()
                            ts(kw[:], m_KTOK[:, tile, :], m_DTM[d][:, lc:lc + 1], None, ALU.mult, None,
                               ["mKTOK", "mDTM%d" % d], [kwk])
                            stB[(i, d, "kw")] = (kw, kwk)

                def C(i):
                    if not need_upd(i):
                        return
                    for d in range(2):
                        tile = tl(i, d)
                        kw, kwk = stB[(i, d, "kw")]
                        for cc in range(2):
                            bc, bck = bank()
                            mm(bc[:, 0:257], kw[:, cc * 128:(cc + 1) * 128], VAUG[:, tile, :], True, True,
                               [kwk, "VAUG", "VAUGo"], [bck], inc=True)
                            stC[(i, d, cc)] = (bc, bck)

                def Dn(i):
                    for d in range(2):
                        tile = tl(i, d)
                        tsl = slice(tile * 128, (tile + 1) * 128)
                        first = (i == 0 and not have[d])
                        pt, ptk = stB[(i, d)]
                        bn, bnk = bank()
                        mm(bn[:, 0:257], pt[:], VAUG[:, tile, :], True, first, [ptk, "VAUG", "VAUGo"], [bnk], inc=first)
                        if not first:
                            for cc in range(2):
                                mm(bn[:, 0:257], m_QS[d][:, cc, tsl], CBF[d][:, cc, :], False, cc == 1,
                                   ["mQS%d" % d, "CBF%d" % d], [bnk], inc=(cc == 1))
                        stB[(i, d, "bn")] = (bn, bnk)

                def E(i):
                    if not need_upd(i):
                        return
                    for d in range(2):
                        tile = tl(i, d)
                        first = (i == 0 and not have[d])
                        for cc in range(2):
                            bc, bck = stC[(i, d, cc)]
                            if first:
                                acp(CST[d][:, cc, :], bc[:, 0:257], [bck], ["CST%d" % d + sq_])
                            else:
                                stt(CST[d][:, cc, :], CST[d][:, cc, :], DEC[d][:, tile:tile + 1], bc[:, 0:257],
                                    ALU.mult, ALU.add, [bck, "CST%d" % d + sq_, "DEC%d" % d], ["CST%d" % d + sq_])
                    if i < n - 1:
                        for d in range(2):
                            acp(CBF[d][:], CST[d][:], ["CST%d" % d + sq_], ["CBF%d" % d])
                    if i == n - 1 and is_prompt:
                        for d in range(2):
                            dma_sp(omc[si, l, d, h].rearrange("(c p) v -> p c v", p=128), CST[d][:, :, 0:256],
                                   ["CST%d" % d + sq_], [])
                            for cc in range(2):
                                dma_sp(omn[si, l, d, h, cc * 128:(cc + 1) * 128].rearrange("(p o) -> p o", o=1),
                                       CST[d][:, cc, 256:257], ["CST%d" % d + sq_], [])

                def F(i):
                    ds_ = []
                    for d in range(2):
                        r = d * 32 + h
                        tile = tl(i, d)
                        bn, bnk = stB[(i, d, "bn")]
                        d1, d1k = SM.get()
                        tt(d1[:], bn[:, 256:257], ECOL[:, tile, r:r + 1], ALU.max, [bnk, "ECOL"], [d1k])
                        ds_.append((d1, d1k, bn, bnk, tile))
                    for (d1, d1k, bn, bnk, tile) in ds_:
                        stt(d1[:], bn[:, 256:257], -1.0, d1[:], ALU.mult, ALU.max, [bnk, d1k], [d1k])
                    for (d1, d1k, bn, bnk, tile) in ds_:
                        recip(d1[:], d1[:], [d1k], [d1k])
                    for (d1, d1k, bn, bnk, tile) in ds_:
                        if tile not in written:
                            written.add(tile)
                            act(OSUM[:, tile, :], bn[:, 0:256], AF.Identity, [bnk, d1k], [osk(tile)], scale=d1[:, 0:1])
                        else:
                            stt(OSUM[:, tile, :], bn[:, 0:256], d1[:, 0:1], OSUM[:, tile, :], ALU.mult, ALU.add,
                                [bnk, d1k, osk(tile)], [osk(tile)])

                A(0)
                B(0)
                C(0)
                for i in range(n):
                    if i + 1 < n:
                        A(i + 1)
                    Dn(i)
                    E(i)
                    if i + 1 < n:
                        B(i + 1)
                        C(i + 1)
                    F(i)
            ck("mlE")
            head_epilogue(m_OG, hmT, h, "mOG")

    def merge_out(l, p):
        ada_chunk()
        for dc in range(8):
            cs = slice(dc * 128, (dc + 1) * 128)
            sA, sAk = slab([(w_brg[l][:, cs], 0), (w_brm[l][:, cs], 128)], 8, 256)
            sB, sBk = slab([(w_in[l][:, O_GG + dc * 128:O_GG + (dc + 1) * 128], 0),
                            (w_in[l][:, O_GM + dc * 128:O_GM + (dc + 1) * 128], 128)], 8, 256)
            for th in range(2):
                hs = slice(th * 512, (th + 1) * 512)
                byg, bygk = bank()
                bym, bymk = bank()
                bgg, bggk = bank()
                bgm, bgmk = bank()
                for kc in range(8):
                    mm(byg[:], sA[:, kc, 0:128], ogT[:, kc, hs], kc == 0, kc == 7, [sAk, "mixT"], [bygk], inc=(kc == 7))
                for kc in range(8):
                    mm(bym[:], sA[:, kc, 128:256], hmT[:, kc, hs], kc == 0, kc == 7, [sAk, "mixT"], [bymk], inc=(kc == 7))
                for kc in range(8):
                    mm(bgg[:], sB[:, kc, 0:128], hT[:, kc, hs], kc == 0, kc == 7, [sBk, hk(kc, th)], [bggk], inc=(kc == 7))
                for kc in range(8):
                    mm(bgm[:], sB[:, kc, 128:256], hT[:, kc, hs], kc == 0, kc == 7, [sBk, hk(kc, th)], [bgmk], inc=(kc == 7))
                t1, t1k = T5()
                act(t1[:], bgg[:], AF.Sigmoid, [bggk], [t1k])
                tt(t1[:], byg[:], t1[:], ALU.mult, [bygk, t1k], [t1k])
                t2, t2k = T5()
                act(t2[:], bgm[:], AF.Sigmoid, [bgmk], [t2k])
                tt(t2[:], bym[:], t2[:], ALU.mult, [bymk, t2k], [t2k])
                tt(MT[:, dc, hs], t1[:], t2[:], ALU.add, [t1k, t2k], ["MT%d" % th])
        for dc in range(8):
            so, sok = slab([(w_out[l][:, dc * 128:(dc + 1) * 128], 0)], 8, 128)
            for th in range(2):
                b, bk = bank()
                for kc in range(8):
                    mm(b[:], so[:, kc, :], MT[:, kc, th * 512:(th + 1) * 512], kc == 0, kc == 7, [sok, "MT%d" % th], [bk],
                       inc=(kc == 7))
                resid_update(b, bk, dc, th, adac(ADA, l, 5, dc, p), "ADA%d_5" % l)

    try:
        ck("setup", locals())
        S.barrier()
        for p in range(2):
            is_prompt = (p == 0)
            seqs = [(0, 2), (2, 2), (4, 2), (6, 2)] if is_prompt else [(0, 8)]
            pcfg = (is_prompt, seqs, p)
            for tile in range(8):
                stg = TA if tile % 2 == 0 else TB
                sk_ = "TA" if tile % 2 == 0 else "TB"
                dma_sp(stg, x_in[p][tile * 128:(tile + 1) * 128, :], [], [sk_])
                for half in range(2):
                    b, bk = bank()
                    for j in range(4):
                        dc = half * 4 + j
                        tr(b[:, j * 128:(j + 1) * 128], stg[:, dc * 128:(dc + 1) * 128], identf[:], [sk_, "identf"], [bk],
                           inc=(j == 3))
                    acp(xT[:, half * 4:(half + 1) * 4, tile * 128:(tile + 1) * 128],
                        b[:].rearrange("p (a t) -> p a t", a=4), [bk],
                        [xk(dc_, tile // 4) for dc_ in range(half * 4, half * 4 + 4)])
            if not is_prompt:
                for dc in range(8):
                    v = dc % 4
                    if dc < 4:
                        in1 = S4[:, v, 0:16].unsqueeze(2).to_broadcast([128, 16, 64])
                    else:
                        in1 = S4[:, v, :].unsqueeze(1).to_broadcast([128, 16, 64])
                    xv = xT[:, dc, :].rearrange("p (r c) -> p r c", r=16)
                    tt(xv, xv, in1, ALU.add, [xk(dc, 0), xk(dc, 1), "S4"], [xk(dc, 0), xk(dc, 1)])
            for dc in range(8):
                for th in range(2):
                    ts(hT[:, dc, th * 512:(th + 1) * 512], xT[:, dc, th * 512:(th + 1) * 512], adac(ONEP, 0, 1, dc, p),
                       adac(ADA, 0, 0, dc, p), ALU.mult, ALU.add, [xk(dc, th), "ONEP0_1", "ADA0_0"], [hk(dc, th)])
            ck("loadx%d" % p, locals())
            for l in range(2):
                S.barrier()
                ffn(l, 0, p)
                ck("ffn1_%d_%d" % (p, l), locals())
                layernorm(l, 0, (l, 4, 3, p))
                ck("ln0_%d_%d" % (p, l), locals())
                S.barrier()
                gla(l, pcfg)
                ck("gla_%d_%d" % (p, l), locals())
                S.barrier()
                mlstm(l, pcfg)
                ck("mlstm_%d_%d" % (p, l), locals())
                S.barrier()
                merge_out(l, p)
                ck("merge_%d_%d" % (p, l), locals())
                layernorm(l, 1, (l, 7, 6, p))
                ck("ln1_%d_%d" % (p, l), locals())
                S.barrier()
                ffn(l, 1, p)
                layernorm(l, 2, (1, 1, 0, p) if l == 0 else None)
                ck("ln2_%d_%d" % (p, l), locals())
            S.barrier()
            for tile in range(8):
                stg = TA if tile % 2 == 0 else TB
                sk_ = "TA" if tile % 2 == 0 else "TB"
                for half in range(2):
                    b, bk = bank()
                    for j in range(4):
                        dc = half * 4 + j
                        tr(b[:, j * 128:(j + 1) * 128], xT[:, dc, tile * 128:(tile + 1) * 128], identf[:],
                           [xk(dc, tile // 4), "identf"], [bk], inc=(j == 3))
                    acp(stg[:, half * 512:(half + 1) * 512], b[:], [bk], [sk_])
                dma_sp(y_out[p][tile * 128:(tile + 1) * 128, :], stg, [sk_], [])
    except _Stop:
        pass

    S.finish()
    S.run()
    sbuf_left = nc.sbuf_bytes_remaining
    st.close()
    S.sbuf_left = sbuf_left
    return nc, S, dump_aps


_CACHE = {}


def kernel(x_prompt, x_sample, c, state_gla_s, state_mlstm_c, state_mlstm_n, state_mlstm_m, c_ctx,
           w_ada, b_ada, ffn1_w_gate, ffn1_w_up, ffn1_w_down, w_in, w_decay, b_decay, w_conv, b_conv,
           f_bias, gla_norm_g, mlstm_norm_g, w_br_gla, w_br_mlstm, w_out,
           ffn2_w_gate, ffn2_w_up, ffn2_w_down, ln_g, ln_b):
    f = lambda a: np.ascontiguousarray(np.asarray(a, dtype=np.float32))
    if "nc" not in _CACHE:
        _CACHE["nc"] = build_program()[0]
    nc = _CACHE["nc"]
    shared = {
        "w_ada": f(w_ada), "b_ada": f(b_ada).reshape(2, 72, 128),
        "ffn1_w_gate": f(ffn1_w_gate), "ffn1_w_up": f(ffn1_w_up), "ffn1_w_down": f(ffn1_w_down),
        "ffn2_w_gate": f(ffn2_w_gate), "ffn2_w_up": f(ffn2_w_up), "ffn2_w_down": f(ffn2_w_down),
        "w_in": f(w_in), "w_decay": f(w_decay), "b_decay": f(b_decay),
        "w_conv": f(w_conv).reshape(96, 128), "b_conv": f(b_conv).reshape(32, 128),
        "f_bias": f(f_bias), "gla_norm_g": f(gla_norm_g), "mlstm_norm_g": f(mlstm_norm_g),
        "w_br_gla": f(w_br_gla), "w_br_mlstm": f(w_br_mlstm), "w_out": f(w_out),
        "ln_g": f(ln_g).reshape(48, 128), "ln_b": f(ln_b).reshape(48, 128),
    }
    xp = f(x_prompt)
    xs = f(x_sample)
    cc = f(c)
    cctx = f(c_ctx)
    in_maps = []
    for i in range(8):
        m = dict(shared)
        m["xp"] = xp[4 * i:4 * i + 4].reshape(T, D)
        m["xs"] = xs[i]
        m["cond"] = np.ascontiguousarray(np.concatenate([cctx.reshape(8, 128), cc[i].reshape(8, 128)], axis=0))
        m["gs0"] = f(state_gla_s[i])
        m["mc0"] = f(state_mlstm_c[i])
        m["mn0"] = f(state_mlstm_n[i])
        m["mm0"] = f(state_mlstm_m[i])
        in_maps.append(m)
    res = run_bass_kernel_spmd(nc, in_maps, core_ids=list(range(8)))
    r = res.results
    y_prompt = np.concatenate([r[i]["yp"].reshape(4, 256, D) for i in range(8)], axis=0)
    y_sample = np.stack([r[i]["ys"] for i in range(8)], axis=0)
    new_gla_s = np.concatenate([r[i]["ogs"] for i in range(8)], axis=0)
    new_mlstm_c = np.concatenate([r[i]["omc"] for i in range(8)], axis=0)
    new_mlstm_n = np.concatenate([r[i]["omn"] for i in range(8)], axis=0)
    new_mlstm_m = np.concatenate([r[i]["omm"] for i in range(8)], axis=0)
    return (y_prompt.astype(np.float32), y_sample.astype(np.float32), new_gla_s.astype(np.float32),
            new_mlstm_c.astype(np.float32), new_mlstm_n.astype(np.float32), new_mlstm_m.astype(np.float32))
```

```python
import contextlib
import math
import numpy as np
import concourse.bass as bass
import concourse.mybir as mybir
from concourse.bass_utils import run_bass_kernel_spmd

F32 = mybir.dt.float32
BF16 = mybir.dt.bfloat16
I32 = mybir.dt.int32
ALU = mybir.AluOpType
AF = mybir.ActivationFunctionType

ENGS = ("pe", "act", "dve", "pool", "sp")
SAME_ENG_SYNC = True
SAME_ENG_DIST = 2
N_DMA_SEMS = 40

D = 1024
T = 1024
DFF = 2816
DIN = 9264
ALPHA = 4.0 ** 0.25
LN_EPS = 1e-5
NORM_EPS = 1e-6
O_QG, O_KG, O_VG, O_RG, O_A = 0, 512, 1024, 2048, 3072
O_QM, O_KM, O_VM, O_OM = 3104, 4128, 5152, 6176
O_IF, O_FF, O_IB, O_FB = 7200, 7204, 7208, 7212
O_GG, O_GM = 7216, 8240


class Sched:
    def __init__(self, nc):
        self.nc = nc
        self.prog = {e: [] for e in ENGS}
        self.cnt = {e: 0 for e in ENGS}
        self.seen = {e: {} for e in ENGS}
        self.lastw = {}
        self.readers = {}
        self.dma_i = 0
        self.dma_cnt = [0] * N_DMA_SEMS
        self.n_inst = 0
        self.idx = {e: 0 for e in ENGS}
        self.idx_of = {e: {} for e in ENGS}

    def _need(self, eng, reads, writes):
        need = {}

        def add(prod, c):
            if need.get(prod, 0) < c:
                need[prod] = c

        for k in reads:
            lw = self.lastw.get(k)
            if lw:
                add(*lw)
        for k in writes:
            lw = self.lastw.get(k)
            if lw:
                add(*lw)
            for p, c in self.readers.get(k, {}).items():
                add(p, c)
        out = []
        for p, c in need.items():
            if p == eng and (eng == "pe" or not SAME_ENG_SYNC):
                continue
            if p == eng and c > self.cnt[eng]:
                continue
            if p == eng and self.idx[eng] - self.idx_of[eng].get(c, -10 ** 9) > SAME_ENG_DIST:
                continue
            if self.seen[eng].get(p, 0) >= c:
                continue
            self.seen[eng][p] = c
            out.append((p, c))
        return out

    def _record(self, prod, c, reads, writes):
        for k in reads:
            d = self.readers.setdefault(k, {})
            if d.get(prod, 0) < c:
                d[prod] = c
        for k in writes:
            self.lastw[k] = (prod, c)
            self.readers[k] = {}

    def op(self, eng, fn, reads=(), writes=(), inc=True):
        for p, c in self._need(eng, reads, writes):
            self.prog[eng].append(("wait", p, c))
        c = self.cnt[eng] + 1
        if inc:
            self.cnt[eng] = c
        self.idx_of[eng][c] = self.idx[eng]
        self.idx[eng] += 1
        self.prog[eng].append(("op", fn, inc))
        self._record(eng, c, reads, writes)
        self.n_inst += 1

    def dma(self, eng, fn, reads=(), writes=()):
        s = self.dma_i % N_DMA_SEMS
        self.dma_i += 1
        prod = ("dma", s)
        waits = self._need(eng, reads, writes)
        prev = self.dma_cnt[s]
        if prev and self.seen[eng].get(prod, 0) < prev:
            self.seen[eng][prod] = prev
            waits.append((prod, prev))
        for p, c in waits:
            self.prog[eng].append(("wait", p, c))
        c = prev + 16
        self.dma_cnt[s] = c
        self.prog[eng].append(("dma", fn, s))
        self._record(prod, c, reads, writes)
        self.n_inst += 1

    def barrier(self):
        for e in ENGS:
            for p in ENGS:
                if p != e and self.cnt[p] > self.seen[e].get(p, 0):
                    self.seen[e][p] = self.cnt[p]
                    self.prog[e].append(("wait", p, self.cnt[p]))
            for s in range(N_DMA_SEMS):
                prod = ("dma", s)
                if self.dma_cnt[s] > self.seen[e].get(prod, 0):
                    self.seen[e][prod] = self.dma_cnt[s]
                    self.prog[e].append(("wait", prod, self.dma_cnt[s]))

    def finish(self):
        self.barrier()

    def run(self):
        nc = self.nc
        with contextlib.ExitStack() as st:
            esem = {e: st.enter_context(nc.semaphore("s_" + e)) for e in ENGS}
            dsem = [st.enter_context(nc.semaphore("d_%d" % i)) for i in range(N_DMA_SEMS)]
            block = st.enter_context(nc.Block())

            def semof(p):
                return dsem[p[1]] if isinstance(p, tuple) else esem[p]

            def replay(e, engobj):
                for it in self.prog[e]:
                    if it[0] == "wait":
                        engobj.wait_ge(semof(it[1]), it[2])
                    elif it[0] == "op":
                        ins = it[1](engobj)
                        if it[2]:
                            ins.then_inc(esem[e], 1)
                    else:
                        ins = it[1](engobj)
                        ins.then_inc(dsem[it[2]], 16)

            @block.sync
            def _(eng):
                replay("sp", eng)

            @block.scalar
            def _(eng):
                replay("act", eng)

            @block.vector
            def _(eng):
                replay("dve", eng)

            @block.gpsimd
            def _(eng):
                replay("pool", eng)

            @block.tensor
            def _(eng):
                replay("pe", eng)


class Rot:
    def __init__(self, tiles, name):
        self.tiles = tiles
        self.name = name
        self.i = 0

    def get(self):
        j = self.i % len(self.tiles)
        self.i += 1
        return self.tiles[j], "%s%d" % (self.name, j)


class _Stop(Exception):
    pass


def build_program(stop_at=None, dumps=()):
    nc = bass.Bass("TRN2", target_bir_lowering=False)
    dump_aps = {}

    def ck(name, env=None):
        for dn, (cname, fn) in dict(dumps).items():
            if cname == name:
                ap, keys = fn(env)
                dt = ap.dtype
                d = nc.dram_tensor("dbg_" + dn, list(ap.shape), dt, kind="ExternalOutput").ap()
                dump_aps[dn] = d
                S.dma("sp", lambda e, d=d, ap=ap: e.dma_start(out=d, in_=ap), keys, [])
        if stop_at == name:
            raise _Stop()

    def din(name, shape):
        return nc.dram_tensor(name, list(shape), F32, kind="ExternalInput").ap()

    def dout(name, shape):
        return nc.dram_tensor(name, list(shape), F32, kind="ExternalOutput").ap()

    x_in = [din("xp", [T, D]), din("xs", [T, D])]
    cond = din("cond", [16, 128])
    gs0 = din("gs0", [2, 2, 4, 128, 256])
    mc0 = din("mc0", [2, 2, 4, 256, 256])
    mn0 = din("mn0", [2, 2, 4, 256])
    mm0 = din("mm0", [2, 2, 4])
    w_ada = din("w_ada", [2, D, 9 * D])
    b_ada = din("b_ada", [2, 72, 128])
    w_g = [din("ffn1_w_gate", [2, D, DFF]), din("ffn2_w_gate", [2, D, DFF])]
    w_u = [din("ffn1_w_up", [2, D, DFF]), din("ffn2_w_up", [2, D, DFF])]
    w_d = [din("ffn1_w_down", [2, DFF, D]), din("ffn2_w_down", [2, DFF, D])]
    w_in = din("w_in", [2, D, DIN])
    w_decay = din("w_decay", [2, 2, 16, 512])
    b_decay = din("b_decay", [2, 2, 512])
    w_conv = din("w_conv", [96, 128])
    b_conv = din("b_conv", [32, 128])
    f_bias = din("f_bias", [2, 2, 4])
    gla_ng = din("gla_norm_g", [2, 256])
    ml_ng = din("mlstm_norm_g", [2, 256])
    w_brg = din("w_br_gla", [2, D, D])
    w_brm = din("w_br_mlstm", [2, D, D])
    w_out = din("w_out", [2, D, D])
    ln_g = din("ln_g", [48, 128])
    ln_b = din("ln_b", [48, 128])

    y_out = [dout("yp", [T, D]), dout("ys", [T, D])]
    ogs = dout("ogs", [4, 2, 2, 4, 128, 256])
    omc = dout("omc", [4, 2, 2, 4, 256, 256])
    omn = dout("omn", [4, 2, 2, 4, 256])
    omm = dout("omm", [4, 2, 2, 4])

    S = Sched(nc)
    st = contextlib.ExitStack()

    def sb(name, shape, dt=F32):
        return st.enter_context(nc.sbuf_tensor(name, list(shape), dt))

    def psm(name, shape, dt=F32):
        return st.enter_context(nc.psum_tensor(name, list(shape), dt))

    def act(out, in_, func, r, w, **kw):
        S.op("act", lambda e: e.activation(out=out, in_=in_, func=func, **kw), r, w)

    def acp(out, in_, r, w):
        S.op("act", lambda e: e.copy(out=out, in_=in_), r, w)

    def vcp(out, in_, r, w):
        S.op("dve", lambda e: e.tensor_copy(out=out, in_=in_), r, w)

    def tt(out, in0, in1, op, r, w):
        S.op("dve", lambda e: e.tensor_tensor(out=out, in0=in0, in1=in1, op=op), r, w)

    def ts(out, in0, s1, s2, op0, op1, r, w):
        if s2 is None:
            S.op("dve", lambda e: e.tensor_scalar(out=out, in0=in0, scalar1=s1, scalar2=None, op0=op0), r, w)
        else:
            S.op("dve", lambda e: e.tensor_scalar(out=out, in0=in0, scalar1=s1, scalar2=s2, op0=op0, op1=op1), r, w)

    def stt(out, in0, scalar, in1, op0, op1, r, w):
        S.op("dve", lambda e: e.scalar_tensor_tensor(out=out, in0=in0, scalar=scalar, in1=in1, op0=op0, op1=op1), r, w)

    def recip(out, in_, r, w):
        S.op("dve", lambda e: e.reciprocal(out=out, in_=in_), r, w)

    def scan(out, d0, d1, init, op0, op1, r, w):
        S.op("dve", lambda e: e.tensor_tensor_scan(out=out, data0=d0, data1=d1, initial=init, op0=op0, op1=op1), r, w)

    def mm(out, lhsT, rhs, start, stop, r, w, inc):
        S.op("pe", lambda e: e.matmul(out, lhsT=lhsT, rhs=rhs, start=start, stop=stop), r, w, inc=inc)

    def tr(out, in_, ident, r, w, inc):
        S.op("pe", lambda e: e.transpose(out=out, in_=in_, identity=ident), r, w, inc=inc)

    def pmemset(ap, val, w):
        S.op("pool", lambda e: e.memset(ap, val), (), w)

    def vmemset(ap, val, w):
        S.op("dve", lambda e: e.memset(ap, val), (), w)

    def dma_sp(out, in_, r, w):
        S.dma("sp", lambda e: e.dma_start(out=out, in_=in_), r, w)

    def dma_cast(out, in_, r, w):
        S.dma("pool", lambda e: e.dma_start(out=out, in_=in_), r, w)

    banks = [psm("bank%d" % i, [128, 512]) for i in range(7)]
    bank_rot = Rot(banks, "B")

    def bank():
        return bank_rot.get()

    identf = sb("identf", [128, 128])
    ones = sb("ones", [128, 128])
    onesm = sb("onesm", [128, 128])
    neg16 = sb("neg16", [128, 128])
    tric = [sb("tric0", [128, 128]), sb("tric1", [128, 128])]
    trir = [sb("trir0", [128, 128]), sb("trir1", [128, 128])]
    mask = [sb("mask0", [128, 128]), sb("mask1", [128, 128])]
    sel = sb("sel", [64, 4, 128])
    rows_stage = sb("rows_stage", [128, 128])

    pmemset(ones[:], 1.0, ["ones"])
    pmemset(onesm[:], 1.0 / 1024.0, ["onesm"])
    pmemset(neg16[:], -1.0 / 16.0, ["neg16"])

    def asel(out, in_, pattern, op, cm, r, w):
        S.op("pool", lambda e: e.affine_select(out=out, in_=in_, pattern=pattern, compare_op=op, fill=0.0,
                                               base=0, channel_multiplier=cm), r, w)

    asel(identf[:], ones[:], [[-1, 128]], ALU.is_equal, 1, ["ones"], ["identf"])
    asel(tric[0][:], neg16[:], [[1, 128]], ALU.is_ge, -1, ["neg16"], ["tric0"])
    asel(tric[1][:], neg16[:], [[-1, 128]], ALU.is_ge, 1, ["neg16"], ["tric1"])
    asel(trir[0][:], neg16[:], [[-1, 128]], ALU.is_gt, 1, ["neg16"], ["trir0"])
    asel(trir[1][:], neg16[:], [[1, 128]], ALU.is_gt, -1, ["neg16"], ["trir1"])
    asel(mask[0][:], ones[:], [[1, 128]], ALU.is_ge, -1, ["ones"], ["mask0"])
    asel(mask[1][:], ones[:], [[-1, 128]], ALU.is_ge, 1, ["ones"], ["mask1"])

    xT = sb("xT", [128, 8, T])
    hT = sb("hT", [128, 8, T], BF16)
    slab_bufs = [sb("slab%d" % i, [128, 2816], BF16) for i in range(3)]
    slab_rot = Rot(slab_bufs, "slab")
    t5_rot = Rot([sb("t5_%d" % i, [128, 512]) for i in range(3)], "t5")
    TAB = sb("TAB", [128, 2048])
    TA = TAB[:, 0:1024]
    TB = TAB[:, 1024:2048]
    ones4 = TAB[0:64, 1024:1536].rearrange("p (a b) -> p a b", a=4)
    pmemset(ones4, 1.0, ["TB"])
    for p0 in (0, 32):
        asel(sel[p0:p0 + 32], ones4[p0:p0 + 32], [[-1, 4], [0, 128]], ALU.is_equal, 1, ["TB"], ["sel%d" % p0])
    UNI = sb("UNI", [128, 32768], BF16)

    ADA = sb("ADA", [128, 2, 72, 2])
    ONEP = sb("ONEP", [128, 2, 72, 2])
    GH = sb("GH", [128, 2, 72, 2])
    BADA = sb("BADA", [128, 72])
    CONDT = sb("CONDT", [128, 16])
    SCB = sb("SCB", [128, 8, 2], BF16)
    LNG = sb("LNG", [128, 48])
    LNB = sb("LNB", [128, 48])
    WC = sb("WC", [128, 96])
    BC = sb("BC", [128, 32])
    S4 = sb("S4", [128, 4, 64])

    def uni_bf(off, n):
        return UNI[:, off:off + n]

    def uni_f32(off, n):
        return UNI[:, off:off + 2 * n].bitcast(F32)

    def xk(dc, th):
        return "xT%d_%d" % (dc, th)

    def hk(kc, th):
        return "hT%d_%d" % (kc, th)

    def T5():
        return t5_rot.get()

    def load_cols(dst, dkey, src, R):
        dma_sp(rows_stage[0:R, :], src, [], ["rows_stage"])
        b, bk = bank()
        tr(b[:, 0:R], rows_stage[0:R, :], identf[0:R, 0:R], ["rows_stage", "identf"], [bk], True)
        vcp(dst, b[:, 0:R], [bk], [dkey])

    load_cols(LNG[:], "LNG", ln_g, 48)
    load_cols(LNB[:], "LNB", ln_b, 48)
    load_cols(WC[:], "WC", w_conv, 96)
    load_cols(BC[:], "BC", b_conv, 32)
    load_cols(CONDT[:], "CONDT", cond, 16)
    act(SCB[:], CONDT[:].rearrange("p (a k) -> p k a", a=2), AF.Silu, ["CONDT"], ["SCB"])

    def slab(parts, KC, C):
        buf, key = slab_rot.get()
        v = buf[:, 0:KC * C].rearrange("p (k c) -> p k c", k=KC)
        for src, off in parts:
            c = src.shape[1]
            dma_cast(v[:, :, off:off + c], src.rearrange("(k p) c -> p k c", p=128), [], [key])
        return v, key

    pidx_i = sb("pidx_i", [128, 1], I32)
    pidx = sb("pidx", [128, 1])
    OM = sb("OM", [128, 2])
    nidx_i = TAB[:, 0:64].bitcast(I32)
    nidx = TAB[:, 64:128]
    U4 = TAB[:, 128:384].rearrange("p (a b) -> p a b", a=4)
    K4i = TAB[:, 384:640].bitcast(I32).rearrange("p (a b) -> p a b", a=4)
    K4 = TAB[:, 640:896].rearrange("p (a b) -> p a b", a=4)
    S.op("pool", lambda e: e.iota(pidx_i[:], pattern=[[0, 1]], base=0, channel_multiplier=1), (), ["pidx_i"])
    S.op("pool", lambda e: e.iota(nidx_i, pattern=[[1, 64]], base=0, channel_multiplier=0), (), ["nidx_i"])
    vcp(pidx[:], pidx_i[:], ["pidx_i"], ["pidx"])
    vcp(nidx, nidx_i, ["nidx_i"], ["nidx"])
    lk = math.log(10000.0) / 256.0
    for jc in range(2):
        act(OM[:, jc:jc + 1], pidx[:], AF.Exp, ["pidx"], ["OM"], scale=-lk, bias=-lk * 128.0 * jc)
    ts(OM[:], OM[:], 1.0 / (2.0 * math.pi), None, ALU.mult, None, ["OM"], ["OM"])
    for v in range(4):
        jc = v % 2
        ts(U4[:, v, :], nidx, OM[:, jc:jc + 1], None, ALU.mult, None, ["nidx", "OM"], ["U4"])
        if v >= 2:
            ts(U4[:, v, :], U4[:, v, :], 0.25, None, ALU.add, None, ["U4"], ["U4"])
    vcp(K4i, U4, ["U4"], ["K4i"])
    vcp(K4, K4i, ["K4i"], ["K4"])
    tt(U4, U4, K4, ALU.subtract, ["U4", "K4"], ["U4"])
    ts(K4, U4, 0.5, None, ALU.is_gt, None, ["U4"], ["K4"])
    tt(U4, U4, K4, ALU.subtract, ["U4", "K4"], ["U4"])
    ts(K4, U4, -0.5, None, ALU.is_lt, None, ["U4"], ["K4"])
    tt(U4, U4, K4, ALU.add, ["U4", "K4"], ["U4"])
    act(S4[:], U4, AF.Sin, ["U4"], ["S4"], scale=6.283185)

    BADA2 = [BADA, sb("BADA1", [128, 72])]
    for l in range(2):
        load_cols(BADA2[l][:], "BADA%d" % l, b_ada[l], 72)
    ada_pending = [(l, j) for l in range(2) for j in range(9)]

    def ada_chunk():
        if not ada_pending:
            return
        l, j = ada_pending.pop(0)
        ab, abk = bank()
        for sl in range(4):
            c0 = j * 1024 + sl * 256
            sv, sk = slab([(w_ada[l][:, c0:c0 + 256], 0)], 8, 256)
            for cg in range(2):
                col = sl * 2 + cg
                for kc in range(8):
                    mm(ab[:, col * 2:col * 2 + 2], sv[:, kc, cg * 128:(cg + 1) * 128], SCB[:, kc, :],
                       kc == 0, kc == 7, [sk, "SCB"], [abk], inc=(kc == 7 and cg == 1))
        js = slice(j * 8, (j + 1) * 8)
        tt(ADA[:, l, js, :], ab[:, 0:16].rearrange("p (c a) -> p c a", a=2),
           BADA2[l][:, js].unsqueeze(2).to_broadcast([128, 8, 2]), ALU.add, [abk, "BADA%d" % l], ["ADA%d_%d" % (l, j)])
        ts(ONEP[:, l, js, :], ADA[:, l, js, :], 1.0, None, ALU.add, None, ["ADA%d_%d" % (l, j)], ["ONEP%d_%d" % (l, j)])
        ts(GH[:, l, js, :], ADA[:, l, js, :], 0.5, None, ALU.mult, None, ["ADA%d_%d" % (l, j)], ["GH%d_%d" % (l, j)])

    for _ in range(3):
        ada_chunk()

    def adac(arr, l, j, dc, p):
        return arr[:, l, j * 8 + dc, p:p + 1]

    actT = uni_bf(0, 22 * T).rearrange("p (j t) -> p j t", j=22)
    ogT = uni_bf(0, 8 * T).rearrange("p (c t) -> p c t", c=8)
    hmT = uni_bf(8 * T, 8 * T).rearrange("p (c t) -> p c t", c=8)
    HB = UNI[:, 16384:32768]

    def hb_bf(off, n):
        return HB[:, off:off + n]

    def hb_f32(off, n):
        return HB[:, off:off + 2 * n].bitcast(F32)

    g_SP = [sb("GSP0", [128, 1024]), sb("GSP1", [128, 1024])]
    g_QD = [hb_bf(4096, 1024), hb_bf(5120, 1024)]
    g_KD = [hb_bf(6144, 1024), hb_bf(7168, 1024)]
    g_KW = [hb_bf(8192, 1024), hb_bf(9216, 1024)]
    g_V = hb_bf(10240, 2048).rearrange("p (t v) -> p t v", t=8)
    g_RG = hb_bf(12288, 2048).rearrange("p (t v) -> p t v", t=8)
    g_SST2 = [[sb("GSST%d_%d" % (q, d), [128, 256]) for d in range(2)] for q in range(2)]
    g_SBF = [hb_bf(15360, 256), hb_bf(15616, 256)]
    g_ATT = Rot([hb_bf(15872 + i * 128, 128) for i in range(4)], "gatt")
    m_QK = [hb_bf(0, 2048).rearrange("p (c t) -> p c t", c=2), hb_bf(2048, 2048).rearrange("p (c t) -> p c t", c=2)]
    m_KTOK = hb_bf(4096, 2048).rearrange("p (t c) -> p t c", t=8)
    m_OG = hb_bf(6144, 2048).rearrange("p (t v) -> p t v", t=8)
    m_QS = [hb_bf(8192, 2048).rearrange("p (c t) -> p c t", c=2), hb_bf(10240, 2048).rearrange("p (c t) -> p c t", c=2)]
    m_DTM = [hb_bf(12288, 1024), hb_bf(13312, 1024)]
    m_PT = Rot([hb_bf(14336 + i * 128, 128) for i in range(4)], "mpt")
    m_KWT = Rot([hb_bf(14848 + i * 256, 256) for i in range(4)], "mkw")
    MT = hb_bf(0, 8 * T).rearrange("p (c t) -> p c t", c=8)

    VAUG = sb("VAUG", [128, 8, 257], BF16)
    CST2 = [[sb("CST%d_%d" % (q, d), [128, 2, 257]) for d in range(2)] for q in range(2)]
    CBF = [sb("CBF0", [128, 2, 257], BF16), sb("CBF1", [128, 2, 257], BF16)]
    OGTMP = Rot([sb("ogtmp%d" % i, [128, 256]) for i in range(2)], "ogtmp")
    SSQ = sb("SSQ", [128, 8])
    RS = sb("RS", [128, 8])
    DEC = [sb("DEC0", [128, 8]), sb("DEC1", [128, 8])]
    SM = Rot([sb("sm%d" % i, [128, 1]) for i in range(4)], "sm")
    SLA = sb("SLA", [128, 8, 32], BF16)
    SLF = sb("SLF", [128, 8, 64], BF16)
    SLI = sb("SLI", [128, 8, 64], BF16)
    GNB = sb("GNB", [128, 256])
    MNB = sb("MNB", [128, 256])
    R1 = sb("R1", [64, T])
    R2 = sb("R2", [64, T])
    R3 = sb("R3", [64, T])
    R4 = sb("R4", [64, T])
    AT = R1[0:33, :]
    WDEC = R2[0:33, :].rearrange("p (d c) -> p d c", d=2)
    TOT = sb("TOT", [64, 8])
    MM = sb("MM", [64, 16])
    FB = sb("FB", [64, 1])
    NFB = sb("NFB", [64, 1])
    UCOL = sb("UCOL", [128, 8, 36])
    ECOL = sb("ECOL", [128, 8, 36])

    OSUM = TAB[:].rearrange("p (t v) -> p t v", t=8)

    def osk(tile):
        return "TA" if tile < 4 else "TB"

    pmemset(SLF[:], 0.0, ["SLF"])
    pmemset(SLI[:], 0.0, ["SLI"])
    pmemset(FB[:], 0.0, ["FB"])
    pmemset(VAUG[:, :, 256:257], 1.0, ["VAUGo"])

    LNT = Rot([uni_f32(24576 + i * 1024, 512) for i in range(4)], "lnt")
    lnm1 = uni_f32(24576 + 4 * 1024, 512)
    lnr1 = uni_f32(24576 + 5 * 1024, 512)
    lnm2 = uni_f32(24576 + 6 * 1024, 512)
    lnr2 = uni_f32(24576 + 7 * 1024, 512)
    LNG2 = sb("LNG2", [128, 8])
    LNB2 = sb("LNB2", [128, 8])

    def layernorm(l, i, mod):
        c0 = l * 24 + i * 8
        if mod is not None:
            l2, jsc, jsh, p = mod
            tt(LNG2[:], LNG[:, c0:c0 + 8], ONEP[:, l2, jsc * 8:(jsc + 1) * 8, p], ALU.mult, ["LNG", "ONEP%d_%d" % (l2, jsc)], ["LNG2"])
            tt(LNB2[:], LNB[:, c0:c0 + 8], ONEP[:, l2, jsc * 8:(jsc + 1) * 8, p], ALU.mult, ["LNB", "ONEP%d_%d" % (l2, jsc)], ["LNB2"])
            tt(LNB2[:], LNB2[:], ADA[:, l2, jsh * 8:(jsh + 1) * 8, p], ALU.add, ["LNB2", "ADA%d_%d" % (l2, jsh)], ["LNB2"])
        bms, bqs = [], []
        for th in range(2):
            ts_ = slice(th * 512, (th + 1) * 512)
            bm, bmk = bank()
            bq, bqk = bank()
            bms.append((bm, bmk))
            bqs.append((bq, bqk))
            for dc in range(8):
                t, tk = LNT.get()
                act(t, xT[:, dc, ts_], AF.Square, [xk(dc, th)], [tk])
                mm(bm[:], onesm[:], xT[:, dc, ts_], dc == 0, dc == 7, ["onesm", xk(dc, th)], [bmk], inc=(dc == 7))
                mm(bq[:], onesm[:], t, dc == 0, dc == 7, ["onesm", tk], [bqk], inc=True)
        mean = [lnm1, lnm2]
        rstd = [lnr1, lnr2]
        mk = ["lnm1", "lnm2"]
        rk = ["lnr1", "lnr2"]
        for th in range(2):
            acp(mean[th], bms[th][0][:], [bms[th][1]], [mk[th]])
        for th in range(2):
            act(rstd[th], bms[th][0][:], AF.Square, [bms[th][1]], [rk[th]])
        for th in range(2):
            tt(rstd[th], bqs[th][0][:], rstd[th], ALU.subtract, [bqs[th][1], rk[th]], [rk[th]])
        for th in range(2):
            ts(rstd[th], rstd[th], 0.0, None, ALU.max, None, [rk[th]], [rk[th]])
        for th in range(2):
            act(rstd[th], rstd[th], AF.Sqrt, [rk[th]], [rk[th]], bias=LN_EPS)
        for th in range(2):
            recip(rstd[th], rstd[th], [rk[th]], [rk[th]])
        items = [(dc, th) for th in range(2) for dc in range(8)]
        tmp = {}

        def st1(k):
            dc, th = items[k]
            ts_ = slice(th * 512, (th + 1) * 512)
            t, tk = LNT.get()
            tmp[k] = (t, tk)
            tt(t, xT[:, dc, ts_], mean[th], ALU.subtract, [xk(dc, th), mk[th]], [tk])

        def st2(k):
            dc, th = items[k]
            t, tk = tmp[k]
            tt(t, t, rstd[th], ALU.mult, [tk, rk[th]], [tk])

        def st3(k):
            dc, th = items[k]
            ts_ = slice(th * 512, (th + 1) * 512)
            t, tk = tmp[k]
            c = c0 + dc
            act(xT[:, dc, ts_], t, AF.Identity, [tk, "LNG", "LNB"], [xk(dc, th)], scale=LNG[:, c:c + 1], bias=LNB[:, c:c + 1])
            if mod is not None:
                act(hT[:, dc, ts_], t, AF.Identity, [tk, "LNG2", "LNB2"], [hk(dc, th)], scale=LNG2[:, dc:dc + 1],
                    bias=LNB2[:, dc:dc + 1])

        n = len(items)
        for k in range(n + 2):
            if k < n:
                st1(k)
            if 1 <= k <= n:
                st2(k - 1)
            if 2 <= k <= n + 1:
                st3(k - 2)

    def resid_update(b, bk, dc, th, gate_ap, gkey):
        ts_ = slice(th * 512, (th + 1) * 512)
        t, tk = T5()
        act(t[:], b[:], AF.Identity, [bk, gkey], [tk], scale=gate_ap)
        stt(xT[:, dc, ts_], xT[:, dc, ts_], ALPHA, t[:], ALU.mult, ALU.add, [xk(dc, th), tk], [xk(dc, th)])

    def ffn(l, which, p):
        wg, wu, wd = w_g[which][l], w_u[which][l], w_d[which][l]
        jg = 2 if which == 0 else 8
        for js in range(11):
            if js < 6:
                ada_chunk()
            sg, sgk = slab([(wg[:, js * 256:(js + 1) * 256], 0)], 8, 256)
            su, suk = slab([(wu[:, js * 256:(js + 1) * 256], 0)], 8, 256)
            for jj in range(2):
                jc = js * 2 + jj
                for th in range(2):
                    ts_ = slice(th * 512, (th + 1) * 512)
                    bg, bgk = bank()
                    bu, buk = bank()
                    for kc in range(8):
                        mm(bg[:], sg[:, kc, jj * 128:(jj + 1) * 128], hT[:, kc, ts_], kc == 0, kc == 7,
                           [sgk, hk(kc, th)], [bgk], inc=(kc == 7))
                    for kc in range(8):
                        mm(bu[:], su[:, kc, jj * 128:(jj + 1) * 128], hT[:, kc, ts_], kc == 0, kc == 7,
                           [suk, hk(kc, th)], [buk], inc=(kc == 7))
                    t, tk = T5()
                    act(t[:], bg[:], AF.Silu, [bgk], [tk])
                    tt(actT[:, jc, ts_], bu[:], t[:], ALU.mult, [buk, tk], ["act%d_%d" % (jc, th)])
        for ds in range(8):
            sd, sdk = slab([(wd[:, ds * 128:(ds + 1) * 128], 0)], 22, 128)
            for th in range(2):
                ts_ = slice(th * 512, (th + 1) * 512)
                b, bk = bank()
                for jc in range(22):
                    mm(b[:], sd[:, jc, :], actT[:, jc, ts_], jc == 0, jc == 21,
                       [sdk, "act%d_%d" % (jc, th)], [bk], inc=(jc == 21))
                resid_update(b, bk, ds, th, adac(GH, l, jg, ds, p), "GH%d_%d" % (l, jg))

    def head_epilogue(gate, dstT, h, ngkey):
        for tile in range(8):
            jt, jtk = T5()
            act(jt[:, 0:256], OSUM[:, tile, :], AF.Square, [osk(tile)], [jtk, "SSQ"], accum_out=SSQ[:, tile:tile + 1])
        ck("epA")
        ts(RS[:], SSQ[:], 1.0 / 256.0, None, ALU.mult, None, ["SSQ"], ["RS"])
        act(RS[:], RS[:], AF.Sqrt, ["RS"], ["RS"], bias=NORM_EPS)
        recip(RS[:], RS[:], ["RS"], ["RS"])
        ck("epB")
        for tile in range(8):
            og, ogk = OGTMP.get()
            stt(og[:], OSUM[:, tile, :], RS[:, tile:tile + 1], gate[:, tile, :], ALU.mult, ALU.mult,
                [osk(tile), "RS", ngkey], [ogk])
            ck("epC")
            pt, ptk = bank()
            for vc in range(2):
                tr(pt[:, vc * 128:(vc + 1) * 128], og[:, vc * 128:(vc + 1) * 128], identf[:], [ogk, "identf"], [ptk],
                   inc=(vc == 1))
            acp(dstT[:, h * 2:(h + 1) * 2, tile * 128:(tile + 1) * 128],
                pt[:, 0:256].rearrange("p (a b) -> p a b", a=2), [ptk], ["mixT"])
            ck("epD%d" % tile)

    def tok_proj(l, col0, dst, dkey, post):
        sv, sk = slab([(w_in[l][:, col0:col0 + 256], 0)], 8, 256)
        for pair in range(4):
            b, bk = bank()
            for j in range(2):
                tile = pair * 2 + j
                th = tile // 4
                for kc in range(8):
                    mm(b[:, j * 256:(j + 1) * 256], hT[:, kc, tile * 128:(tile + 1) * 128], sv[:, kc, :],
                       kc == 0, kc == 7, [sk, hk(kc, th)], [bk], inc=(kc == 7 and j == 1))
            post(b[:].rearrange("p (a v) -> p a v", a=2), bk, dst[:, pair * 2:(pair + 1) * 2, 0:256], dkey)

    def feat_proj(l, col0, ncols, dst_fn):
        sv, sk = slab([(w_in[l][:, col0:col0 + ncols], 0)], 8, ncols)
        for cc in range(ncols // 128):
            for th in range(2):
                b, bk = bank()
                for kc in range(8):
                    mm(b[:], sv[:, kc, cc * 128:(cc + 1) * 128], hT[:, kc, th * 512:(th + 1) * 512], kc == 0, kc == 7,
                       [sk, hk(kc, th)], [bk], inc=(kc == 7))
                dst_fn(cc, th, b, bk)

    def gla(l, pcfg):
        is_prompt, seqs, p = pcfg
        pmemset(AT[32:33, :], 1.0, ["AT1"])
        pmemset(WDEC, 0.0, ["WDEC"])
        dma_cast(SLA[:], w_in[l][:, O_A:O_A + 32].rearrange("(k p) c -> p k c", p=128), [], ["SLA"])
        for th in range(2):
            b, bk = bank()
            for kc in range(8):
                mm(b[0:32, :], SLA[:, kc, :], hT[:, kc, th * 512:(th + 1) * 512], kc == 0, kc == 7,
                   ["SLA", hk(kc, th)], [bk], inc=(kc == 7))
            acp(AT[0:32, th * 512:(th + 1) * 512], b[0:32, :], [bk], ["AT%d" % th])
        dma_sp(WDEC[0:16, 0, :], w_decay[l, 0], [], ["WDEC"])
        dma_sp(WDEC[16:32, 1, :], w_decay[l, 1], [], ["WDEC"])
        for d in range(2):
            dma_sp(WDEC[32:33, d, :], b_decay[l, d:d + 1, :], [], ["WDEC"])
        dma_sp(GNB[:], gla_ng[l].partition_broadcast(128), [], ["GNB"])
        ck("glaA")

        for h in range(4):
            ada_chunk()
            for d in range(2):
                for half in range(2):
                    b, bk = bank()
                    for j in range(4):
                        tile = half * 4 + j
                        mm(b[:, j * 128:(j + 1) * 128], AT[0:33, tile * 128:(tile + 1) * 128],
                           WDEC[0:33, d, h * 128:(h + 1) * 128], True, True,
                           ["AT%d" % half, "AT1", "WDEC"], [bk], inc=(j == 3))
                    t, tk = T5()
                    act(t[:], b[:], AF.Exp, [bk], [tk], scale=-1.0)
                    act(g_SP[d][:, half * 512:(half + 1) * 512], t[:], AF.Ln, [tk], ["gSP%d" % d], bias=1.0)
            def put_q(cc, th, b, bk):
                acp(TA[:, th * 512:(th + 1) * 512], b[:], [bk], ["TA"])

            def put_k(cc, th, b, bk):
                acp(TB[:, th * 512:(th + 1) * 512], b[:], [bk], ["TB"])

            feat_proj(l, O_QG + h * 128, 128, put_q)
            feat_proj(l, O_KG + h * 128, 128, put_k)

            def post_v(b3, bk, out3, dkey):
                vcp(out3, b3, [bk], [dkey])

            def post_r(b3, bk, out3, dkey):
                t, tk = T5()
                act(t[:].rearrange("p (a v) -> p a v", a=2), b3, AF.Silu, [bk], [tk])
                tt(out3, t[:].rearrange("p (a v) -> p a v", a=2), GNB[:].unsqueeze(1).to_broadcast([128, 2, 256]),
                   ALU.mult, [tk, "GNB"], [dkey])

            tok_proj(l, O_VG + h * 256, g_V, "gV", post_v)
            tok_proj(l, O_RG + h * 256, g_RG, "gRG", post_r)
            ck("glaB")

            for d in range(2):
                lastc = 127 if d == 0 else 0
                for half in range(2):
                    hs = slice(half * 512, (half + 1) * 512)
                    b, bk = bank()
                    for j in range(4):
                        tile = half * 4 + j
                        mm(b[:, j * 128:(j + 1) * 128], g_SP[d][:, tile * 128:(tile + 1) * 128], tric[d][:], True, True,
                           ["gSP%d" % d, "tric%d" % d], [bk], inc=(j == 3))
                    t, tk = T5()
                    act(t[:], b[:], AF.Exp, [bk], [tk])
                    vcp(DEC[d][:, half * 4:(half + 1) * 4], t[:, lastc::128], [tk], ["DEC%d" % d])
                    stt(g_QD[d][:, hs], TA[:, hs], 128.0 ** -0.5, t[:], ALU.mult, ALU.mult, ["TA", tk], ["gQD%d" % d])
                    t2, t2k = T5()
                    act(t2[:], b[:], AF.Exp, [bk], [t2k], scale=-1.0)
                    tt(g_KD[d][:, hs], TB[:, hs], t2[:], ALU.mult, ["TB", t2k], ["gKD%d" % d])
            for half in range(2):
                hs = slice(half * 512, (half + 1) * 512)
                bkT, bkTk = bank()
                for j in range(4):
                    tile = half * 4 + j
                    tr(bkT[:, j * 128:(j + 1) * 128], TB[:, tile * 128:(tile + 1) * 128], identf[:], ["TB", "identf"],
                       [bkTk], inc=(j == 3))
                for d in range(2):
                    b, bk = bank()
                    for j in range(4):
                        tile = half * 4 + j
                        mm(b[:, j * 128:(j + 1) * 128], trir[d][:], g_SP[d][:, tile * 128:(tile + 1) * 128], True, True,
                           ["gSP%d" % d, "trir%d" % d], [bk], inc=(j == 3))
                    t, tk = T5()
                    act(t[:], b[:], AF.Exp, [bk], [tk])
                    tt(g_KW[d][:, hs], bkT[:], t[:], ALU.mult, [bkTk, tk], ["gKW%d" % d])

            ck("glaC")
            written = set()
            for si, (t0, n) in enumerate(seqs):
                have = [False, False]
                g_SST = g_SST2[si % 2]
                sq_ = "q%d" % (si % 2)
                if not is_prompt:
                    for d in range(2):
                        dma_sp(g_SST[d][:], gs0[l, d, h], [], ["gSST%d" % d + sq_])
                        acp(g_SBF[d][:], g_SST[d][:], ["gSST%d" % d + sq_], ["gSBF%d" % d])
                        have[d] = True
                tl = lambda i, d: (t0 + i) if d == 0 else (t0 + n - 1 - i)
                need_upd = lambda i: (i < n - 1) or is_prompt
                stA, stB, stK, stO = {}, {}, {}, {}

                def A(i):
                    b, bk = bank()
                    for d in range(2):
                        tsl = slice(tl(i, d) * 128, (tl(i, d) + 1) * 128)
                        mm(b[:, d * 128:(d + 1) * 128], g_KD[d][:, tsl], g_QD[d][:, tsl], True, True,
                           ["gKD%d" % d, "gQD%d" % d], [bk], inc=(d == 1))
                    stA[i] = (b, bk)

                def B(i):
                    b, bk = stA[i]
                    for d in range(2):
                        am, amk = g_ATT.get()
                        tt(am[:], b[:, d * 128:(d + 1) * 128], mask[d][:], ALU.mult, [bk, "mask%d" % d], [amk])
                        stB[(i, d)] = (am, amk)

                def K(i):
                    if not need_upd(i):
                        return
                    b, bk = bank()
                    for d in range(2):
                        tile = tl(i, d)
                        tsl = slice(tile * 128, (tile + 1) * 128)
                        mm(b[:, d * 256:(d + 1) * 256], g_KW[d][:, tsl], g_V[:, tile, :], True, True,
                           ["gKW%d" % d, "gV"], [bk], inc=(d == 1))
                    stK[i] = (b, bk)

                def O(i):
                    b, bk = bank()
                    for d in range(2):
                        tile = tl(i, d)
                        tsl = slice(tile * 128, (tile + 1) * 128)
                        first = (i == 0 and not have[d])
                        am, amk = stB[(i, d)]
                        mm(b[:, d * 256:(d + 1) * 256], am[:], g_V[:, tile, :], True, first, [amk, "gV"], [bk],
                           inc=(first and d == 1))
                        if not first:
                            mm(b[:, d * 256:(d + 1) * 256], g_QD[d][:, tsl], g_SBF[d][:], False, True,
                               ["gQD%d" % d, "gSBF%d" % d], [bk], inc=(d == 1))
                    stO[i] = (b, bk)

                def E(i):
                    if not need_upd(i):
                        return
                    b, bk = stK[i]
                    for d in range(2):
                        tile = tl(i, d)
                        first = (i == 0 and not have[d])
                        if first:
                            acp(g_SST[d][:], b[:, d * 256:(d + 1) * 256], [bk], ["gSST%d" % d + sq_])
                        else:
                            stt(g_SST[d][:], g_SST[d][:], DEC[d][:, tile:tile + 1], b[:, d * 256:(d + 1) * 256],
                                ALU.mult, ALU.add, [bk, "gSST%d" % d + sq_, "DEC%d" % d], ["gSST%d" % d + sq_])
                    if i < n - 1:
                        for d in range(2):
                            acp(g_SBF[d][:], g_SST[d][:], ["gSST%d" % d + sq_], ["gSBF%d" % d])
                    if i == n - 1 and is_prompt:
                        for d in range(2):
                            dma_sp(ogs[si, l, d, h], g_SST[d][:], ["gSST%d" % d + sq_], [])

                def F(i):
                    b, bk = stO[i]
                    for d in range(2):
                        tile = tl(i, d)
                        if tile not in written:
                            written.add(tile)
                            acp(OSUM[:, tile, :], b[:, d * 256:(d + 1) * 256], [bk], [osk(tile)])
                        else:
                            tt(OSUM[:, tile, :], OSUM[:, tile, :], b[:, d * 256:(d + 1) * 256], ALU.add,
                               [bk, osk(tile)], [osk(tile)])

                A(0)
                B(0)
                K(0)
                for i in range(n):
                    if i + 1 < n:
                        A(i + 1)
                    O(i)
                    E(i)
                    if i + 1 < n:
                        B(i + 1)
                        K(i + 1)
                    F(i)
            ck("glaD")
            head_epilogue(g_RG, ogT, h, "gRG")
            ck("glaE")

    def mlstm(l, pcfg):
        is_prompt, seqs, p = pcfg
        L = seqs[0][1] * 128
        NS = len(seqs)
        dma_cast(SLA[:, :, 0:16], w_in[l][:, O_IF:O_IF + 16].rearrange("(k p) c -> p k c", p=128), [], ["SLA"])
        for (dst, dk_, c0, o) in ((SLF, "SLF", 0, 4), (SLF, "SLF", 32, 12), (SLI, "SLI", 0, 0), (SLI, "SLI", 32, 8)):
            vcp(dst[:, :, c0:c0 + 4], SLA[:, :, o:o + 4], ["SLA"], [dk_])
        for d in range(2):
            dma_sp(FB[d * 32:d * 32 + 4, 0:1], f_bias[l, d].rearrange("(p o) -> p o", o=1), [], ["FB"])
        ts(NFB[:], FB[:], -1.0, None, ALU.mult, None, ["FB"], ["NFB"])
        dma_sp(MNB[:], ml_ng[l].partition_broadcast(128), [], ["MNB"])
        for th in range(2):
            hs = slice(th * 512, (th + 1) * 512)
            bF, bFk = bank()
            for kc in range(8):
                mm(bF[0:64, :], SLF[:, kc, :], hT[:, kc, hs], kc == 0, kc == 7, ["SLF", hk(kc, th)], [bFk], inc=(kc == 7))
            act(R1[:, hs], bF[0:64, :], AF.Exp, [bFk, "NFB"], ["R1"], scale=-1.0, bias=NFB[:, 0:1])
            act(R1[:, hs], R1[:, hs], AF.Ln, ["R1"], ["R1"], bias=1.0)
            bI, bIk = bank()
            for kc in range(8):
                mm(bI[0:64, :], SLI[:, kc, :], hT[:, kc, hs], kc == 0, kc == 7, ["SLI", hk(kc, th)], [bIk], inc=(kc == 7))
            vcp(R3[:, hs], bI[0:64, :], [bIk], ["R3"])
        for tile in range(8):
            tsl = slice(tile * 128, (tile + 1) * 128)
            scan(R2[:, tsl], ones[0:64, :], R1[:, tsl], 0.0, ALU.mult, ALU.add, ["R1", "ones"], ["R2"])
        vcp(TOT[32:64, :], R2[32:64, 127::128], ["R2"], ["TOT"])
        tt(R4[32:64, :], R1[32:64, :], R2[32:64, :], ALU.subtract, ["R1", "R2"], ["R4"])
        tt(R2[32:64, :].rearrange("p (t s) -> p t s", t=8), R4[32:64, :].rearrange("p (t s) -> p t s", t=8),
           TOT[32:64, :].unsqueeze(2).to_broadcast([32, 8, 128]), ALU.add, ["R4", "TOT"], ["R2"])
        tt(R3[:], R3[:], R2[:], ALU.add, ["R3", "R2"], ["R3"])
        for tile in range(8):
            tsl = slice(tile * 128, (tile + 1) * 128)
            scan(R4[0:32, tsl], ones[0:32, :], R3[0:32, tsl], -1e30, ALU.mult, ALU.max, ["R3", "ones"], ["R4"])
            rsl = slice(tile * 128 + 127, tile * 128 - 1 if tile > 0 else None, -1)
            scan(R4[32:64, rsl], ones[32:64, :], R3[32:64, rsl], -1e30, ALU.mult, ALU.max, ["R3", "ones"], ["R4"])
        vmemset(MM[:], 0.0, ["MM"])
        col = 0
        for si, (t0, n) in enumerate(seqs):
            if not is_prompt:
                for d in range(2):
                    dma_sp(MM[d * 32:d * 32 + 4, col:col + 1], mm0[l, d].rearrange("(p o) -> p o", o=1), [], ["MM"])
            for d in range(2):
                rs = slice(d * 32, d * 32 + 32)
                c = col
                for i in range(n):
                    tile = t0 + i if d == 0 else t0 + n - 1 - i
                    tsl = slice(tile * 128, (tile + 1) * 128)
                    lc = tile * 128 + (127 if d == 0 else 0)
                    ts(R4[rs, tsl], R4[rs, tsl], MM[rs, c:c + 1], None, ALU.max, None, ["R4", "MM"], ["R4"])
                    act(R1[rs, tsl], R4[rs, tsl], AF.Exp, ["R4", "MM"], ["R1"], scale=-1.0, bias=MM[rs, c:c + 1])
                    tt(MM[rs, c + 1:c + 2], R4[rs, lc:lc + 1], R2[rs, lc:lc + 1], ALU.subtract, ["R4", "R2"], ["MM"])
                    c += 1
                if is_prompt:
                    dma_sp(omm[si, l, d].rearrange("(p o) -> p o", o=1), MM[d * 32:d * 32 + 4, c:c + 1], ["MM"], [])
            col += n + 1
        ck("mlA")
        tt(R2[:], R4[:], R2[:], ALU.subtract, ["R4", "R2"], ["R2"])
        act(R2[:], R2[:], AF.Exp, ["R2"], ["R2"], scale=-1.0)
        for (src, skey, dst, dkey) in ((R3, "R3", UCOL, "UCOL"), (R2, "R2", ECOL, "ECOL")):
            for half in range(2):
                b, bk = bank()
                for j in range(4):
                    tile = half * 4 + j
                    tr(b[:, j * 64:(j + 1) * 64], src[0:64, tile * 128:(tile + 1) * 128], identf[0:64, 0:64],
                       [skey, "identf"], [bk], inc=(j == 3))
                vcp(dst[:, half * 4:(half + 1) * 4, :], b[:, 0:256].rearrange("p (a c) -> p a c", a=4)[:, :, 0:36],
                    [bk], [dkey])

        ck("mlB")
        RAWv = TA.rearrange("p (s l) -> p s l", s=NS)
        ACCv = TB.rearrange("p (s l) -> p s l", s=NS)
        for h in range(4):
            ada_chunk()
            scr = [(TA, TB, "TA", "TB"),
                   (m_QS[0][:].rearrange("p c t -> p (c t)").bitcast(F32), m_QS[1][:].rearrange("p c t -> p (c t)").bitcast(F32),
                    "mQS0", "mQS1")]
            units = [(which, cc) for which in range(2) for cc in range(2)]
            slabs_qk = {}

            def s1(u):
                which, cc = units[u]
                RAW, ACC, rk_, ak_ = scr[u % 2]
                if which not in slabs_qk:
                    slabs_qk[which] = slab([(w_in[l][:, O_QM + which * 1024 + h * 256:O_QM + which * 1024 + (h + 1) * 256], 0)],
                                           8, 256)
                sv, sk = slabs_qk[which]
                for th in range(2):
                    b, bk = bank()
                    for kc in range(8):
                        mm(b[:], sv[:, kc, cc * 128:(cc + 1) * 128], hT[:, kc, th * 512:(th + 1) * 512], kc == 0,
                           kc == 7, [sk, hk(kc, th)], [bk], inc=(kc == 7))
                    acp(RAW[:, th * 512:(th + 1) * 512], b[:], [bk], [rk_])

            def s2(u):
                which, cc = units[u]
                RAW, ACC, rk_, ak_ = scr[u % 2]
                RAWv = RAW.rearrange("p (s l) -> p s l", s=NS)
                ACCv = ACC.rearrange("p (s l) -> p s l", s=NS)
                ch = l * 48 + which * 8 + h * 2 + cc
                bch = l * 16 + which * 8 + h * 2 + cc
                ts(ACC, RAW, WC[:, ch + 16:ch + 17], BC[:, bch:bch + 1], ALU.mult, ALU.add, [rk_, "WC", "BC"], [ak_])
                stt(ACCv[:, :, 1:L], RAWv[:, :, 0:L - 1], WC[:, ch:ch + 1], ACCv[:, :, 1:L], ALU.mult, ALU.add,
                    [rk_, ak_, "WC"], [ak_])
                stt(ACCv[:, :, 0:L - 1], RAWv[:, :, 1:L], WC[:, ch + 32:ch + 33], ACCv[:, :, 0:L - 1], ALU.mult, ALU.add,
                    [rk_, ak_, "WC"], [ak_])

            def s3(u):
                which, cc = units[u]
                RAW, ACC, rk_, ak_ = scr[u % 2]
                act(m_QK[which][:, cc, :], ACC, AF.Silu, [ak_], ["mQK%d" % which])
                if which == 1:
                    act(RAW, ACC, AF.Silu, [ak_], [rk_])
                    for half in range(2):
                        b, bk = bank()
                        for j in range(4):
                            tile = half * 4 + j
                            tr(b[:, j * 128:(j + 1) * 128], RAW[:, tile * 128:(tile + 1) * 128], identf[:],
                               [rk_, "identf"], [bk], inc=(j == 3))
                        vcp(m_KTOK[:, half * 4:(half + 1) * 4, cc * 128:(cc + 1) * 128],
                            b[:].rearrange("p (a c) -> p a c", a=4), [bk], ["mKTOK"])

            def post_v(b3, bk, out3, dkey):
                vcp(out3, b3, [bk], [dkey])

            def post_o(b3, bk, out3, dkey):
                t, tk = T5()
                act(t[:].rearrange("p (a v) -> p a v", a=2), b3, AF.Sigmoid, [bk], [tk])
                tt(out3, t[:].rearrange("p (a v) -> p a v", a=2), MNB[:].unsqueeze(1).to_broadcast([128, 2, 256]),
                   ALU.mult, [tk, "MNB"], [dkey])

            s1(0)
            s1(1)
            s2(0)
            s2(1)
            tok_proj(l, O_VM + h * 256, VAUG, "VAUG", post_v)
            s3(0)
            s1(2)
            s3(1)
            s1(3)
            s2(2)
            s2(3)
            tok_proj(l, O_OM + h * 256, m_OG, "mOG", post_o)
            s3(2)
            s3(3)

            ck("mlC")
            for d in range(2):
                r = d * 32 + h
                rs4 = slice(d * 32, d * 32 + 4)
                lastc = 127 if d == 0 else 0
                for th in range(2):
                    hs = slice(th * 512, (th + 1) * 512)
                    bg, bgk = bank()
                    mm(bg[:], sel[rs4, h, :], R4[rs4, hs], True, True, ["sel%d" % (d * 32), "R4"], [bgk], inc=True)
                    t, tk = T5()
                    for j in range(4):
                        tile = th * 4 + j
                        act(t[:, j * 128:(j + 1) * 128], bg[:, j * 128:(j + 1) * 128], AF.Exp, [bgk, "UCOL"], [tk],
                            scale=-1.0, bias=UCOL[:, tile, r:r + 1])
                    tt(m_DTM[d][:, hs].rearrange("p (a s) -> p a s", a=4), t[:].rearrange("p (a s) -> p a s", a=4),
                       mask[d][:].unsqueeze(1).to_broadcast([128, 4, 128]), ALU.mult, [tk, "mask%d" % d], ["mDTM%d" % d])
                    bi, bik = bank()
                    mm(bi[:], sel[rs4, h, :], R1[rs4, hs], True, True, ["sel%d" % (d * 32), "R1"], [bik], inc=True)
                    vcp(DEC[d][:, th * 4:(th + 1) * 4], bi[:, lastc::128], [bik], ["DEC%d" % d])
                    for cc in range(2):
                        stt(m_QS[d][:, cc, hs], m_QK[0][:, cc, hs], 1.0 / 16.0, bi[:], ALU.mult, ALU.mult,
                            ["mQK0", bik], ["mQS%d" % d])

            ck("mlD")
            written = set()
            for si, (t0, n) in enumerate(seqs):
                have = [False, False]
                CST = CST2[si % 2]
                sq_ = "q%d" % (si % 2)
                if not is_prompt:
                    for d in range(2):
                        dma_sp(CST[d][:, :, 0:256], mc0[l, d, h].rearrange("(c p) v -> p c v", p=128), [], ["CST%d" % d + sq_])
                        for cc in range(2):
                            dma_sp(CST[d][:, cc, 256:257],
                                   mn0[l, d, h, cc * 128:(cc + 1) * 128].rearrange("(p o) -> p o", o=1), [], ["CST%d" % d + sq_])
                        acp(CBF[d][:], CST[d][:], ["CST%d" % d + sq_], ["CBF%d" % d])
                        have[d] = True
                tl = lambda i, d: (t0 + i) if d == 0 else (t0 + n - 1 - i)
                need_upd = lambda i: (i < n - 1) or is_prompt
                stA, stB, stC = {}, {}, {}

                def A(i):
                    b, bk = bank()
                    for d in range(2):
                        tsl = slice(tl(i, d) * 128, (tl(i, d) + 1) * 128)
                        for cc in range(2):
                            mm(b[:, d * 128:(d + 1) * 128], m_QK[1][:, cc, tsl], m_QK[0][:, cc, tsl], cc == 0, cc == 1,
                               ["mQK0", "mQK1"], [bk], inc=(cc == 1 and d == 1))
                    stA[i] = (b, bk)

                def B(i):
                    b, bk = stA[i]
                    for d in range(2):
                        tile = tl(i, d)
                        tsl = slice(tile * 128, (tile + 1) * 128)
                        pt, ptk = m_PT.get()
                        stt(pt[:], b[:, d * 128:(d + 1) * 128], 1.0 / 16.0, m_DTM[d][:, tsl], ALU.mult, ALU.mult,
                            [bk, "mDTM%d" % d], [ptk])
                        stB[(i, d)] = (pt, ptk)
                    if need_upd(i):
                        for d in range(2):
                            tile = tl(i, d)
                            lc = tile * 128 + (127 if d == 0 else 0)
                            kw, kwk = m_KWT.get()
                            ts(kw[:], m_KTOK[:, tile, :], m_DTM[d][:, lc:lc + 1], None, ALU.mult, None,
                               ["mKTOK", "mDTM%d" % d], [kwk])
                            stB[(i, d, "kw")] = (kw, kwk)

                def C(i):
                    if not need_upd(i):
                        return
                    for d in range(2):
                        tile = tl(i, d)
                        kw, kwk = stB[(i, d, "kw")]
                        for cc in range(2):
                            bc, bck = bank()
                            mm(bc[:, 0:257], kw[:, cc * 128:(cc + 1) * 128], VAUG[:, tile, :], True, True,
                               [kwk, "VAUG", "VAUGo"], [bck], inc=True)
                            stC[(i, d, cc)] = (bc, bck)

                def Dn(i):
                    for d in range(2):
                        tile = tl(i, d)
                        tsl = slice(tile * 128, (tile + 1) * 128)
                        first = (i == 0 and not have[d])
                        pt, ptk = stB[(i, d)]
                        bn, bnk = bank()
                        mm(bn[:, 0:257], pt[:], VAUG[:, tile, :], True, first, [ptk, "VAUG", "VAUGo"], [bnk], inc=first)
                        if not first:
                            for cc in range(2):
                                mm(bn[:, 0:257], m_QS[d][:, cc, tsl], CBF[d][:, cc, :], False, cc == 1,
                                   ["mQS%d" % d, "CBF%d" % d], [bnk], inc=(cc == 1))
                        stB[(i, d, "bn")] = (bn, bnk)

                def E(i):
                    if not need_upd(i):
                        return
                    for d in range(2):
                        tile = tl(i, d)
                        first = (i == 0 and not have[d])
                        for cc in range(2):
                            bc, bck = stC[(i, d, cc)]
                            if first:
                                acp(CST[d][:, cc, :], bc[:, 0:257], [bck], ["CST%d" % d + sq_])
                            else:
                                stt(CST[d][:, cc, :], CST[d][:, cc, :], DEC[d][:, tile:tile + 1], bc[:, 0:257],
                                    ALU.mult, ALU.add, [bck, "CST%d" % d + sq_, "DEC%d" % d], ["CST%d" % d + sq_])
                    if i < n - 1:
                        for d in range(2):
                            acp(CBF[d][:], CST[d][:], ["CST%d" % d + sq_], ["CBF%d" % d])
                    if i == n - 1 and is_prompt:
                        for d in range(2):
                            dma_sp(omc[si, l, d, h].rearrange("(c p) v -> p c v", p=128), CST[d][:, :, 0:256],
                                   ["CST%d" % d + sq_], [])
                            for cc in range(2):
                                dma_sp(omn[si, l, d, h, cc * 128:(cc + 1) * 128].rearrange("(p o) -> p o", o=1),
                                       CST[d][:, cc, 256:257], ["CST%d" % d + sq_], [])

                def F(i):
                    ds_ = []
                    for d in range(2):
                        r = d * 32 + h
                        tile = tl(i, d)
                        bn, bnk = stB[(i, d, "bn")]
                        d1, d1k = SM.get()
                        tt(d1[:], bn[:, 256:257], ECOL[:, tile, r:r + 1], ALU.max, [bnk, "ECOL"], [d1k])
                        ds_.append((d1, d1k, bn, bnk, tile))
                    for (d1, d1k, bn, bnk, tile) in ds_:
                        stt(d1[:], bn[:, 256:257], -1.0, d1[:], ALU.mult, ALU.max, [bnk, d1k], [d1k])
                    for (d1, d1k, bn, bnk, tile) in ds_:
                        recip(d1[:], d1[:], [d1k], [d1k])
                    for (d1, d1k, bn, bnk, tile) in ds_:
                        if tile not in written:
                            written.add(tile)
                            act(OSUM[:, tile, :], bn[:, 0:256], AF.Identity, [bnk, d1k], [osk(tile)], scale=d1[:, 0:1])
                        else:
                            stt(OSUM[:, tile, :], bn[:, 0:256], d1[:, 0:1], OSUM[:, tile, :], ALU.mult, ALU.add,
                                [bnk, d1k, osk(tile)], [osk(tile)])

                A(0)
                B(0)
                C(0)
                for i in range(n):
                    if i + 1 < n:
                        A(i + 1)
                    Dn(i)
                    E(i)
                    if i + 1 < n:
                        B(i + 1)
                        C(i + 1)
                    F(i)
            ck("mlE")
            head_epilogue(m_OG, hmT, h, "mOG")

    def merge_out(l, p):
        ada_chunk()
        for dc in range(8):
            cs = slice(dc * 128, (dc + 1) * 128)
            sA, sAk = slab([(w_brg[l][:, cs], 0), (w_brm[l][:, cs], 128)], 8, 256)
            sB, sBk = slab([(w_in[l][:, O_GG + dc * 128:O_GG + (dc + 1) * 128], 0),
                            (w_in[l][:, O_GM + dc * 128:O_GM + (dc + 1) * 128], 128)], 8, 256)
            for th in range(2):
                hs = slice(th * 512, (th + 1) * 512)
                byg, bygk = bank()
                bym, bymk = bank()
                bgg, bggk = bank()
                bgm, bgmk = bank()
                for kc in range(8):
                    mm(byg[:], sA[:, kc, 0:128], ogT[:, kc, hs], kc == 0, kc == 7, [sAk, "mixT"], [bygk], inc=(kc == 7))
                for kc in range(8):
                    mm(bym[:], sA[:, kc, 128:256], hmT[:, kc, hs], kc == 0, kc == 7, [sAk, "mixT"], [bymk], inc=(kc == 7))
                for kc in range(8):
                    mm(bgg[:], sB[:, kc, 0:128], hT[:, kc, hs], kc == 0, kc == 7, [sBk, hk(kc, th)], [bggk], inc=(kc == 7))
                for kc in range(8):
                    mm(bgm[:], sB[:, kc, 128:256], hT[:, kc, hs], kc == 0, kc == 7, [sBk, hk(kc, th)], [bgmk], inc=(kc == 7))
                t1, t1k = T5()
                act(t1[:], bgg[:], AF.Sigmoid, [bggk], [t1k])
                tt(t1[:], byg[:], t1[:], ALU.mult, [bygk, t1k], [t1k])
                t2, t2k = T5()
                act(t2[:], bgm[:], AF.Sigmoid, [bgmk], [t2k])
                tt(t2[:], bym[:], t2[:], ALU.mult, [bymk, t2k], [t2k])
                tt(MT[:, dc, hs], t1[:], t2[:], ALU.add, [t1k, t2k], ["MT%d" % th])
        for dc in range(8):
            so, sok = slab([(w_out[l][:, dc * 128:(dc + 1) * 128], 0)], 8, 128)
            for th in range(2):
                b, bk = bank()
                for kc in range(8):
                    mm(b[:], so[:, kc, :], MT[:, kc, th * 512:(th + 1) * 512], kc == 0, kc == 7, [sok, "MT%d" % th], [bk],
                       inc=(kc == 7))
                resid_update(b, bk, dc, th, adac(ADA, l, 5, dc, p), "ADA%d_5" % l)

    try:
        ck("setup", locals())
        S.barrier()
        for p in range(2):
            is_prompt = (p == 0)
            seqs = [(0, 2), (2, 2), (4, 2), (6, 2)] if is_prompt else [(0, 8)]
            pcfg = (is_prompt, seqs, p)
            for tile in range(8):
                stg = TA if tile % 2 == 0 else TB
                sk_ = "TA" if tile % 2 == 0 else "TB"
                dma_sp(stg, x_in[p][tile * 128:(tile + 1) * 128, :], [], [sk_])
                for half in range(2):
                    b, bk = bank()
                    for j in range(4):
                        dc = half * 4 + j
                        tr(b[:, j * 128:(j + 1) * 128], stg[:, dc * 128:(dc + 1) * 128], identf[:], [sk_, "identf"], [bk],
                           inc=(j == 3))
                    acp(xT[:, half * 4:(half + 1) * 4, tile * 128:(tile + 1) * 128],
                        b[:].rearrange("p (a t) -> p a t", a=4), [bk],
                        [xk(dc_, tile // 4) for dc_ in range(half * 4, half * 4 + 4)])
            if not is_prompt:
                for dc in range(8):
                    v = dc % 4
                    if dc < 4:
                        in1 = S4[:, v, 0:16].unsqueeze(2).to_broadcast([128, 16, 64])
                    else:
                        in1 = S4[:, v, :].unsqueeze(1).to_broadcast([128, 16, 64])
                    xv = xT[:, dc, :].rearrange("p (r c) -> p r c", r=16)
                    tt(xv, xv, in1, ALU.add, [xk(dc, 0), xk(dc, 1), "S4"], [xk(dc, 0), xk(dc, 1)])
            for dc in range(8):
                for th in range(2):
                    ts(hT[:, dc, th * 512:(th + 1) * 512], xT[:, dc, th * 512:(th + 1) * 512], adac(ONEP, 0, 1, dc, p),
                       adac(ADA, 0, 0, dc, p), ALU.mult, ALU.add, [xk(dc, th), "ONEP0_1", "ADA0_0"], [hk(dc, th)])
            ck("loadx%d" % p, locals())
            for l in range(2):
                S.barrier()
                ffn(l, 0, p)
                ck("ffn1_%d_%d" % (p, l), locals())
                layernorm(l, 0, (l, 4, 3, p))
                ck("ln0_%d_%d" % (p, l), locals())
                S.barrier()
                gla(l, pcfg)
                ck("gla_%d_%d" % (p, l), locals())
                S.barrier()
                mlstm(l, pcfg)
                ck("mlstm_%d_%d" % (p, l), locals())
                S.barrier()
                merge_out(l, p)
                ck("merge_%d_%d" % (p, l), locals())
                layernorm(l, 1, (l, 7, 6, p))
                ck("ln1_%d_%d" % (p, l), locals())
                S.barrier()
                ffn(l, 1, p)
                layernorm(l, 2, (1, 1, 0, p) if l == 0 else None)
                ck("ln2_%d_%d" % (p, l), locals())
            S.barrier()
            for tile in range(8):
                stg = TA if tile % 2 == 0 else TB
                sk_ = "TA" if tile % 2 == 0 else "TB"
                for half in range(2):
                    b, bk = bank()
                    for j in range(4):
                        dc = half * 4 + j
                        tr(b[:, j * 128:(j + 1) * 128], xT[:, dc, tile * 128:(tile + 1) * 128], identf[:],
                           [xk(dc, tile // 4), "identf"], [bk], inc=(j == 3))
                    acp(stg[:, half * 512:(half + 1) * 512], b[:], [bk], [sk_])
                dma_sp(y_out[p][tile * 128:(tile + 1) * 128, :], stg, [sk_], [])
    except _Stop:
        pass

    S.finish()
    S.run()
    sbuf_left = nc.sbuf_bytes_remaining
    st.close()
    S.sbuf_left = sbuf_left
    return nc, S, dump_aps


_CACHE = {}


def kernel(x_prompt, x_sample, c, state_gla_s, state_mlstm_c, state_mlstm_n, state_mlstm_m, c_ctx,
           w_ada, b_ada, ffn1_w_gate, ffn1_w_up, ffn1_w_down, w_in, w_decay, b_decay, w_conv, b_conv,
           f_bias, gla_norm_g, mlstm_norm_g, w_br_gla, w_br_mlstm, w_out,
           ffn2_w_gate, ffn2_w_up, ffn2_w_down, ln_g, ln_b):
    f = lambda a: np.ascontiguousarray(np.asarray(a, dtype=np.float32))
    if "nc" not in _CACHE:
        _CACHE["nc"] = build_program()[0]
    nc = _CACHE["nc"]
    shared = {
        "w_ada": f(w_ada), "b_ada": f(b_ada).reshape(2, 72, 128),
        "ffn1_w_gate": f(ffn1_w_gate), "ffn1_w_up": f(ffn1_w_up), "ffn1_w_down": f(ffn1_w_down),
        "ffn2_w_gate": f(ffn2_w_gate), "ffn2_w_up": f(ffn2_w_up), "ffn2_w_down": f(ffn2_w_down),
        "w_in": f(w_in), "w_decay": f(w_decay), "b_decay": f(b_decay),
        "w_conv": f(w_conv).reshape(96, 128), "b_conv": f(b_conv).reshape(32, 128),
        "f_bias": f(f_bias), "gla_norm_g": f(gla_norm_g), "mlstm_norm_g": f(mlstm_norm_g),
        "w_br_gla": f(w_br_gla), "w_br_mlstm": f(w_br_mlstm), "w_out": f(w_out),
        "ln_g": f(ln_g).reshape(48, 128), "ln_b": f(ln_b).reshape(48, 128),
    }
    xp = f(x_prompt)
    xs = f(x_sample)
    cc = f(c)
    cctx = f(c_ctx)
    in_maps = []
    for i in range(8):
        m = dict(shared)
        m["xp"] = xp[4 * i:4 * i + 4].reshape(T, D)
        m["xs"] = xs[i]
        m["cond"] = np.ascontiguousarray(np.concatenate([cctx.reshape(8, 128), cc[i].reshape(8, 128)], axis=0))
        m["gs0"] = f(state_gla_s[i])
        m["mc0"] = f(state_mlstm_c[i])
        m["mn0"] = f(state_mlstm_n[i])
        m["mm0"] = f(state_mlstm_m[i])
        in_maps.append(m)
    res = run_bass_kernel_spmd(nc, in_maps, core_ids=list(range(8)))
    r = res.results
    y_prompt = np.concatenate([r[i]["yp"].reshape(4, 256, D) for i in range(8)], axis=0)
    y_sample = np.stack([r[i]["ys"] for i in range(8)], axis=0)
    new_gla_s = np.concatenate([r[i]["ogs"] for i in range(8)], axis=0)
    new_mlstm_c = np.concatenate([r[i]["omc"] for i in range(8)], axis=0)
    new_mlstm_n = np.concatenate([r[i]["omn"] for i in range(8)], axis=0)
    new_mlstm_m = np.concatenate([r[i]["omm"] for i in range(8)], axis=0)
    return (y_prompt.astype(np.float32), y_sample.astype(np.float32), new_gla_s.astype(np.float32),
            new_mlstm_c.astype(np.float32), new_mlstm_n.astype(np.float32), new_mlstm_m.astype(np.float32))
```

```python
import contextlib
import math
import numpy as np
import concourse.bass as bass
import concourse.mybir as mybir
from concourse.bass_utils import run_bass_kernel_spmd

F32 = mybir.dt.float32
BF16 = mybir.dt.bfloat16
I32 = mybir.dt.int32
ALU = mybir.AluOpType
AF = mybir.ActivationFunctionType

ENGS = ("pe", "act", "dve", "pool", "sp")
SAME_ENG_SYNC = True
SAME_ENG_DIST = 1
N_DMA_SEMS = 40

D = 1024
T = 1024
DFF = 2816
DIN = 9264
ALPHA = 4.0 ** 0.25
LN_EPS = 1e-5
NORM_EPS = 1e-6
O_QG, O_KG, O_VG, O_RG, O_A = 0, 512, 1024, 2048, 3072
O_QM, O_KM, O_VM, O_OM = 3104, 4128, 5152, 6176
O_IF, O_FF, O_IB, O_FB = 7200, 7204, 7208, 7212
O_GG, O_GM = 7216, 8240


class Sched:
    def __init__(self, nc):
        self.nc = nc
        self.prog = {e: [] for e in ENGS}
        self.cnt = {e: 0 for e in ENGS}
        self.seen = {e: {} for e in ENGS}
        self.lastw = {}
        self.readers = {}
        self.dma_i = 0
        self.dma_cnt = [0] * N_DMA_SEMS
        self.n_inst = 0
        self.idx = {e: 0 for e in ENGS}
        self.idx_of = {e: {} for e in ENGS}

    def _need(self, eng, reads, writes):
        need = {}

        def add(prod, c):
            if need.get(prod, 0) < c:
                need[prod] = c

        for k in reads:
            lw = self.lastw.get(k)
            if lw:
                add(*lw)
        for k in writes:
            lw = self.lastw.get(k)
            if lw:
                add(*lw)
            for p, c in self.readers.get(k, {}).items():
                add(p, c)
        out = []
        for p, c in need.items():
            if p == eng and (eng == "pe" or not SAME_ENG_SYNC):
                continue
            if p == eng and c > self.cnt[eng]:
                continue
            if p == eng and self.idx[eng] - self.idx_of[eng].get(c, -10 ** 9) > SAME_ENG_DIST:
                continue
            if self.seen[eng].get(p, 0) >= c:
                continue
            self.seen[eng][p] = c
            out.append((p, c))
        return out

    def _record(self, prod, c, reads, writes):
        for k in reads:
            d = self.readers.setdefault(k, {})
            if d.get(prod, 0) < c:
                d[prod] = c
        for k in writes:
            self.lastw[k] = (prod, c)
            self.readers[k] = {}

    def op(self, eng, fn, reads=(), writes=(), inc=True):
        for p, c in self._need(eng, reads, writes):
            self.prog[eng].append(("wait", p, c))
        c = self.cnt[eng] + 1
        if inc:
            self.cnt[eng] = c
        self.idx_of[eng][c] = self.idx[eng]
        self.idx[eng] += 1
        self.prog[eng].append(("op", fn, inc))
        self._record(eng, c, reads, writes)
        self.n_inst += 1

    def dma(self, eng, fn, reads=(), writes=()):
        s = self.dma_i % N_DMA_SEMS
        self.dma_i += 1
        prod = ("dma", s)
        waits = self._need(eng, reads, writes)
        prev = self.dma_cnt[s]
        if prev and self.seen[eng].get(prod, 0) < prev:
            self.seen[eng][prod] = prev
            waits.append((prod, prev))
        for p, c in waits:
            self.prog[eng].append(("wait", p, c))
        c = prev + 16
        self.dma_cnt[s] = c
        self.prog[eng].append(("dma", fn, s))
        self._record(prod, c, reads, writes)
        self.n_inst += 1

    def barrier(self):
        for e in ENGS:
            for p in ENGS:
                if p != e and self.cnt[p] > self.seen[e].get(p, 0):
                    self.seen[e][p] = self.cnt[p]
                    self.prog[e].append(("wait", p, self.cnt[p]))
            for s in range(N_DMA_SEMS):
                prod = ("dma", s)
                if self.dma_cnt[s] > self.seen[e].get(prod, 0):
                    self.seen[e][prod] = self.dma_cnt[s]
                    self.prog[e].append(("wait", prod, self.dma_cnt[s]))

    def finish(self):
        self.barrier()

    def run(self):
        nc = self.nc
        with contextlib.ExitStack() as st:
            esem = {e: st.enter_context(nc.semaphore("s_" + e)) for e in ENGS}
            dsem = [st.enter_context(nc.semaphore("d_%d" % i)) for i in range(N_DMA_SEMS)]
            block = st.enter_context(nc.Block())

            def semof(p):
                return dsem[p[1]] if isinstance(p, tuple) else esem[p]

            def replay(e, engobj):
                for it in self.prog[e]:
                    if it[0] == "wait":
                        engobj.wait_ge(semof(it[1]), it[2])
                    elif it[0] == "op":
                        ins = it[1](engobj)
                        if it[2]:
                            ins.then_inc(esem[e], 1)
                    else:
                        ins = it[1](engobj)
                        ins.then_inc(dsem[it[2]], 16)

            @block.sync
            def _(eng):
                replay("sp", eng)

            @block.scalar
            def _(eng):
                replay("act", eng)

            @block.vector
            def _(eng):
                replay("dve", eng)

            @block.gpsimd
            def _(eng):
                replay("pool", eng)

            @block.tensor
            def _(eng):
                replay("pe", eng)


class Rot:
    def __init__(self, tiles, name):
        self.tiles = tiles
        self.name = name
        self.i = 0

    def get(self):
        j = self.i % len(self.tiles)
        self.i += 1
        return self.tiles[j], "%s%d" % (self.name, j)


class _Stop(Exception):
    pass


def build_program(stop_at=None, dumps=()):
    nc = bass.Bass("TRN2", target_bir_lowering=False)
    dump_aps = {}

    def ck(name, env=None):
        for dn, (cname, fn) in dict(dumps).items():
            if cname == name:
                ap, keys = fn(env)
                dt = ap.dtype
                d = nc.dram_tensor("dbg_" + dn, list(ap.shape), dt, kind="ExternalOutput").ap()
                dump_aps[dn] = d
                S.dma("sp", lambda e, d=d, ap=ap: e.dma_start(out=d, in_=ap), keys, [])
        if stop_at == name:
            raise _Stop()

    def din(name, shape):
        return nc.dram_tensor(name, list(shape), F32, kind="ExternalInput").ap()

    def dout(name, shape):
        return nc.dram_tensor(name, list(shape), F32, kind="ExternalOutput").ap()

    x_in = [din("xp", [T, D]), din("xs", [T, D])]
    cond = din("cond", [16, 128])
    gs0 = din("gs0", [2, 2, 4, 128, 256])
    mc0 = din("mc0", [2, 2, 4, 256, 256])
    mn0 = din("mn0", [2, 2, 4, 256])
    mm0 = din("mm0", [2, 2, 4])
    w_ada = din("w_ada", [2, D, 9 * D])
    b_ada = din("b_ada", [2, 72, 128])
    w_g = [din("ffn1_w_gate", [2, D, DFF]), din("ffn2_w_gate", [2, D, DFF])]
    w_u = [din("ffn1_w_up", [2, D, DFF]), din("ffn2_w_up", [2, D, DFF])]
    w_d = [din("ffn1_w_down", [2, DFF, D]), din("ffn2_w_down", [2, DFF, D])]
    w_in = din("w_in", [2, D, DIN])
    w_decay = din("w_decay", [2, 2, 16, 512])
    b_decay = din("b_decay", [2, 2, 512])
    w_conv = din("w_conv", [96, 128])
    b_conv = din("b_conv", [32, 128])
    f_bias = din("f_bias", [2, 2, 4])
    gla_ng = din("gla_norm_g", [2, 256])
    ml_ng = din("mlstm_norm_g", [2, 256])
    w_brg = din("w_br_gla", [2, D, D])
    w_brm = din("w_br_mlstm", [2, D, D])
    w_out = din("w_out", [2, D, D])
    ln_g = din("ln_g", [48, 128])
    ln_b = din("ln_b", [48, 128])

    y_out = [dout("yp", [T, D]), dout("ys", [T, D])]
    ogs = dout("ogs", [4, 2, 2, 4, 128, 256])
    omc = dout("omc", [4, 2, 2, 4, 256, 256])
    omn = dout("omn", [4, 2, 2, 4, 256])
    omm = dout("omm", [4, 2, 2, 4])

    S = Sched(nc)
    st = contextlib.ExitStack()

    def sb(name, shape, dt=F32):
        return st.enter_context(nc.sbuf_tensor(name, list(shape), dt))

    def psm(name, shape, dt=F32):
        return st.enter_context(nc.psum_tensor(name, list(shape), dt))

    def act(out, in_, func, r, w, **kw):
        S.op("act", lambda e: e.activation(out=out, in_=in_, func=func, **kw), r, w)

    def acp(out, in_, r, w):
        S.op("act", lambda e: e.copy(out=out, in_=in_), r, w)

    def vcp(out, in_, r, w):
        S.op("dve", lambda e: e.tensor_copy(out=out, in_=in_), r, w)

    def tt(out, in0, in1, op, r, w):
        S.op("dve", lambda e: e.tensor_tensor(out=out, in0=in0, in1=in1, op=op), r, w)

    def ts(out, in0, s1, s2, op0, op1, r, w):
        if s2 is None:
            S.op("dve", lambda e: e.tensor_scalar(out=out, in0=in0, scalar1=s1, scalar2=None, op0=op0), r, w)
        else:
            S.op("dve", lambda e: e.tensor_scalar(out=out, in0=in0, scalar1=s1, scalar2=s2, op0=op0, op1=op1), r, w)

    def stt(out, in0, scalar, in1, op0, op1, r, w):
        S.op("dve", lambda e: e.scalar_tensor_tensor(out=out, in0=in0, scalar=scalar, in1=in1, op0=op0, op1=op1), r, w)

    def recip(out, in_, r, w):
        S.op("dve", lambda e: e.reciprocal(out=out, in_=in_), r, w)

    def scan(out, d0, d1, init, op0, op1, r, w):
        S.op("dve", lambda e: e.tensor_tensor_scan(out=out, data0=d0, data1=d1, initial=init, op0=op0, op1=op1), r, w)

    def mm(out, lhsT, rhs, start, stop, r, w, inc):
        S.op("pe", lambda e: e.matmul(out, lhsT=lhsT, rhs=rhs, start=start, stop=stop), r, w, inc=inc)

    def tr(out, in_, ident, r, w, inc):
        S.op("pe", lambda e: e.transpose(out=out, in_=in_, identity=ident), r, w, inc=inc)

    def pmemset(ap, val, w):
        S.op("pool", lambda e: e.memset(ap, val), (), w)

    def vmemset(ap, val, w):
        S.op("dve", lambda e: e.memset(ap, val), (), w)

    def dma_sp(out, in_, r, w):
        S.dma("sp", lambda e: e.dma_start(out=out, in_=in_), r, w)

    def dma_cast(out, in_, r, w):
        S.dma("pool", lambda e: e.dma_start(out=out, in_=in_), r, w)

    banks = [psm("bank%d" % i, [128, 512]) for i in range(7)]
    bank_rot = Rot(banks, "B")

    def bank():
        return bank_rot.get()

    identf = sb("identf", [128, 128])
    ones = sb("ones", [128, 128])
    onesm = sb("onesm", [128, 128])
    neg16 = sb("neg16", [128, 128])
    tric = [sb("tric0", [128, 128]), sb("tric1", [128, 128])]
    trir = [sb("trir0", [128, 128]), sb("trir1", [128, 128])]
    mask = [sb("mask0", [128, 128]), sb("mask1", [128, 128])]
    sel = sb("sel", [64, 4, 128])
    rows_stage = sb("rows_stage", [128, 128])

    pmemset(ones[:], 1.0, ["ones"])
    pmemset(onesm[:], 1.0 / 1024.0, ["onesm"])
    pmemset(neg16[:], -1.0 / 16.0, ["neg16"])

    def asel(out, in_, pattern, op, cm, r, w):
        S.op("pool", lambda e: e.affine_select(out=out, in_=in_, pattern=pattern, compare_op=op, fill=0.0,
                                               base=0, channel_multiplier=cm), r, w)

    asel(identf[:], ones[:], [[-1, 128]], ALU.is_equal, 1, ["ones"], ["identf"])
    asel(tric[0][:], neg16[:], [[1, 128]], ALU.is_ge, -1, ["neg16"], ["tric0"])
    asel(tric[1][:], neg16[:], [[-1, 128]], ALU.is_ge, 1, ["neg16"], ["tric1"])
    asel(trir[0][:], neg16[:], [[-1, 128]], ALU.is_gt, 1, ["neg16"], ["trir0"])
    asel(trir[1][:], neg16[:], [[1, 128]], ALU.is_gt, -1, ["neg16"], ["trir1"])
    asel(mask[0][:], ones[:], [[1, 128]], ALU.is_ge, -1, ["ones"], ["mask0"])
    asel(mask[1][:], ones[:], [[-1, 128]], ALU.is_ge, 1, ["ones"], ["mask1"])

    xT = sb("xT", [128, 8, T])
    hT = sb("hT", [128, 8, T], BF16)
    slab_bufs = [sb("slab%d" % i, [128, 2816], BF16) for i in range(3)]
    slab_rot = Rot(slab_bufs, "slab")
    t5_rot = Rot([sb("t5_%d" % i, [128, 512]) for i in range(3)], "t5")
    TAB = sb("TAB", [128, 2048])
    TA = TAB[:, 0:1024]
    TB = TAB[:, 1024:2048]
    ones4 = TAB[0:64, 1024:1536].rearrange("p (a b) -> p a b", a=4)
    pmemset(ones4, 1.0, ["TB"])
    for p0 in (0, 32):
        asel(sel[p0:p0 + 32], ones4[p0:p0 + 32], [[-1, 4], [0, 128]], ALU.is_equal, 1, ["TB"], ["sel%d" % p0])
    UNI = sb("UNI", [128, 32768], BF16)

    ADA = sb("ADA", [128, 2, 72, 2])
    ONEP = sb("ONEP", [128, 2, 72, 2])
    GH = sb("GH", [128, 2, 72, 2])
    BADA = sb("BADA", [128, 72])
    CONDT = sb("CONDT", [128, 16])
    SCB = sb("SCB", [128, 8, 2], BF16)
    LNG = sb("LNG", [128, 48])
    LNB = sb("LNB", [128, 48])
    WC = sb("WC", [128, 96])
    BC = sb("BC", [128, 32])
    S4 = sb("S4", [128, 4, 64])

    def uni_bf(off, n):
        return UNI[:, off:off + n]

    def uni_f32(off, n):
        return UNI[:, off:off + 2 * n].bitcast(F32)

    def xk(dc, th):
        return "xT%d_%d" % (dc, th)

    def hk(kc, th):
        return "hT%d_%d" % (kc, th)

    def T5():
        return t5_rot.get()

    def load_cols(dst, dkey, src, R):
        dma_sp(rows_stage[0:R, :], src, [], ["rows_stage"])
        b, bk = bank()
        tr(b[:, 0:R], rows_stage[0:R, :], identf[0:R, 0:R], ["rows_stage", "identf"], [bk], True)
        vcp(dst, b[:, 0:R], [bk], [dkey])

    load_cols(LNG[:], "LNG", ln_g, 48)
    load_cols(LNB[:], "LNB", ln_b, 48)
    load_cols(WC[:], "WC", w_conv, 96)
    load_cols(BC[:], "BC", b_conv, 32)
    load_cols(CONDT[:], "CONDT", cond, 16)
    act(SCB[:], CONDT[:].rearrange("p (a k) -> p k a", a=2), AF.Silu, ["CONDT"], ["SCB"])

    def slab(parts, KC, C):
        buf, key = slab_rot.get()
        v = buf[:, 0:KC * C].rearrange("p (k c) -> p k c", k=KC)
        for src, off in parts:
            c = src.shape[1]
            dma_cast(v[:, :, off:off + c], src.rearrange("(k p) c -> p k c", p=128), [], [key])
        return v, key

    pidx_i = sb("pidx_i", [128, 1], I32)
    pidx = sb("pidx", [128, 1])
    OM = sb("OM", [128, 2])
    nidx_i = TAB[:, 0:64].bitcast(I32)
    nidx = TAB[:, 64:128]
    U4 = TAB[:, 128:384].rearrange("p (a b) -> p a b", a=4)
    K4i = TAB[:, 384:640].bitcast(I32).rearrange("p (a b) -> p a b", a=4)
    K4 = TAB[:, 640:896].rearrange("p (a b) -> p a b", a=4)
    S.op("pool", lambda e: e.iota(pidx_i[:], pattern=[[0, 1]], base=0, channel_multiplier=1), (), ["pidx_i"])
    S.op("pool", lambda e: e.iota(nidx_i, pattern=[[1, 64]], base=0, channel_multiplier=0), (), ["nidx_i"])
    vcp(pidx[:], pidx_i[:], ["pidx_i"], ["pidx"])
    vcp(nidx, nidx_i, ["nidx_i"], ["nidx"])
    lk = math.log(10000.0) / 256.0
    for jc in range(2):
        act(OM[:, jc:jc + 1], pidx[:], AF.Exp, ["pidx"], ["OM"], scale=-lk, bias=-lk * 128.0 * jc)
    ts(OM[:], OM[:], 1.0 / (2.0 * math.pi), None, ALU.mult, None, ["OM"], ["OM"])
    for v in range(4):
        jc = v % 2
        ts(U4[:, v, :], nidx, OM[:, jc:jc + 1], None, ALU.mult, None, ["nidx", "OM"], ["U4"])
        if v >= 2:
            ts(U4[:, v, :], U4[:, v, :], 0.25, None, ALU.add, None, ["U4"], ["U4"])
    vcp(K4i, U4, ["U4"], ["K4i"])
    vcp(K4, K4i, ["K4i"], ["K4"])
    tt(U4, U4, K4, ALU.subtract, ["U4", "K4"], ["U4"])
    ts(K4, U4, 0.5, None, ALU.is_gt, None, ["U4"], ["K4"])
    tt(U4, U4, K4, ALU.subtract, ["U4", "K4"], ["U4"])
    ts(K4, U4, -0.5, None, ALU.is_lt, None, ["U4"], ["K4"])
    tt(U4, U4, K4, ALU.add, ["U4", "K4"], ["U4"])
    act(S4[:], U4, AF.Sin, ["U4"], ["S4"], scale=6.283185)

    BADA2 = [BADA, sb("BADA1", [128, 72])]
    for l in range(2):
        load_cols(BADA2[l][:], "BADA%d" % l, b_ada[l], 72)
    ada_pending = [(l, j) for l in range(2) for j in range(9)]

    def ada_chunk():
        if not ada_pending:
            return
        l, j = ada_pending.pop(0)
        ab, abk = bank()
        for sl in range(4):
            c0 = j * 1024 + sl * 256
            sv, sk = slab([(w_ada[l][:, c0:c0 + 256], 0)], 8, 256)
            for cg in range(2):
                col = sl * 2 + cg
                for kc in range(8):
                    mm(ab[:, col * 2:col * 2 + 2], sv[:, kc, cg * 128:(cg + 1) * 128], SCB[:, kc, :],
                       kc == 0, kc == 7, [sk, "SCB"], [abk], inc=(kc == 7 and cg == 1))
        js = slice(j * 8, (j + 1) * 8)
        tt(ADA[:, l, js, :], ab[:, 0:16].rearrange("p (c a) -> p c a", a=2),
           BADA2[l][:, js].unsqueeze(2).to_broadcast([128, 8, 2]), ALU.add, [abk, "BADA%d" % l], ["ADA%d_%d" % (l, j)])
        ts(ONEP[:, l, js, :], ADA[:, l, js, :], 1.0, None, ALU.add, None, ["ADA%d_%d" % (l, j)], ["ONEP%d_%d" % (l, j)])
        ts(GH[:, l, js, :], ADA[:, l, js, :], 0.5, None, ALU.mult, None, ["ADA%d_%d" % (l, j)], ["GH%d_%d" % (l, j)])

    for _ in range(3):
        ada_chunk()

    def adac(arr, l, j, dc, p):
        return arr[:, l, j * 8 + dc, p:p + 1]

    actT = uni_bf(0, 22 * T).rearrange("p (j t) -> p j t", j=22)
    ogT = uni_bf(0, 8 * T).rearrange("p (c t) -> p c t", c=8)
    hmT = uni_bf(8 * T, 8 * T).rearrange("p (c t) -> p c t", c=8)
    HB = UNI[:, 16384:32768]

    def hb_bf(off, n):
        return HB[:, off:off + n]

    def hb_f32(off, n):
        return HB[:, off:off + 2 * n].bitcast(F32)

    g_SP = [sb("GSP0", [128, 1024]), sb("GSP1", [128, 1024])]
    g_QD = [hb_bf(4096, 1024), hb_bf(5120, 1024)]
    g_KD = [hb_bf(6144, 1024), hb_bf(7168, 1024)]
    g_KW = [hb_bf(8192, 1024), hb_bf(9216, 1024)]
    g_V = hb_bf(10240, 2048).rearrange("p (t v) -> p t v", t=8)
    g_RG = hb_bf(12288, 2048).rearrange("p (t v) -> p t v", t=8)
    g_SST2 = [[sb("GSST%d_%d" % (q, d), [128, 256]) for d in range(2)] for q in range(2)]
    g_SBF = [hb_bf(15360, 256), hb_bf(15616, 256)]
    g_ATT = Rot([hb_bf(15872 + i * 128, 128) for i in range(4)], "gatt")
    m_QK = [hb_bf(0, 2048).rearrange("p (c t) -> p c t", c=2), hb_bf(2048, 2048).rearrange("p (c t) -> p c t", c=2)]
    m_KTOK = hb_bf(4096, 2048).rearrange("p (t c) -> p t c", t=8)
    m_OG = hb_bf(6144, 2048).rearrange("p (t v) -> p t v", t=8)
    m_QS = [hb_bf(8192, 2048).rearrange("p (c t) -> p c t", c=2), hb_bf(10240, 2048).rearrange("p (c t) -> p c t", c=2)]
    m_DTM = [hb_bf(12288, 1024), hb_bf(13312, 1024)]
    m_PT = Rot([hb_bf(14336 + i * 128, 128) for i in range(4)], "mpt")
    m_KWT = Rot([hb_bf(14848 + i * 256, 256) for i in range(4)], "mkw")
    MT = hb_bf(0, 8 * T).rearrange("p (c t) -> p c t", c=8)

    VAUG = sb("VAUG", [128, 8, 257], BF16)
    CST2 = [[sb("CST%d_%d" % (q, d), [128, 2, 257]) for d in range(2)] for q in range(2)]
    CBF = [sb("CBF0", [128, 2, 257], BF16), sb("CBF1", [128, 2, 257], BF16)]
    OGTMP = Rot([sb("ogtmp%d" % i, [128, 256]) for i in range(2)], "ogtmp")
    SSQ = sb("SSQ", [128, 8])
    RS = sb("RS", [128, 8])
    DEC = [sb("DEC0", [128, 8]), sb("DEC1", [128, 8])]
    SM = Rot([sb("sm%d" % i, [128, 1]) for i in range(4)], "sm")
    SLA = sb("SLA", [128, 8, 32], BF16)
    SLF = sb("SLF", [128, 8, 64], BF16)
    SLI = sb("SLI", [128, 8, 64], BF16)
    GNB = sb("GNB", [128, 256])
    MNB = sb("MNB", [128, 256])
    R1 = sb("R1", [64, T])
    R2 = sb("R2", [64, T])
    R3 = sb("R3", [64, T])
    R4 = sb("R4", [64, T])
    AT = R1[0:33, :]
    WDEC = R2[0:33, :].rearrange("p (d c) -> p d c", d=2)
    TOT = sb("TOT", [64, 8])
    MM = sb("MM", [64, 16])
    FB = sb("FB", [64, 1])
    NFB = sb("NFB", [64, 1])
    UCOL = sb("UCOL", [128, 8, 36])
    ECOL = sb("ECOL", [128, 8, 36])

    OSUM = TAB[:].rearrange("p (t v) -> p t v", t=8)

    def osk(tile):
        return "TA" if tile < 4 else "TB"

    pmemset(SLF[:], 0.0, ["SLF"])
    pmemset(SLI[:], 0.0, ["SLI"])
    pmemset(FB[:], 0.0, ["FB"])
    pmemset(VAUG[:, :, 256:257], 1.0, ["VAUGo"])

    LNT = Rot([uni_f32(24576 + i * 1024, 512) for i in range(4)], "lnt")
    lnm1 = uni_f32(24576 + 4 * 1024, 512)
    lnr1 = uni_f32(24576 + 5 * 1024, 512)
    lnm2 = uni_f32(24576 + 6 * 1024, 512)
    lnr2 = uni_f32(24576 + 7 * 1024, 512)
    LNG2 = sb("LNG2", [128, 8])
    LNB2 = sb("LNB2", [128, 8])

    def layernorm(l, i, mod):
        c0 = l * 24 + i * 8
        if mod is not None:
            l2, jsc, jsh, p = mod
            tt(LNG2[:], LNG[:, c0:c0 + 8], ONEP[:, l2, jsc * 8:(jsc + 1) * 8, p], ALU.mult, ["LNG", "ONEP%d_%d" % (l2, jsc)], ["LNG2"])
            tt(LNB2[:], LNB[:, c0:c0 + 8], ONEP[:, l2, jsc * 8:(jsc + 1) * 8, p], ALU.mult, ["LNB", "ONEP%d_%d" % (l2, jsc)], ["LNB2"])
            tt(LNB2[:], LNB2[:], ADA[:, l2, jsh * 8:(jsh + 1) * 8, p], ALU.add, ["LNB2", "ADA%d_%d" % (l2, jsh)], ["LNB2"])
        bms, bqs = [], []
        for th in range(2):
            ts_ = slice(th * 512, (th + 1) * 512)
            bm, bmk = bank()
            bq, bqk = bank()
            bms.append((bm, bmk))
            bqs.append((bq, bqk))
            for dc in range(8):
                t, tk = LNT.get()
                act(t, xT[:, dc, ts_], AF.Square, [xk(dc, th)], [tk])
                mm(bm[:], onesm[:], xT[:, dc, ts_], dc == 0, dc == 7, ["onesm", xk(dc, th)], [bmk], inc=(dc == 7))
                mm(bq[:], onesm[:], t, dc == 0, dc == 7, ["onesm", tk], [bqk], inc=True)
        mean = [lnm1, lnm2]
        rstd = [lnr1, lnr2]
        mk = ["lnm1", "lnm2"]
        rk = ["lnr1", "lnr2"]
        for th in range(2):
            acp(mean[th], bms[th][0][:], [bms[th][1]], [mk[th]])
        for th in range(2):
            act(rstd[th], bms[th][0][:], AF.Square, [bms[th][1]], [rk[th]])
        for th in range(2):
            tt(rstd[th], bqs[th][0][:], rstd[th], ALU.subtract, [bqs[th][1], rk[th]], [rk[th]])
        for th in range(2):
            ts(rstd[th], rstd[th], 0.0, None, ALU.max, None, [rk[th]], [rk[th]])
        for th in range(2):
            act(rstd[th], rstd[th], AF.Sqrt, [rk[th]], [rk[th]], bias=LN_EPS)
        for th in range(2):
            recip(rstd[th], rstd[th], [rk[th]], [rk[th]])
        items = [(dc, th) for th in range(2) for dc in range(8)]
        tmp = {}

        def st1(k):
            dc, th = items[k]
            ts_ = slice(th * 512, (th + 1) * 512)
            t, tk = LNT.get()
            tmp[k] = (t, tk)
            tt(t, xT[:, dc, ts_], mean[th], ALU.subtract, [xk(dc, th), mk[th]], [tk])

        def st2(k):
            dc, th = items[k]
            t, tk = tmp[k]
            tt(t, t, rstd[th], ALU.mult, [tk, rk[th]], [tk])

        def st3(k):
            dc, th = items[k]
            ts_ = slice(th * 512, (th + 1) * 512)
            t, tk = tmp[k]
            c = c0 + dc
            act(xT[:, dc, ts_], t, AF.Identity, [tk, "LNG", "LNB"], [xk(dc, th)], scale=LNG[:, c:c + 1], bias=LNB[:, c:c + 1])
            if mod is not None:
                act(hT[:, dc, ts_], t, AF.Identity, [tk, "LNG2", "LNB2"], [hk(dc, th)], scale=LNG2[:, dc:dc + 1],
                    bias=LNB2[:, dc:dc + 1])

        n = len(items)
        for k in range(n + 2):
            if k < n:
                st1(k)
            if 1 <= k <= n:
                st2(k - 1)
            if 2 <= k <= n + 1:
                st3(k - 2)

    def resid_update(b, bk, dc, th, gate_ap, gkey):
        ts_ = slice(th * 512, (th + 1) * 512)
        t, tk = T5()
        act(t[:], b[:], AF.Identity, [bk, gkey], [tk], scale=gate_ap)
        stt(xT[:, dc, ts_], xT[:, dc, ts_], ALPHA, t[:], ALU.mult, ALU.add, [xk(dc, th), tk], [xk(dc, th)])

    def ffn(l, which, p):
        wg, wu, wd = w_g[which][l], w_u[which][l], w_d[which][l]
        jg = 2 if which == 0 else 8
        for js in range(11):
            if js < (2 if which == 0 else 4):
                ada_chunk()
            sg, sgk = slab([(wg[:, js * 256:(js + 1) * 256], 0)], 8, 256)
            su, suk = slab([(wu[:, js * 256:(js + 1) * 256], 0)], 8, 256)
            for jj in range(2):
                jc = js * 2 + jj
                for th in range(2):
                    ts_ = slice(th * 512, (th + 1) * 512)
                    bg, bgk = bank()
                    bu, buk = bank()
                    for kc in range(8):
                        mm(bg[:], sg[:, kc, jj * 128:(jj + 1) * 128], hT[:, kc, ts_], kc == 0, kc == 7,
                           [sgk, hk(kc, th)], [bgk], inc=(kc == 7))
                    for kc in range(8):
                        mm(bu[:], su[:, kc, jj * 128:(jj + 1) * 128], hT[:, kc, ts_], kc == 0, kc == 7,
                           [suk, hk(kc, th)], [buk], inc=(kc == 7))
                    t, tk = T5()
                    act(t[:], bg[:], AF.Silu, [bgk], [tk])
                    tt(actT[:, jc, ts_], bu[:], t[:], ALU.mult, [buk, tk], ["act%d_%d" % (jc, th)])
        for ds in range(8):
            sd, sdk = slab([(wd[:, ds * 128:(ds + 1) * 128], 0)], 22, 128)
            for th in range(2):
                ts_ = slice(th * 512, (th + 1) * 512)
                b, bk = bank()
                for jc in range(22):
                    mm(b[:], sd[:, jc, :], actT[:, jc, ts_], jc == 0, jc == 21,
                       [sdk, "act%d_%d" % (jc, th)], [bk], inc=(jc == 21))
                resid_update(b, bk, ds, th, adac(GH, l, jg, ds, p), "GH%d_%d" % (l, jg))

    def head_epilogue(gate, dstT, h, ngkey):
        for tile in range(8):
            jt, jtk = T5()
            act(jt[:, 0:256], OSUM[:, tile, :], AF.Square, [osk(tile)], [jtk, "SSQ"], accum_out=SSQ[:, tile:tile + 1])
        ck("epA")
        ts(RS[:], SSQ[:], 1.0 / 256.0, None, ALU.mult, None, ["SSQ"], ["RS"])
        act(RS[:], RS[:], AF.Sqrt, ["RS"], ["RS"], bias=NORM_EPS)
        recip(RS[:], RS[:], ["RS"], ["RS"])
        ck("epB")
        for tile in range(8):
            og, ogk = OGTMP.get()
            stt(og[:], OSUM[:, tile, :], RS[:, tile:tile + 1], gate[:, tile, :], ALU.mult, ALU.mult,
                [osk(tile), "RS", ngkey], [ogk])
            ck("epC")
            pt, ptk = bank()
            for vc in range(2):
                tr(pt[:, vc * 128:(vc + 1) * 128], og[:, vc * 128:(vc + 1) * 128], identf[:], [ogk, "identf"], [ptk],
                   inc=(vc == 1))
            acp(dstT[:, h * 2:(h + 1) * 2, tile * 128:(tile + 1) * 128],
                pt[:, 0:256].rearrange("p (a b) -> p a b", a=2), [ptk], ["mixT"])
            ck("epD%d" % tile)

    def tok_proj(l, col0, dst, dkey, post):
        sv, sk = slab([(w_in[l][:, col0:col0 + 256], 0)], 8, 256)
        for pair in range(4):
            b, bk = bank()
            for j in range(2):
                tile = pair * 2 + j
                th = tile // 4
                for kc in range(8):
                    mm(b[:, j * 256:(j + 1) * 256], hT[:, kc, tile * 128:(tile + 1) * 128], sv[:, kc, :],
                       kc == 0, kc == 7, [sk, hk(kc, th)], [bk], inc=(kc == 7 and j == 1))
            post(b[:].rearrange("p (a v) -> p a v", a=2), bk, dst[:, pair * 2:(pair + 1) * 2, 0:256], dkey)

    def feat_proj(l, col0, ncols, dst_fn):
        sv, sk = slab([(w_in[l][:, col0:col0 + ncols], 0)], 8, ncols)
        for cc in range(ncols // 128):
            for th in range(2):
                b, bk = bank()
                for kc in range(8):
                    mm(b[:], sv[:, kc, cc * 128:(cc + 1) * 128], hT[:, kc, th * 512:(th + 1) * 512], kc == 0, kc == 7,
                       [sk, hk(kc, th)], [bk], inc=(kc == 7))
                dst_fn(cc, th, b, bk)

    def gla(l, pcfg):
        is_prompt, seqs, p = pcfg
        pmemset(AT[32:33, :], 1.0, ["AT1"])
        pmemset(WDEC, 0.0, ["WDEC"])
        dma_cast(SLA[:], w_in[l][:, O_A:O_A + 32].rearrange("(k p) c -> p k c", p=128), [], ["SLA"])
        for th in range(2):
            b, bk = bank()
            for kc in range(8):
                mm(b[0:32, :], SLA[:, kc, :], hT[:, kc, th * 512:(th + 1) * 512], kc == 0, kc == 7,
                   ["SLA", hk(kc, th)], [bk], inc=(kc == 7))
            acp(AT[0:32, th * 512:(th + 1) * 512], b[0:32, :], [bk], ["AT%d" % th])
        dma_sp(WDEC[0:16, 0, :], w_decay[l, 0], [], ["WDEC"])
        dma_sp(WDEC[16:32, 1, :], w_decay[l, 1], [], ["WDEC"])
        for d in range(2):
            dma_sp(WDEC[32:33, d, :], b_decay[l, d:d + 1, :], [], ["WDEC"])
        dma_sp(GNB[:], gla_ng[l].partition_broadcast(128), [], ["GNB"])
        ck("glaA")

        for h in range(4):
            ada_chunk()
            for d in range(2):
                for half in range(2):
                    b, bk = bank()
                    for j in range(4):
                        tile = half * 4 + j
                        mm(b[:, j * 128:(j + 1) * 128], AT[0:33, tile * 128:(tile + 1) * 128],
                           WDEC[0:33, d, h * 128:(h + 1) * 128], True, True,
                           ["AT%d" % half, "AT1", "WDEC"], [bk], inc=(j == 3))
                    t, tk = T5()
                    act(t[:], b[:], AF.Exp, [bk], [tk], scale=-1.0)
                    act(g_SP[d][:, half * 512:(half + 1) * 512], t[:], AF.Ln, [tk], ["gSP%d" % d], bias=1.0)
            def put_q(cc, th, b, bk):
                acp(TA[:, th * 512:(th + 1) * 512], b[:], [bk], ["TA"])

            def put_k(cc, th, b, bk):
                acp(TB[:, th * 512:(th + 1) * 512], b[:], [bk], ["TB"])

            feat_proj(l, O_QG + h * 128, 128, put_q)
            feat_proj(l, O_KG + h * 128, 128, put_k)

            def post_v(b3, bk, out3, dkey):
                vcp(out3, b3, [bk], [dkey])

            def post_r(b3, bk, out3, dkey):
                t, tk = T5()
                act(t[:].rearrange("p (a v) -> p a v", a=2), b3, AF.Silu, [bk], [tk])
                tt(out3, t[:].rearrange("p (a v) -> p a v", a=2), GNB[:].unsqueeze(1).to_broadcast([128, 2, 256]),
                   ALU.mult, [tk, "GNB"], [dkey])

            tok_proj(l, O_VG + h * 256, g_V, "gV", post_v)
            tok_proj(l, O_RG + h * 256, g_RG, "gRG", post_r)
            ck("glaB")

            for d in range(2):
                lastc = 127 if d == 0 else 0
                for half in range(2):
                    hs = slice(half * 512, (half + 1) * 512)
                    b, bk = bank()
                    for j in range(4):
                        tile = half * 4 + j
                        mm(b[:, j * 128:(j + 1) * 128], g_SP[d][:, tile * 128:(tile + 1) * 128], tric[d][:], True, True,
                           ["gSP%d" % d, "tric%d" % d], [bk], inc=(j == 3))
                    t, tk = T5()
                    act(t[:], b[:], AF.Exp, [bk], [tk])
                    vcp(DEC[d][:, half * 4:(half + 1) * 4], t[:, lastc::128], [tk], ["DEC%d" % d])
                    stt(g_QD[d][:, hs], TA[:, hs], 128.0 ** -0.5, t[:], ALU.mult, ALU.mult, ["TA", tk], ["gQD%d" % d])
                    t2, t2k = T5()
                    act(t2[:], b[:], AF.Exp, [bk], [t2k], scale=-1.0)
                    tt(g_KD[d][:, hs], TB[:, hs], t2[:], ALU.mult, ["TB", t2k], ["gKD%d" % d])
            for half in range(2):
                hs = slice(half * 512, (half + 1) * 512)
                bkT, bkTk = bank()
                for j in range(4):
                    tile = half * 4 + j
                    tr(bkT[:, j * 128:(j + 1) * 128], TB[:, tile * 128:(tile + 1) * 128], identf[:], ["TB", "identf"],
                       [bkTk], inc=(j == 3))
                for d in range(2):
                    b, bk = bank()
                    for j in range(4):
                        tile = half * 4 + j
                        mm(b[:, j * 128:(j + 1) * 128], trir[d][:], g_SP[d][:, tile * 128:(tile + 1) * 128], True, True,
                           ["gSP%d" % d, "trir%d" % d], [bk], inc=(j == 3))
                    t, tk = T5()
                    act(t[:], b[:], AF.Exp, [bk], [tk])
                    tt(g_KW[d][:, hs], bkT[:], t[:], ALU.mult, [bkTk, tk], ["gKW%d" % d])

            ck("glaC")
            written = set()
            for si, (t0, n) in enumerate(seqs):
                have = [False, False]
                g_SST = g_SST2[si % 2]
                sq_ = "q%d" % (si % 2)
                if not is_prompt:
                    for d in range(2):
                        dma_sp(g_SST[d][:], gs0[l, d, h], [], ["gSST%d" % d + sq_])
                        acp(g_SBF[d][:], g_SST[d][:], ["gSST%d" % d + sq_], ["gSBF%d" % d])
                        have[d] = True
                tl = lambda i, d: (t0 + i) if d == 0 else (t0 + n - 1 - i)
                need_upd = lambda i: (i < n - 1) or is_prompt
                stA, stB, stK, stO = {}, {}, {}, {}

                def A(i):
                    b, bk = bank()
                    for d in range(2):
                        tsl = slice(tl(i, d) * 128, (tl(i, d) + 1) * 128)
                        mm(b[:, d * 128:(d + 1) * 128], g_KD[d][:, tsl], g_QD[d][:, tsl], True, True,
                           ["gKD%d" % d, "gQD%d" % d], [bk], inc=(d == 1))
                    stA[i] = (b, bk)

                def B(i):
                    b, bk = stA[i]
                    for d in range(2):
                        am, amk = g_ATT.get()
                        tt(am[:], b[:, d * 128:(d + 1) * 128], mask[d][:], ALU.mult, [bk, "mask%d" % d], [amk])
                        stB[(i, d)] = (am, amk)

                def K(i):
                    if not need_upd(i):
                        return
                    b, bk = bank()
                    for d in range(2):
                        tile = tl(i, d)
                        tsl = slice(tile * 128, (tile + 1) * 128)
                        mm(b[:, d * 256:(d + 1) * 256], g_KW[d][:, tsl], g_V[:, tile, :], True, True,
                           ["gKW%d" % d, "gV"], [bk], inc=(d == 1))
                    stK[i] = (b, bk)

                def O(i):
                    b, bk = bank()
                    for d in range(2):
                        tile = tl(i, d)
                        tsl = slice(tile * 128, (tile + 1) * 128)
                        first = (i == 0 and not have[d])
                        am, amk = stB[(i, d)]
                        mm(b[:, d * 256:(d + 1) * 256], am[:], g_V[:, tile, :], True, first, [amk, "gV"], [bk],
                           inc=(first and d == 1))
                        if not first:
                            mm(b[:, d * 256:(d + 1) * 256], g_QD[d][:, tsl], g_SBF[d][:], False, True,
                               ["gQD%d" % d, "gSBF%d" % d], [bk], inc=(d == 1))
                    stO[i] = (b, bk)

                def E(i):
                    if not need_upd(i):
                        return
                    b, bk = stK[i]
                    for d in range(2):
                        tile = tl(i, d)
                        first = (i == 0 and not have[d])
                        if first:
                            acp(g_SST[d][:], b[:, d * 256:(d + 1) * 256], [bk], ["gSST%d" % d + sq_])
                        else:
                            stt(g_SST[d][:], g_SST[d][:], DEC[d][:, tile:tile + 1], b[:, d * 256:(d + 1) * 256],
                                ALU.mult, ALU.add, [bk, "gSST%d" % d + sq_, "DEC%d" % d], ["gSST%d" % d + sq_])
                    if i < n - 1:
                        for d in range(2):
                            acp(g_SBF[d][:], g_SST[d][:], ["gSST%d" % d + sq_], ["gSBF%d" % d])
                    if i == n - 1 and is_prompt:
                        for d in range(2):
                            dma_sp(ogs[si, l, d, h], g_SST[d][:], ["gSST%d" % d + sq_], [])

                def F(i):
                    b, bk = stO[i]
                    for d in range(2):
                        tile = tl(i, d)
                        if tile not in written:
                            written.add(tile)
                            acp(OSUM[:, tile, :], b[:, d * 256:(d + 1) * 256], [bk], [osk(tile)])
                        else:
                            tt(OSUM[:, tile, :], OSUM[:, tile, :], b[:, d * 256:(d + 1) * 256], ALU.add,
                               [bk, osk(tile)], [osk(tile)])

                A(0)
                B(0)
                K(0)
                for i in range(n):
                    if i + 1 < n:
                        A(i + 1)
                    O(i)
                    E(i)
                    if i + 1 < n:
                        B(i + 1)
                        K(i + 1)
                    F(i)
            ck("glaD")
            head_epilogue(g_RG, ogT, h, "gRG")
            ck("glaE")

    R4ALL = ["R4"] + ["R4t%d" % t for t in range(8)]
    R1ALL = ["R1"] + ["R1t%d" % t for t in range(8)]

    def mlstm(l, pcfg):
        is_prompt, seqs, p = pcfg
        L = seqs[0][1] * 128
        NS = len(seqs)
        dma_cast(SLA[:, :, 0:16], w_in[l][:, O_IF:O_IF + 16].rearrange("(k p) c -> p k c", p=128), [], ["SLA"])
        for (dst, dk_, c0, o) in ((SLF, "SLF", 0, 4), (SLF, "SLF", 32, 12), (SLI, "SLI", 0, 0), (SLI, "SLI", 32, 8)):
            vcp(dst[:, :, c0:c0 + 4], SLA[:, :, o:o + 4], ["SLA"], [dk_])
        for d in range(2):
            dma_sp(FB[d * 32:d * 32 + 4, 0:1], f_bias[l, d].rearrange("(p o) -> p o", o=1), [], ["FB"])
        ts(NFB[:], FB[:], -1.0, None, ALU.mult, None, ["FB"], ["NFB"])
        dma_sp(MNB[:], ml_ng[l].partition_broadcast(128), [], ["MNB"])
        for th in range(2):
            hs = slice(th * 512, (th + 1) * 512)
            bF, bFk = bank()
            for kc in range(8):
                mm(bF[0:64, :], SLF[:, kc, :], hT[:, kc, hs], kc == 0, kc == 7, ["SLF", hk(kc, th)], [bFk], inc=(kc == 7))
            act(R1[:, hs], bF[0:64, :], AF.Exp, [bFk, "NFB"], ["R1"], scale=-1.0, bias=NFB[:, 0:1])
            act(R1[:, hs], R1[:, hs], AF.Ln, ["R1"], ["R1"], bias=1.0)
            bI, bIk = bank()
            for kc in range(8):
                mm(bI[0:64, :], SLI[:, kc, :], hT[:, kc, hs], kc == 0, kc == 7, ["SLI", hk(kc, th)], [bIk], inc=(kc == 7))
            vcp(R3[:, hs], bI[0:64, :], [bIk], ["R3"])
        for tile in range(8):
            tsl = slice(tile * 128, (tile + 1) * 128)
            scan(R2[:, tsl], ones[0:64, :], R1[:, tsl], 0.0, ALU.mult, ALU.add, ["R1", "ones"], ["R2"])
        vcp(TOT[32:64, :], R2[32:64, 127::128], ["R2"], ["TOT"])
        tt(R4[32:64, :], R1[32:64, :], R2[32:64, :], ALU.subtract, ["R1", "R2"], ["R4"])
        tt(R2[32:64, :].rearrange("p (t s) -> p t s", t=8), R4[32:64, :].rearrange("p (t s) -> p t s", t=8),
           TOT[32:64, :].unsqueeze(2).to_broadcast([32, 8, 128]), ALU.add, ["R4", "TOT"], ["R2"])
        tt(R3[:], R3[:], R2[:], ALU.add, ["R3", "R2"], ["R3"])
        for tile in range(8):
            tsl = slice(tile * 128, (tile + 1) * 128)
            scan(R4[0:32, tsl], ones[0:32, :], R3[0:32, tsl], -1e30, ALU.mult, ALU.max, ["R3", "ones"], ["R4"])
            rsl = slice(tile * 128 + 127, tile * 128 - 1 if tile > 0 else None, -1)
            scan(R4[32:64, rsl], ones[32:64, :], R3[32:64, rsl], -1e30, ALU.mult, ALU.max, ["R3", "ones"], ["R4"])
        vmemset(MM[:], 0.0, ["MM"])
        col = 0
        cols = []
        for si, (t0, n) in enumerate(seqs):
            if not is_prompt:
                for d in range(2):
                    dma_sp(MM[d * 32:d * 32 + 4, col:col + 1], mm0[l, d].rearrange("(p o) -> p o", o=1), [], ["MM"])
            cols.append(col)
            col += n + 1
        nmax = max(n for (_, n) in seqs)
        for i in range(nmax):
            for si, (t0, n) in enumerate(seqs):
                if i >= n:
                    continue
                for d in range(2):
                    rs = slice(d * 32, d * 32 + 32)
                    c = cols[si] + i
                    tile = t0 + i if d == 0 else t0 + n - 1 - i
                    tsl = slice(tile * 128, (tile + 1) * 128)
                    lc = tile * 128 + (127 if d == 0 else 0)
                    mkey = "MM%d_%d" % (si, d)
                    ts(R4[rs, tsl], R4[rs, tsl], MM[rs, c:c + 1], None, ALU.max, None, ["R4", "R4t%d" % tile, "MM", mkey], ["R4t%d" % tile])
                    act(R1[rs, tsl], R4[rs, tsl], AF.Exp, ["R4t%d" % tile, "MM", mkey], ["R1t%d" % tile], scale=-1.0, bias=MM[rs, c:c + 1])
                    tt(MM[rs, c + 1:c + 2], R4[rs, lc:lc + 1], R2[rs, lc:lc + 1], ALU.subtract, ["R4t%d" % tile, "R2"], [mkey])
        for si, (t0, n) in enumerate(seqs):
            if is_prompt:
                for d in range(2):
                    c = cols[si] + n
                    dma_sp(omm[si, l, d].rearrange("(p o) -> p o", o=1), MM[d * 32:d * 32 + 4, c:c + 1],
                           ["MM", "MM%d_%d" % (si, d)], [])
        ck("mlA")
        tt(R2[:], R4[:], R2[:], ALU.subtract, R4ALL + ["R2"], ["R2"])
        act(R2[:], R2[:], AF.Exp, ["R2"], ["R2"], scale=-1.0)
        for (src, skey, dst, dkey) in ((R3, "R3", UCOL, "UCOL"), (R2, "R2", ECOL, "ECOL")):
            for half in range(2):
                b, bk = bank()
                for j in range(4):
                    tile = half * 4 + j
                    tr(b[:, j * 64:(j + 1) * 64], src[0:64, tile * 128:(tile + 1) * 128], identf[0:64, 0:64],
                       [skey, "identf"], [bk], inc=(j == 3))
                vcp(dst[:, half * 4:(half + 1) * 4, :], b[:, 0:256].rearrange("p (a c) -> p a c", a=4)[:, :, 0:36],
                    [bk], [dkey])

        ck("mlB")
        RAWv = TA.rearrange("p (s l) -> p s l", s=NS)
        ACCv = TB.rearrange("p (s l) -> p s l", s=NS)
        for h in range(4):
            ada_chunk()
            scr = [(TA, TB, "TA", "TB"),
                   (m_QS[0][:].rearrange("p c t -> p (c t)").bitcast(F32), m_QS[1][:].rearrange("p c t -> p (c t)").bitcast(F32),
                    "mQS0", "mQS1")]
            units = [(which, cc) for which in range(2) for cc in range(2)]
            slabs_qk = {}

            def s1(u):
                which, cc = units[u]
                RAW, ACC, rk_, ak_ = scr[u % 2]
                if which not in slabs_qk:
                    slabs_qk[which] = slab([(w_in[l][:, O_QM + which * 1024 + h * 256:O_QM + which * 1024 + (h + 1) * 256], 0)],
                                           8, 256)
                sv, sk = slabs_qk[which]
                for th in range(2):
                    b, bk = bank()
                    for kc in range(8):
                        mm(b[:], sv[:, kc, cc * 128:(cc + 1) * 128], hT[:, kc, th * 512:(th + 1) * 512], kc == 0,
                           kc == 7, [sk, hk(kc, th)], [bk], inc=(kc == 7))
                    acp(RAW[:, th * 512:(th + 1) * 512], b[:], [bk], [rk_])

            def s2(u):
                which, cc = units[u]
                RAW, ACC, rk_, ak_ = scr[u % 2]
                RAWv = RAW.rearrange("p (s l) -> p s l", s=NS)
                ACCv = ACC.rearrange("p (s l) -> p s l", s=NS)
                ch = l * 48 + which * 8 + h * 2 + cc
                bch = l * 16 + which * 8 + h * 2 + cc
                ts(ACC, RAW, WC[:, ch + 16:ch + 17], BC[:, bch:bch + 1], ALU.mult, ALU.add, [rk_, "WC", "BC"], [ak_])
                stt(ACCv[:, :, 1:L], RAWv[:, :, 0:L - 1], WC[:, ch:ch + 1], ACCv[:, :, 1:L], ALU.mult, ALU.add,
                    [rk_, ak_, "WC"], [ak_])
                stt(ACCv[:, :, 0:L - 1], RAWv[:, :, 1:L], WC[:, ch + 32:ch + 33], ACCv[:, :, 0:L - 1], ALU.mult, ALU.add,
                    [rk_, ak_, "WC"], [ak_])

            def s3(u):
                which, cc = units[u]
                RAW, ACC, rk_, ak_ = scr[u % 2]
                act(m_QK[which][:, cc, :], ACC, AF.Silu, [ak_], ["mQK%d" % which])
                if which == 1:
                    act(RAW, ACC, AF.Silu, [ak_], [rk_])
                    for half in range(2):
                        b, bk = bank()
                        for j in range(4):
                            tile = half * 4 + j
                            tr(b[:, j * 128:(j + 1) * 128], RAW[:, tile * 128:(tile + 1) * 128], identf[:],
                               [rk_, "identf"], [bk], inc=(j == 3))
                        vcp(m_KTOK[:, half * 4:(half + 1) * 4, cc * 128:(cc + 1) * 128],
                            b[:].rearrange("p (a c) -> p a c", a=4), [bk], ["mKTOK"])

            def post_v(b3, bk, out3, dkey):
                vcp(out3, b3, [bk], [dkey])

            def post_o(b3, bk, out3, dkey):
                t, tk = T5()
                act(t[:].rearrange("p (a v) -> p a v", a=2), b3, AF.Sigmoid, [bk], [tk])
                tt(out3, t[:].rearrange("p (a v) -> p a v", a=2), MNB[:].unsqueeze(1).to_broadcast([128, 2, 256]),
                   ALU.mult, [tk, "MNB"], [dkey])

            s1(0)
            s1(1)
            s2(0)
            s2(1)
            tok_proj(l, O_VM + h * 256, VAUG, "VAUG", post_v)
            s3(0)
            s1(2)
            s3(1)
            s1(3)
            s2(2)
            s2(3)
            tok_proj(l, O_OM + h * 256, m_OG, "mOG", post_o)
            s3(2)
            s3(3)

            ck("mlC")
            for d in range(2):
                r = d * 32 + h
                rs4 = slice(d * 32, d * 32 + 4)
                lastc = 127 if d == 0 else 0
                for th in range(2):
                    hs = slice(th * 512, (th + 1) * 512)
                    bg, bgk = bank()
                    mm(bg[:], sel[rs4, h, :], R4[rs4, hs], True, True, ["sel%d" % (d * 32)] + R4ALL, [bgk], inc=True)
                    t, tk = T5()
                    for j in range(4):
                        tile = th * 4 + j
                        act(t[:, j * 128:(j + 1) * 128], bg[:, j * 128:(j + 1) * 128], AF.Exp, [bgk, "UCOL"], [tk],
                            scale=-1.0, bias=UCOL[:, tile, r:r + 1])
                    tt(m_DTM[d][:, hs].rearrange("p (a s) -> p a s", a=4), t[:].rearrange("p (a s) -> p a s", a=4),
                       mask[d][:].unsqueeze(1).to_broadcast([128, 4, 128]), ALU.mult, [tk, "mask%d" % d], ["mDTM%d" % d])
                    bi, bik = bank()
                    mm(bi[:], sel[rs4, h, :], R1[rs4, hs], True, True, ["sel%d" % (d * 32)] + R1ALL, [bik], inc=True)
                    vcp(DEC[d][:, th * 4:(th + 1) * 4], bi[:, lastc::128], [bik], ["DEC%d" % d])
                    for cc in range(2):
                        stt(m_QS[d][:, cc, hs], m_QK[0][:, cc, hs], 1.0 / 16.0, bi[:], ALU.mult, ALU.mult,
                            ["mQK0", bik], ["mQS%d" % d])

            ck("mlD")
            written = set()
            for si, (t0, n) in enumerate(seqs):
                have = [False, False]
                CST = CST2[si % 2]
                sq_ = "q%d" % (si % 2)
                if not is_prompt:
                    for d in range(2):
                        dma_sp(CST[d][:, :, 0:256], mc0[l, d, h].rearrange("(c p) v -> p c v", p=128), [], ["CST%d" % d + sq_])
                        for cc in range(2):
                            dma_sp(CST[d][:, cc, 256:257],
                                   mn0[l, d, h, cc * 128:(cc + 1) * 128].rearrange("(p o) -> p o", o=1), [], ["CST%d" % d + sq_])
                        acp(CBF[d][:], CST[d][:], ["CST%d" % d + sq_], ["CBF%d" % d])
                        have[d] = True
                tl = lambda i, d: (t0 + i) if d == 0 else (t0 + n - 1 - i)
                need_upd = lambda i: (i < n - 1) or is_prompt
                stA, stB, stC = {}, {}, {}

                def A(i):
                    b, bk = bank()
                    for d in range(2):
                        tsl = slice(tl(i, d) * 128, (tl(i, d) + 1) * 128)
                        for cc in range(2):
                            mm(b[:, d * 128:(d + 1) * 128], m_QK[1][:, cc, tsl], m_QK[0][:, cc, tsl], cc == 0, cc == 1,
                               ["mQK0", "mQK1"], [bk], inc=(cc == 1 and d == 1))
                    stA[i] = (b, bk)

                def B(i):
                    b, bk = stA[i]
                    for d in range(2):
                        tile = tl(i, d)
                        tsl = slice(tile * 128, (tile + 1) * 128)
                        pt, ptk = m_PT.get()
                        stt(pt[:], b[:, d * 128:(d + 1) * 128], 1.0 / 16.0, m_DTM[d][:, tsl], ALU.mult, ALU.mult,
                            [bk, "mDTM%d" % d], [ptk])
                        stB[(i, d)] = (pt, ptk)
                    if need_upd(i):
                        for d in range(2):
                            tile = tl(i, d)
                            lc = tile * 128 + (127 if d == 0 else 0)
                            kw, kwk = m_KWT.get()
                            ts(kw[:], m_KTOK[:, tile, :], m_DTM[d][:, lc:lc + 1], None, ALU.mult, None,
                               ["mKTOK", "mDTM%d" % d], [kwk])
                            stB[(i, d, "kw")] = (kw, kwk)

                def C(i):
                    if not need_upd(i):
                        return
                    for d in range(2):
                        tile = tl(i, d)
                        kw, kwk = stB[(i, d, "kw")]
                        for cc in range(2):
                            bc, bck = bank()
                            mm(bc[:, 0:257], kw[:, cc * 128:(cc + 1) * 128], VAUG[:, tile, :], True, True,
                               [kwk, "VAUG", "VAUGo"], [bck], inc=True)
                            stC[(i, d, cc)] = (bc, bck)

                def Dn(i):
                    for d in range(2):
                        tile = tl(i, d)
                        tsl = slice(tile * 128, (tile + 1) * 128)
                        first = (i == 0 and not have[d])
                        pt, ptk = stB[(i, d)]
                        bn, bnk = bank()
                        mm(bn[:, 0:257], pt[:], VAUG[:, tile, :], True, first, [ptk, "VAUG", "VAUGo"], [bnk], inc=first)
                        if not first:
                            for cc in range(2):
                                mm(bn[:, 0:257], m_QS[d][:, cc, tsl], CBF[d][:, cc, :], False, cc == 1,
                                   ["mQS%d" % d, "CBF%d" % d], [bnk], inc=(cc == 1))
                        stB[(i, d, "bn")] = (bn, bnk)

                def E(i):
                    if not need_upd(i):
                        return
                    for d in range(2):
                        tile = tl(i, d)
                        first = (i == 0 and not have[d])
                        for cc in range(2):
                            bc, bck = stC[(i, d, cc)]
                            if first:
                                acp(CST[d][:, cc, :], bc[:, 0:257], [bck], ["CST%d" % d + sq_])
                            else:
                                stt(CST[d][:, cc, :], CST[d][:, cc, :], DEC[d][:, tile:tile + 1], bc[:, 0:257],
                                    ALU.mult, ALU.add, [bck, "CST%d" % d + sq_, "DEC%d" % d], ["CST%d" % d + sq_])
                    if i < n - 1:
                        for d in range(2):
                            acp(CBF[d][:], CST[d][:], ["CST%d" % d + sq_], ["CBF%d" % d])
                    if i == n - 1 and is_prompt:
                        for d in range(2):
                            dma_sp(omc[si, l, d, h].rearrange("(c p) v -> p c v", p=128), CST[d][:, :, 0:256],
                                   ["CST%d" % d + sq_], [])
                            for cc in range(2):
                                dma_sp(omn[si, l, d, h, cc * 128:(cc + 1) * 128].rearrange("(p o) -> p o", o=1),
                                       CST[d][:, cc, 256:257], ["CST%d" % d + sq_], [])

                def F(i):
                    ds_ = []
                    for d in range(2):
                        r = d * 32 + h
                        tile = tl(i, d)
                        bn, bnk = stB[(i, d, "bn")]
                        d1, d1k = SM.get()
                        tt(d1[:], bn[:, 256:257], ECOL[:, tile, r:r + 1], ALU.max, [bnk, "ECOL"], [d1k])
                        ds_.append((d1, d1k, bn, bnk, tile))
                    for (d1, d1k, bn, bnk, tile) in ds_:
                        stt(d1[:], bn[:, 256:257], -1.0, d1[:], ALU.mult, ALU.max, [bnk, d1k], [d1k])
                    for (d1, d1k, bn, bnk, tile) in ds_:
                        recip(d1[:], d1[:], [d1k], [d1k])
                    for (d1, d1k, bn, bnk, tile) in ds_:
                        if tile not in written:
                            written.add(tile)
                            act(OSUM[:, tile, :], bn[:, 0:256], AF.Identity, [bnk, d1k], [osk(tile)], scale=d1[:, 0:1])
                        else:
                            stt(OSUM[:, tile, :], bn[:, 0:256], d1[:, 0:1], OSUM[:, tile, :], ALU.mult, ALU.add,
                                [bnk, d1k, osk(tile)], [osk(tile)])

                A(0)
                B(0)
                C(0)
                for i in range(n):
                    if i + 1 < n:
                        A(i + 1)
                    Dn(i)
                    E(i)
                    if i + 1 < n:
                        B(i + 1)
                        C(i + 1)
                    F(i)
            ck("mlE")
            head_epilogue(m_OG, hmT, h, "mOG")

    def merge_out(l, p):
        ada_chunk()
        for dc in range(8):
            cs = slice(dc * 128, (dc + 1) * 128)
            sA, sAk = slab([(w_brg[l][:, cs], 0), (w_brm[l][:, cs], 128)], 8, 256)
            sB, sBk = slab([(w_in[l][:, O_GG + dc * 128:O_GG + (dc + 1) * 128], 0),
                            (w_in[l][:, O_GM + dc * 128:O_GM + (dc + 1) * 128], 128)], 8, 256)
            for th in range(2):
                hs = slice(th * 512, (th + 1) * 512)
                byg, bygk = bank()
                bym, bymk = bank()
                bgg, bggk = bank()
                bgm, bgmk = bank()
                for kc in range(8):
                    mm(byg[:], sA[:, kc, 0:128], ogT[:, kc, hs], kc == 0, kc == 7, [sAk, "mixT"], [bygk], inc=(kc == 7))
                for kc in range(8):
                    mm(bym[:], sA[:, kc, 128:256], hmT[:, kc, hs], kc == 0, kc == 7, [sAk, "mixT"], [bymk], inc=(kc == 7))
                for kc in range(8):
                    mm(bgg[:], sB[:, kc, 0:128], hT[:, kc, hs], kc == 0, kc == 7, [sBk, hk(kc, th)], [bggk], inc=(kc == 7))
                for kc in range(8):
                    mm(bgm[:], sB[:, kc, 128:256], hT[:, kc, hs], kc == 0, kc == 7, [sBk, hk(kc, th)], [bgmk], inc=(kc == 7))
                t1, t1k = T5()
                act(t1[:], bgg[:], AF.Sigmoid, [bggk], [t1k])
                tt(t1[:], byg[:], t1[:], ALU.mult, [bygk, t1k], [t1k])
                t2, t2k = T5()
                act(t2[:], bgm[:], AF.Sigmoid, [bgmk], [t2k])
                tt(t2[:], bym[:], t2[:], ALU.mult, [bymk, t2k], [t2k])
                tt(MT[:, dc, hs], t1[:], t2[:], ALU.add, [t1k, t2k], ["MT%d" % th])
        for dc in range(8):
            so, sok = slab([(w_out[l][:, dc * 128:(dc + 1) * 128], 0)], 8, 128)
            for th in range(2):
                b, bk = bank()
                for kc in range(8):
                    mm(b[:], so[:, kc, :], MT[:, kc, th * 512:(th + 1) * 512], kc == 0, kc == 7, [sok, "MT%d" % th], [bk],
                       inc=(kc == 7))
                resid_update(b, bk, dc, th, adac(ADA, l, 5, dc, p), "ADA%d_5" % l)

    try:
        ck("setup", locals())
        S.barrier()
        for p in range(2):
            is_prompt = (p == 0)
            seqs = [(0, 2), (2, 2), (4, 2), (6, 2)] if is_prompt else [(0, 8)]
            pcfg = (is_prompt, seqs, p)
            for tile in range(8):
                stg = TA if tile % 2 == 0 else TB
                sk_ = "TA" if tile % 2 == 0 else "TB"
                dma_sp(stg, x_in[p][tile * 128:(tile + 1) * 128, :], [], [sk_])
                for half in range(2):
                    b, bk = bank()
                    for j in range(4):
                        dc = half * 4 + j
                        tr(b[:, j * 128:(j + 1) * 128], stg[:, dc * 128:(dc + 1) * 128], identf[:], [sk_, "identf"], [bk],
                           inc=(j == 3))
                    acp(xT[:, half * 4:(half + 1) * 4, tile * 128:(tile + 1) * 128],
                        b[:].rearrange("p (a t) -> p a t", a=4), [bk],
                        [xk(dc_, tile // 4) for dc_ in range(half * 4, half * 4 + 4)])
            if not is_prompt:
                for dc in range(8):
                    v = dc % 4
                    if dc < 4:
                        in1 = S4[:, v, 0:16].unsqueeze(2).to_broadcast([128, 16, 64])
                    else:
                        in1 = S4[:, v, :].unsqueeze(1).to_broadcast([128, 16, 64])
                    xv = xT[:, dc, :].rearrange("p (r c) -> p r c", r=16)
                    tt(xv, xv, in1, ALU.add, [xk(dc, 0), xk(dc, 1), "S4"], [xk(dc, 0), xk(dc, 1)])
            for dc in range(8):
                for th in range(2):
                    ts(hT[:, dc, th * 512:(th + 1) * 512], xT[:, dc, th * 512:(th + 1) * 512], adac(ONEP, 0, 1, dc, p),
                       adac(ADA, 0, 0, dc, p), ALU.mult, ALU.add, [xk(dc, th), "ONEP0_1", "ADA0_0"], [hk(dc, th)])
            ck("loadx%d" % p, locals())
            for l in range(2):
                S.barrier()
                ffn(l, 0, p)
                ck("ffn1_%d_%d" % (p, l), locals())
                layernorm(l, 0, (l, 4, 3, p))
                ck("ln0_%d_%d" % (p, l), locals())
                S.barrier()
                gla(l, pcfg)
                ck("gla_%d_%d" % (p, l), locals())
                S.barrier()
                mlstm(l, pcfg)
                ck("mlstm_%d_%d" % (p, l), locals())
                S.barrier()
                merge_out(l, p)
                ck("merge_%d_%d" % (p, l), locals())
                layernorm(l, 1, (l, 7, 6, p))
                ck("ln1_%d_%d" % (p, l), locals())
                S.barrier()
                ffn(l, 1, p)
                layernorm(l, 2, (1, 1, 0, p) if l == 0 else None)
                ck("ln2_%d_%d" % (p, l), locals())
            S.barrier()
            for tile in range(8):
                stg = TA if tile % 2 == 0 else TB
                sk_ = "TA" if tile % 2 == 0 else "TB"
                for half in range(2):
                    b, bk = bank()
                    for j in range(4):
                        dc = half * 4 + j
                        tr(b[:, j * 128:(j + 1) * 128], xT[:, dc, tile * 128:(tile + 1) * 128], identf[:],
                           [xk(dc, tile // 4), "identf"], [bk], inc=(j == 3))
                    acp(stg[:, half * 512:(half + 1) * 512], b[:], [bk], [sk_])
                dma_sp(y_out[p][tile * 128:(tile + 1) * 128, :], stg, [sk_], [])
    except _Stop:
        pass

    S.finish()
    S.run()
    sbuf_left = nc.sbuf_bytes_remaining
    st.close()
    S.sbuf_left = sbuf_left
    return nc, S, dump_aps


_CACHE = {}


def kernel(x_prompt, x_sample, c, state_gla_s, state_mlstm_c, state_mlstm_n, state_mlstm_m, c_ctx,
           w_ada, b_ada, ffn1_w_gate, ffn1_w_up, ffn1_w_down, w_in, w_decay, b_decay, w_conv, b_conv,
           f_bias, gla_norm_g, mlstm_norm_g, w_br_gla, w_br_mlstm, w_out,
           ffn2_w_gate, ffn2_w_up, ffn2_w_down, ln_g, ln_b):
    f = lambda a: np.ascontiguousarray(np.asarray(a, dtype=np.float32))
    if "nc" not in _CACHE:
        _CACHE["nc"] = build_program()[0]
    nc = _CACHE["nc"]
    shared = {
        "w_ada": f(w_ada), "b_ada": f(b_ada).reshape(2, 72, 128),
        "ffn1_w_gate": f(ffn1_w_gate), "ffn1_w_up": f(ffn1_w_up), "ffn1_w_down": f(ffn1_w_down),
        "ffn2_w_gate": f(ffn2_w_gate), "ffn2_w_up": f(ffn2_w_up), "ffn2_w_down": f(ffn2_w_down),
        "w_in": f(w_in), "w_decay": f(w_decay), "b_decay": f(b_decay),
        "w_conv": f(w_conv).reshape(96, 128), "b_conv": f(b_conv).reshape(32, 128),
        "f_bias": f(f_bias), "gla_norm_g": f(gla_norm_g), "mlstm_norm_g": f(mlstm_norm_g),
        "w_br_gla": f(w_br_gla), "w_br_mlstm": f(w_br_mlstm), "w_out": f(w_out),
        "ln_g": f(ln_g).reshape(48, 128), "ln_b": f(ln_b).reshape(48, 128),
    }
    xp = f(x_prompt)
    xs = f(x_sample)
    cc = f(c)
    cctx = f(c_ctx)
    in_maps = []
    for i in range(8):
        m = dict(shared)
        m["xp"] = xp[4 * i:4 * i + 4].reshape(T, D)
        m["xs"] = xs[i]
        m["cond"] = np.ascontiguousarray(np.concatenate([cctx.reshape(8, 128), cc[i].reshape(8, 128)], axis=0))
        m["gs0"] = f(state_gla_s[i])
        m["mc0"] = f(state_mlstm_c[i])
        m["mn0"] = f(state_mlstm_n[i])
        m["mm0"] = f(state_mlstm_m[i])
        in_maps.append(m)
    res = run_bass_kernel_spmd(nc, in_maps, core_ids=list(range(8)))
    r = res.results
    y_prompt = np.concatenate([r[i]["yp"].reshape(4, 256, D) for i in range(8)], axis=0)
    y_sample = np.stack([r[i]["ys"] for i in range(8)], axis=0)
    new_gla_s = np.concatenate([r[i]["ogs"] for i in range(8)], axis=0)
    new_mlstm_c = np.concatenate([r[i]["omc"] for i in range(8)], axis=0)
    new_mlstm_n = np.concatenate([r[i]["omn"] for i in range(8)], axis=0)
    new_mlstm_m = np.concatenate([r[i]["omm"] for i in range(8)], axis=0)
    return (y_prompt.astype(np.float32), y_sample.astype(np.float32), new_gla_s.astype(np.float32),
            new_mlstm_c.astype(np.float32), new_mlstm_n.astype(np.float32), new_mlstm_m.astype(np.float32))
```

```python
import contextlib
import math
import numpy as np
import concourse.bass as bass
import concourse.mybir as mybir
from concourse.bass_utils import run_bass_kernel_spmd

F32 = mybir.dt.float32
BF16 = mybir.dt.bfloat16
I32 = mybir.dt.int32
ALU = mybir.AluOpType
AF = mybir.ActivationFunctionType

ENGS = ("pe", "act", "dve", "pool", "sp")
SAME_ENG_SYNC = True
SAME_ENG_DIST = 1
N_DMA_SEMS = 40

D = 1024
T = 1024
DFF = 2816
DIN = 9264
ALPHA = 4.0 ** 0.25
LN_EPS = 1e-5
NORM_EPS = 1e-6
O_QG, O_KG, O_VG, O_RG, O_A = 0, 512, 1024, 2048, 3072
O_QM, O_KM, O_VM, O_OM = 3104, 4128, 5152, 6176
O_IF, O_FF, O_IB, O_FB = 7200, 7204, 7208, 7212
O_GG, O_GM = 7216, 8240


class Sched:
    def __init__(self, nc):
        self.nc = nc
        self.prog = {e: [] for e in ENGS}
        self.cnt = {e: 0 for e in ENGS}
        self.seen = {e: {} for e in ENGS}
        self.lastw = {}
        self.readers = {}
        self.dma_i = 0
        self.dma_cnt = [0] * N_DMA_SEMS
        self.n_inst = 0
        self.idx = {e: 0 for e in ENGS}
        self.idx_of = {e: {} for e in ENGS}

    def _need(self, eng, reads, writes):
        need = {}

        def add(prod, c):
            if need.get(prod, 0) < c:
                need[prod] = c

        for k in reads:
            lw = self.lastw.get(k)
            if lw:
                add(*lw)
        for k in writes:
            lw = self.lastw.get(k)
            if lw:
                add(*lw)
            for p, c in self.readers.get(k, {}).items():
                add(p, c)
        out = []
        for p, c in need.items():
            if p == eng and (eng == "pe" or not SAME_ENG_SYNC):
                continue
            if p == eng and c > self.cnt[eng]:
                continue
            if p == eng and self.idx[eng] - self.idx_of[eng].get(c, -10 ** 9) > SAME_ENG_DIST:
                continue
            if self.seen[eng].get(p, 0) >= c:
                continue
            self.seen[eng][p] = c
            out.append((p, c))
        return out

    def _record(self, prod, c, reads, writes):
        for k in reads:
            d = self.readers.setdefault(k, {})
            if d.get(prod, 0) < c:
                d[prod] = c
        for k in writes:
            self.lastw[k] = (prod, c)
            self.readers[k] = {}

    def op(self, eng, fn, reads=(), writes=(), inc=True):
        for p, c in self._need(eng, reads, writes):
            self.prog[eng].append(("wait", p, c))
        c = self.cnt[eng] + 1
        if inc:
            self.cnt[eng] = c
        self.idx_of[eng][c] = self.idx[eng]
        self.idx[eng] += 1
        self.prog[eng].append(("op", fn, inc))
        self._record(eng, c, reads, writes)
        self.n_inst += 1

    def dma(self, eng, fn, reads=(), writes=()):
        s = self.dma_i % N_DMA_SEMS
        self.dma_i += 1
        prod = ("dma", s)
        waits = self._need(eng, reads, writes)
        prev = self.dma_cnt[s]
        if prev and self.seen[eng].get(prod, 0) < prev:
            self.seen[eng][prod] = prev
            waits.append((prod, prev))
        for p, c in waits:
            self.prog[eng].append(("wait", p, c))
        c = prev + 16
        self.dma_cnt[s] = c
        self.prog[eng].append(("dma", fn, s))
        self._record(prod, c, reads, writes)
        self.n_inst += 1

    def barrier(self):
        for e in ENGS:
            for p in ENGS:
                if p != e and self.cnt[p] > self.seen[e].get(p, 0):
                    self.seen[e][p] = self.cnt[p]
                    self.prog[e].append(("wait", p, self.cnt[p]))
            for s in range(N_DMA_SEMS):
                prod = ("dma", s)
                if self.dma_cnt[s] > self.seen[e].get(prod, 0):
                    self.seen[e][prod] = self.dma_cnt[s]
                    self.prog[e].append(("wait", prod, self.dma_cnt[s]))

    def finish(self):
        self.barrier()

    def run(self):
        nc = self.nc
        with contextlib.ExitStack() as st:
            esem = {e: st.enter_context(nc.semaphore("s_" + e)) for e in ENGS}
            dsem = [st.enter_context(nc.semaphore("d_%d" % i)) for i in range(N_DMA_SEMS)]
            block = st.enter_context(nc.Block())

            def semof(p):
                return dsem[p[1]] if isinstance(p, tuple) else esem[p]

            def replay(e, engobj):
                for it in self.prog[e]:
                    if it[0] == "wait":
                        engobj.wait_ge(semof(it[1]), it[2])
                    elif it[0] == "op":
                        ins = it[1](engobj)
                        if it[2]:
                            ins.then_inc(esem[e], 1)
                    else:
                        ins = it[1](engobj)
                        ins.then_inc(dsem[it[2]], 16)

            @block.sync
            def _(eng):
                replay("sp", eng)

            @block.scalar
            def _(eng):
                replay("act", eng)

            @block.vector
            def _(eng):
                replay("dve", eng)

            @block.gpsimd
            def _(eng):
                replay("pool", eng)

            @block.tensor
            def _(eng):
                replay("pe", eng)


class Rot:
    def __init__(self, tiles, name):
        self.tiles = tiles
        self.name = name
        self.i = 0

    def get(self):
        j = self.i % len(self.tiles)
        self.i += 1
        return self.tiles[j], "%s%d" % (self.name, j)


class _Stop(Exception):
    pass


def build_program(stop_at=None, dumps=()):
    nc = bass.Bass("TRN2", target_bir_lowering=False)
    dump_aps = {}

    def ck(name, env=None):
        for dn, (cname, fn) in dict(dumps).items():
            if cname == name:
                ap, keys = fn(env)
                dt = ap.dtype
                d = nc.dram_tensor("dbg_" + dn, list(ap.shape), dt, kind="ExternalOutput").ap()
                dump_aps[dn] = d
                S.dma("sp", lambda e, d=d, ap=ap: e.dma_start(out=d, in_=ap), keys, [])
        if stop_at == name:
            raise _Stop()

    def din(name, shape):
        return nc.dram_tensor(name, list(shape), F32, kind="ExternalInput").ap()

    def dout(name, shape):
        return nc.dram_tensor(name, list(shape), F32, kind="ExternalOutput").ap()

    x_in = [din("xp", [T, D]), din("xs", [T, D])]
    cond = din("cond", [16, 128])
    gs0 = din("gs0", [2, 2, 4, 128, 256])
    mc0 = din("mc0", [2, 2, 4, 256, 256])
    mn0 = din("mn0", [2, 2, 4, 256])
    mm0 = din("mm0", [2, 2, 4])
    w_ada = din("w_ada", [2, D, 9 * D])
    b_ada = din("b_ada", [2, 72, 128])
    w_g = [din("ffn1_w_gate", [2, D, DFF]), din("ffn2_w_gate", [2, D, DFF])]
    w_u = [din("ffn1_w_up", [2, D, DFF]), din("ffn2_w_up", [2, D, DFF])]
    w_d = [din("ffn1_w_down", [2, DFF, D]), din("ffn2_w_down", [2, DFF, D])]
    w_in = din("w_in", [2, D, DIN])
    w_decay = din("w_decay", [2, 2, 16, 512])
    b_decay = din("b_decay", [2, 2, 512])
    w_conv = din("w_conv", [96, 128])
    b_conv = din("b_conv", [32, 128])
    f_bias = din("f_bias", [2, 2, 4])
    gla_ng = din("gla_norm_g", [2, 256])
    ml_ng = din("mlstm_norm_g", [2, 256])
    w_brg = din("w_br_gla", [2, D, D])
    w_brm = din("w_br_mlstm", [2, D, D])
    w_out = din("w_out", [2, D, D])
    ln_g = din("ln_g", [48, 128])
    ln_b = din("ln_b", [48, 128])

    y_out = [dout("yp", [T, D]), dout("ys", [T, D])]
    ogs = dout("ogs", [4, 2, 2, 4, 128, 256])
    omc = dout("omc", [4, 2, 2, 4, 256, 256])
    omn = dout("omn", [4, 2, 2, 4, 256])
    omm = dout("omm", [4, 2, 2, 4])

    S = Sched(nc)
    st = contextlib.ExitStack()

    def sb(name, shape, dt=F32):
        return st.enter_context(nc.sbuf_tensor(name, list(shape), dt))

    def psm(name, shape, dt=F32):
        return st.enter_context(nc.psum_tensor(name, list(shape), dt))

    def act(out, in_, func, r, w, **kw):
        S.op("act", lambda e: e.activation(out=out, in_=in_, func=func, **kw), r, w)

    def acp(out, in_, r, w):
        S.op("act", lambda e: e.copy(out=out, in_=in_), r, w)

    def vcp(out, in_, r, w):
        S.op("dve", lambda e: e.tensor_copy(out=out, in_=in_), r, w)

    def tt(out, in0, in1, op, r, w):
        S.op("dve", lambda e: e.tensor_tensor(out=out, in0=in0, in1=in1, op=op), r, w)

    def ts(out, in0, s1, s2, op0, op1, r, w):
        if s2 is None:
            S.op("dve", lambda e: e.tensor_scalar(out=out, in0=in0, scalar1=s1, scalar2=None, op0=op0), r, w)
        else:
            S.op("dve", lambda e: e.tensor_scalar(out=out, in0=in0, scalar1=s1, scalar2=s2, op0=op0, op1=op1), r, w)

    def stt(out, in0, scalar, in1, op0, op1, r, w):
        S.op("dve", lambda e: e.scalar_tensor_tensor(out=out, in0=in0, scalar=scalar, in1=in1, op0=op0, op1=op1), r, w)

    def recip(out, in_, r, w):
        S.op("dve", lambda e: e.reciprocal(out=out, in_=in_), r, w)

    def scan(out, d0, d1, init, op0, op1, r, w):
        S.op("dve", lambda e: e.tensor_tensor_scan(out=out, data0=d0, data1=d1, initial=init, op0=op0, op1=op1), r, w)

    def mm(out, lhsT, rhs, start, stop, r, w, inc):
        S.op("pe", lambda e: e.matmul(out, lhsT=lhsT, rhs=rhs, start=start, stop=stop), r, w, inc=inc)

    def tr(out, in_, ident, r, w, inc):
        S.op("pe", lambda e: e.transpose(out=out, in_=in_, identity=ident), r, w, inc=inc)

    def pmemset(ap, val, w):
        S.op("pool", lambda e: e.memset(ap, val), (), w)

    def vmemset(ap, val, w):
        S.op("dve", lambda e: e.memset(ap, val), (), w)

    def dma_sp(out, in_, r, w):
        S.dma("sp", lambda e: e.dma_start(out=out, in_=in_), r, w)

    def dma_cast(out, in_, r, w):
        S.dma("pool", lambda e: e.dma_start(out=out, in_=in_), r, w)

    banks = [psm("bank%d" % i, [128, 512]) for i in range(7)]
    bank_rot = Rot(banks, "B")

    def bank():
        return bank_rot.get()

    identf = sb("identf", [128, 128])
    ones = sb("ones", [128, 128])
    onesm = sb("onesm", [128, 128])
    neg16 = sb("neg16", [128, 128])
    tric = [sb("tric0", [128, 128]), sb("tric1", [128, 128])]
    trir = [sb("trir0", [128, 128]), sb("trir1", [128, 128])]
    mask = [sb("mask0", [128, 128]), sb("mask1", [128, 128])]
    sel = sb("sel", [64, 4, 128])
    rows_stage = sb("rows_stage", [128, 128])

    pmemset(ones[:], 1.0, ["ones"])
    pmemset(onesm[:], 1.0 / 1024.0, ["onesm"])
    pmemset(neg16[:], -1.0 / 16.0, ["neg16"])

    def asel(out, in_, pattern, op, cm, r, w):
        S.op("pool", lambda e: e.affine_select(out=out, in_=in_, pattern=pattern, compare_op=op, fill=0.0,
                                               base=0, channel_multiplier=cm), r, w)

    asel(identf[:], ones[:], [[-1, 128]], ALU.is_equal, 1, ["ones"], ["identf"])
    asel(tric[0][:], neg16[:], [[1, 128]], ALU.is_ge, -1, ["neg16"], ["tric0"])
    asel(tric[1][:], neg16[:], [[-1, 128]], ALU.is_ge, 1, ["neg16"], ["tric1"])
    asel(trir[0][:], neg16[:], [[-1, 128]], ALU.is_gt, 1, ["neg16"], ["trir0"])
    asel(trir[1][:], neg16[:], [[1, 128]], ALU.is_gt, -1, ["neg16"], ["trir1"])
    asel(mask[0][:], ones[:], [[1, 128]], ALU.is_ge, -1, ["ones"], ["mask0"])
    asel(mask[1][:], ones[:], [[-1, 128]], ALU.is_ge, 1, ["ones"], ["mask1"])

    xT = sb("xT", [128, 8, T])
    hT = sb("hT", [128, 8, T], BF16)
    slab_bufs = [sb("slab%d" % i, [128, 2816], BF16) for i in range(3)]
    slab_rot = Rot(slab_bufs, "slab")
    t5_rot = Rot([sb("t5_%d" % i, [128, 512]) for i in range(3)], "t5")
    TAB = sb("TAB", [128, 2048])
    TA = TAB[:, 0:1024]
    TB = TAB[:, 1024:2048]
    ones4 = TAB[0:64, 1024:1536].rearrange("p (a b) -> p a b", a=4)
    pmemset(ones4, 1.0, ["TB"])
    for p0 in (0, 32):
        asel(sel[p0:p0 + 32], ones4[p0:p0 + 32], [[-1, 4], [0, 128]], ALU.is_equal, 1, ["TB"], ["sel%d" % p0])
    UNI = sb("UNI", [128, 32768], BF16)

    ADA = sb("ADA", [128, 2, 72, 2])
    ONEP = sb("ONEP", [128, 2, 72, 2])
    GH = sb("GH", [128, 2, 72, 2])
    BADA = sb("BADA", [128, 72])
    CONDT = sb("CONDT", [128, 16])
    SCB = sb("SCB", [128, 8, 2], BF16)
    LNG = sb("LNG", [128, 48])
    LNB = sb("LNB", [128, 48])
    WC = sb("WC", [128, 96])
    BC = sb("BC", [128, 32])
    S4 = sb("S4", [128, 4, 64])

    def uni_bf(off, n):
        return UNI[:, off:off + n]

    def uni_f32(off, n):
        return UNI[:, off:off + 2 * n].bitcast(F32)

    def xk(dc, th):
        return "xT%d_%d" % (dc, th)

    def hk(kc, th):
        return "hT%d_%d" % (kc, th)

    def T5():
        return t5_rot.get()

    def load_cols(dst, dkey, src, R):
        dma_sp(rows_stage[0:R, :], src, [], ["rows_stage"])
        b, bk = bank()
        tr(b[:, 0:R], rows_stage[0:R, :], identf[0:R, 0:R], ["rows_stage", "identf"], [bk], True)
        vcp(dst, b[:, 0:R], [bk], [dkey])

    load_cols(LNG[:], "LNG", ln_g, 48)
    load_cols(LNB[:], "LNB", ln_b, 48)
    load_cols(WC[:], "WC", w_conv, 96)
    load_cols(BC[:], "BC", b_conv, 32)
    load_cols(CONDT[:], "CONDT", cond, 16)
    act(SCB[:], CONDT[:].rearrange("p (a k) -> p k a", a=2), AF.Silu, ["CONDT"], ["SCB"])

    def slab(parts, KC, C):
        buf, key = slab_rot.get()
        v = buf[:, 0:KC * C].rearrange("p (k c) -> p k c", k=KC)
        for src, off in parts:
            c = src.shape[1]
            dma_cast(v[:, :, off:off + c], src.rearrange("(k p) c -> p k c", p=128), [], [key])
        return v, key

    pidx_i = sb("pidx_i", [128, 1], I32)
    pidx = sb("pidx", [128, 1])
    OM = sb("OM", [128, 2])
    nidx_i = TAB[:, 0:64].bitcast(I32)
    nidx = TAB[:, 64:128]
    U4 = TAB[:, 128:384].rearrange("p (a b) -> p a b", a=4)
    K4i = TAB[:, 384:640].bitcast(I32).rearrange("p (a b) -> p a b", a=4)
    K4 = TAB[:, 640:896].rearrange("p (a b) -> p a b", a=4)
    S.op("pool", lambda e: e.iota(pidx_i[:], pattern=[[0, 1]], base=0, channel_multiplier=1), (), ["pidx_i"])
    S.op("pool", lambda e: e.iota(nidx_i, pattern=[[1, 64]], base=0, channel_multiplier=0), (), ["nidx_i"])
    vcp(pidx[:], pidx_i[:], ["pidx_i"], ["pidx"])
    vcp(nidx, nidx_i, ["nidx_i"], ["nidx"])
    lk = math.log(10000.0) / 256.0
    for jc in range(2):
        act(OM[:, jc:jc + 1], pidx[:], AF.Exp, ["pidx"], ["OM"], scale=-lk, bias=-lk * 128.0 * jc)
    ts(OM[:], OM[:], 1.0 / (2.0 * math.pi), None, ALU.mult, None, ["OM"], ["OM"])
    for v in range(4):
        jc = v % 2
        ts(U4[:, v, :], nidx, OM[:, jc:jc + 1], None, ALU.mult, None, ["nidx", "OM"], ["U4"])
        if v >= 2:
            ts(U4[:, v, :], U4[:, v, :], 0.25, None, ALU.add, None, ["U4"], ["U4"])
    vcp(K4i, U4, ["U4"], ["K4i"])
    vcp(K4, K4i, ["K4i"], ["K4"])
    tt(U4, U4, K4, ALU.subtract, ["U4", "K4"], ["U4"])
    ts(K4, U4, 0.5, None, ALU.is_gt, None, ["U4"], ["K4"])
    tt(U4, U4, K4, ALU.subtract, ["U4", "K4"], ["U4"])
    ts(K4, U4, -0.5, None, ALU.is_lt, None, ["U4"], ["K4"])
    tt(U4, U4, K4, ALU.add, ["U4", "K4"], ["U4"])
    act(S4[:], U4, AF.Sin, ["U4"], ["S4"], scale=6.283185)

    BADA2 = [BADA, sb("BADA1", [128, 72])]
    for l in range(2):
        load_cols(BADA2[l][:], "BADA%d" % l, b_ada[l], 72)
    ada_pending = [(l, j) for l in range(2) for j in range(9)]

    def ada_chunk():
        if not ada_pending:
            return
        l, j = ada_pending.pop(0)
        ab, abk = bank()
        for sl in range(4):
            c0 = j * 1024 + sl * 256
            sv, sk = slab([(w_ada[l][:, c0:c0 + 256], 0)], 8, 256)
            for cg in range(2):
                col = sl * 2 + cg
                for kc in range(8):
                    mm(ab[:, col * 2:col * 2 + 2], sv[:, kc, cg * 128:(cg + 1) * 128], SCB[:, kc, :],
                       kc == 0, kc == 7, [sk, "SCB"], [abk], inc=(kc == 7 and cg == 1))
        js = slice(j * 8, (j + 1) * 8)
        tt(ADA[:, l, js, :], ab[:, 0:16].rearrange("p (c a) -> p c a", a=2),
           BADA2[l][:, js].unsqueeze(2).to_broadcast([128, 8, 2]), ALU.add, [abk, "BADA%d" % l], ["ADA%d_%d" % (l, j)])
        ts(ONEP[:, l, js, :], ADA[:, l, js, :], 1.0, None, ALU.add, None, ["ADA%d_%d" % (l, j)], ["ONEP%d_%d" % (l, j)])
        ts(GH[:, l, js, :], ADA[:, l, js, :], 0.5, None, ALU.mult, None, ["ADA%d_%d" % (l, j)], ["GH%d_%d" % (l, j)])

    for _ in range(3):
        ada_chunk()

    def adac(arr, l, j, dc, p):
        return arr[:, l, j * 8 + dc, p:p + 1]

    actT = uni_bf(0, 22 * T).rearrange("p (j t) -> p j t", j=22)
    ogT = uni_bf(0, 8 * T).rearrange("p (c t) -> p c t", c=8)
    hmT = uni_bf(8 * T, 8 * T).rearrange("p (c t) -> p c t", c=8)
    HB = UNI[:, 16384:32768]

    def hb_bf(off, n):
        return HB[:, off:off + n]

    def hb_f32(off, n):
        return HB[:, off:off + 2 * n].bitcast(F32)

    g_SP = [sb("GSP0", [128, 1024]), sb("GSP1", [128, 1024])]
    g_QD = [hb_bf(4096, 1024), hb_bf(5120, 1024)]
    g_KD = [hb_bf(6144, 1024), hb_bf(7168, 1024)]
    g_KW = [hb_bf(8192, 1024), hb_bf(9216, 1024)]
    g_V = hb_bf(10240, 2048).rearrange("p (t v) -> p t v", t=8)
    g_RG = hb_bf(12288, 2048).rearrange("p (t v) -> p t v", t=8)
    g_SST2 = [[sb("GSST%d_%d" % (q, d), [128, 256]) for d in range(2)] for q in range(2)]
    g_SBF = [hb_bf(15360, 256), hb_bf(15616, 256)]
    g_ATT = Rot([hb_bf(15872 + i * 128, 128) for i in range(4)], "gatt")
    m_QK = [hb_bf(0, 2048).rearrange("p (c t) -> p c t", c=2), hb_bf(2048, 2048).rearrange("p (c t) -> p c t", c=2)]
    m_KTOK = hb_bf(4096, 2048).rearrange("p (t c) -> p t c", t=8)
    m_OG = hb_bf(6144, 2048).rearrange("p (t v) -> p t v", t=8)
    m_QS = [hb_bf(8192, 2048).rearrange("p (c t) -> p c t", c=2), hb_bf(10240, 2048).rearrange("p (c t) -> p c t", c=2)]
    m_DTM = [hb_bf(12288, 1024), hb_bf(13312, 1024)]
    m_PT = Rot([hb_bf(14336 + i * 128, 128) for i in range(4)], "mpt")
    m_KWT = Rot([hb_bf(14848 + i * 256, 256) for i in range(4)], "mkw")
    MT = hb_bf(0, 8 * T).rearrange("p (c t) -> p c t", c=8)

    VAUG = sb("VAUG", [128, 8, 257], BF16)
    CST2 = [[sb("CST%d_%d" % (q, d), [128, 2, 257]) for d in range(2)] for q in range(2)]
    CBF = [sb("CBF0", [128, 2, 257], BF16), sb("CBF1", [128, 2, 257], BF16)]
    OGTMP = Rot([sb("ogtmp%d" % i, [128, 256]) for i in range(2)], "ogtmp")
    SSQ = sb("SSQ", [128, 8])
    RS = sb("RS", [128, 8])
    DEC = [sb("DEC0", [128, 8]), sb("DEC1", [128, 8])]
    SM = Rot([sb("sm%d" % i, [128, 1]) for i in range(4)], "sm")
    SLA = sb("SLA", [128, 8, 32], BF16)
    SLF = sb("SLF", [128, 8, 64], BF16)
    SLI = sb("SLI", [128, 8, 64], BF16)
    GNB = sb("GNB", [128, 256])
    MNB = sb("MNB", [128, 256])
    R1 = sb("R1", [64, T])
    R2 = sb("R2", [64, T])
    R3 = sb("R3", [64, T])
    R4 = sb("R4", [64, T])
    AT = R1[0:33, :]
    WDEC = R2[0:33, :].rearrange("p (d c) -> p d c", d=2)
    TOT = sb("TOT", [64, 8])
    MM = sb("MM", [64, 16])
    FB = sb("FB", [64, 1])
    NFB = sb("NFB", [64, 1])
    UCOL = sb("UCOL", [128, 8, 36])
    ECOL = sb("ECOL", [128, 8, 36])

    OSUM = TAB[:].rearrange("p (t v) -> p t v", t=8)

    def osk(tile):
        return "TA" if tile < 4 else "TB"

    pmemset(SLF[:], 0.0, ["SLF"])
    pmemset(SLI[:], 0.0, ["SLI"])
    pmemset(FB[:], 0.0, ["FB"])
    pmemset(VAUG[:, :, 256:257], 1.0, ["VAUGo"])

    LNT = Rot([uni_f32(24576 + i * 1024, 512) for i in range(4)], "lnt")
    lnm1 = uni_f32(24576 + 4 * 1024, 512)
    lnr1 = uni_f32(24576 + 5 * 1024, 512)
    lnm2 = uni_f32(24576 + 6 * 1024, 512)
    lnr2 = uni_f32(24576 + 7 * 1024, 512)
    LNG2 = sb("LNG2", [128, 8])
    LNB2 = sb("LNB2", [128, 8])

    def layernorm(l, i, mod):
        c0 = l * 24 + i * 8
        if mod is not None:
            l2, jsc, jsh, p = mod
            tt(LNG2[:], LNG[:, c0:c0 + 8], ONEP[:, l2, jsc * 8:(jsc + 1) * 8, p], ALU.mult, ["LNG", "ONEP%d_%d" % (l2, jsc)], ["LNG2"])
            tt(LNB2[:], LNB[:, c0:c0 + 8], ONEP[:, l2, jsc * 8:(jsc + 1) * 8, p], ALU.mult, ["LNB", "ONEP%d_%d" % (l2, jsc)], ["LNB2"])
            tt(LNB2[:], LNB2[:], ADA[:, l2, jsh * 8:(jsh + 1) * 8, p], ALU.add, ["LNB2", "ADA%d_%d" % (l2, jsh)], ["LNB2"])
        bms, bqs = [], []
        for th in range(2):
            ts_ = slice(th * 512, (th + 1) * 512)
            bm, bmk = bank()
            bq, bqk = bank()
            bms.append((bm, bmk))
            bqs.append((bq, bqk))
            for dc in range(8):
                t, tk = LNT.get()
                act(t, xT[:, dc, ts_], AF.Square, [xk(dc, th)], [tk])
                mm(bm[:], onesm[:], xT[:, dc, ts_], dc == 0, dc == 7, ["onesm", xk(dc, th)], [bmk], inc=(dc == 7))
                mm(bq[:], onesm[:], t, dc == 0, dc == 7, ["onesm", tk], [bqk], inc=True)
        mean = [lnm1, lnm2]
        rstd = [lnr1, lnr2]
        mk = ["lnm1", "lnm2"]
        rk = ["lnr1", "lnr2"]
        for th in range(2):
            acp(mean[th], bms[th][0][:], [bms[th][1]], [mk[th]])
        for th in range(2):
            act(rstd[th], bms[th][0][:], AF.Square, [bms[th][1]], [rk[th]])
        for th in range(2):
            tt(rstd[th], bqs[th][0][:], rstd[th], ALU.subtract, [bqs[th][1], rk[th]], [rk[th]])
        for th in range(2):
            ts(rstd[th], rstd[th], 0.0, None, ALU.max, None, [rk[th]], [rk[th]])
        for th in range(2):
            act(rstd[th], rstd[th], AF.Sqrt, [rk[th]], [rk[th]], bias=LN_EPS)
        for th in range(2):
            recip(rstd[th], rstd[th], [rk[th]], [rk[th]])
        items = [(dc, th) for th in range(2) for dc in range(8)]
        tmp = {}

        def st1(k):
            dc, th = items[k]
            ts_ = slice(th * 512, (th + 1) * 512)
            t, tk = LNT.get()
            tmp[k] = (t, tk)
            tt(t, xT[:, dc, ts_], mean[th], ALU.subtract, [xk(dc, th), mk[th]], [tk])

        def st2(k):
            dc, th = items[k]
            t, tk = tmp[k]
            tt(t, t, rstd[th], ALU.mult, [tk, rk[th]], [tk])

        def st3(k):
            dc, th = items[k]
            ts_ = slice(th * 512, (th + 1) * 512)
            t, tk = tmp[k]
            c = c0 + dc
            act(xT[:, dc, ts_], t, AF.Identity, [tk, "LNG", "LNB"], [xk(dc, th)], scale=LNG[:, c:c + 1], bias=LNB[:, c:c + 1])
            if mod is not None:
                act(hT[:, dc, ts_], t, AF.Identity, [tk, "LNG2", "LNB2"], [hk(dc, th)], scale=LNG2[:, dc:dc + 1],
                    bias=LNB2[:, dc:dc + 1])

        n = len(items)
        for k in range(n + 2):
            if k < n:
                st1(k)
            if 1 <= k <= n:
                st2(k - 1)
            if 2 <= k <= n + 1:
                st3(k - 2)

    def resid_update(b, bk, dc, th, gate_ap, gkey):
        ts_ = slice(th * 512, (th + 1) * 512)
        t, tk = T5()
        act(t[:], b[:], AF.Identity, [bk, gkey], [tk], scale=gate_ap)
        stt(xT[:, dc, ts_], xT[:, dc, ts_], ALPHA, t[:], ALU.mult, ALU.add, [xk(dc, th), tk], [xk(dc, th)])

    def ffn(l, which, p):
        wg, wu, wd = w_g[which][l], w_u[which][l], w_d[which][l]
        jg = 2 if which == 0 else 8
        for js in range(11):
            if js < 6:
                ada_chunk()
            sg, sgk = slab([(wg[:, js * 256:(js + 1) * 256], 0)], 8, 256)
            su, suk = slab([(wu[:, js * 256:(js + 1) * 256], 0)], 8, 256)
            for jj in range(2):
                jc = js * 2 + jj
                for th in range(2):
                    ts_ = slice(th * 512, (th + 1) * 512)
                    bg, bgk = bank()
                    bu, buk = bank()
                    for kc in range(8):
                        mm(bg[:], sg[:, kc, jj * 128:(jj + 1) * 128], hT[:, kc, ts_], kc == 0, kc == 7,
                           [sgk, hk(kc, th)], [bgk], inc=(kc == 7))
                    for kc in range(8):
                        mm(bu[:], su[:, kc, jj * 128:(jj + 1) * 128], hT[:, kc, ts_], kc == 0, kc == 7,
                           [suk, hk(kc, th)], [buk], inc=(kc == 7))
                    t, tk = T5()
                    act(t[:], bg[:], AF.Silu, [bgk], [tk])
                    tt(actT[:, jc, ts_], bu[:], t[:], ALU.mult, [buk, tk], ["act%d_%d" % (jc, th)])
        for ds in range(8):
            sd, sdk = slab([(wd[:, ds * 128:(ds + 1) * 128], 0)], 22, 128)
            for th in range(2):
                ts_ = slice(th * 512, (th + 1) * 512)
                b, bk = bank()
                for jc in range(22):
                    mm(b[:], sd[:, jc, :], actT[:, jc, ts_], jc == 0, jc == 21,
                       [sdk, "act%d_%d" % (jc, th)], [bk], inc=(jc == 21))
                resid_update(b, bk, ds, th, adac(GH, l, jg, ds, p), "GH%d_%d" % (l, jg))

    def head_epilogue(gate, dstT, h, ngkey):
        for tile in range(8):
            jt, jtk = T5()
            act(jt[:, 0:256], OSUM[:, tile, :], AF.Square, [osk(tile)], [jtk, "SSQ"], accum_out=SSQ[:, tile:tile + 1])
        ck("epA")
        ts(RS[:], SSQ[:], 1.0 / 256.0, None, ALU.mult, None, ["SSQ"], ["RS"])
        act(RS[:], RS[:], AF.Sqrt, ["RS"], ["RS"], bias=NORM_EPS)
        recip(RS[:], RS[:], ["RS"], ["RS"])
        ck("epB")
        for tile in range(8):
            og, ogk = OGTMP.get()
            stt(og[:], OSUM[:, tile, :], RS[:, tile:tile + 1], gate[:, tile, :], ALU.mult, ALU.mult,
                [osk(tile), "RS", ngkey], [ogk])
            ck("epC")
            pt, ptk = bank()
            for vc in range(2):
                tr(pt[:, vc * 128:(vc + 1) * 128], og[:, vc * 128:(vc + 1) * 128], identf[:], [ogk, "identf"], [ptk],
                   inc=(vc == 1))
            acp(dstT[:, h * 2:(h + 1) * 2, tile * 128:(tile + 1) * 128],
                pt[:, 0:256].rearrange("p (a b) -> p a b", a=2), [ptk], ["mixT"])
            ck("epD%d" % tile)

    def tok_proj(l, col0, dst, dkey, post):
        sv, sk = slab([(w_in[l][:, col0:col0 + 256], 0)], 8, 256)
        for pair in range(4):
            b, bk = bank()
            for j in range(2):
                tile = pair * 2 + j
                th = tile // 4
                for kc in range(8):
                    mm(b[:, j * 256:(j + 1) * 256], hT[:, kc, tile * 128:(tile + 1) * 128], sv[:, kc, :],
                       kc == 0, kc == 7, [sk, hk(kc, th)], [bk], inc=(kc == 7 and j == 1))
            post(b[:].rearrange("p (a v) -> p a v", a=2), bk, dst[:, pair * 2:(pair + 1) * 2, 0:256], dkey)

    def feat_proj(l, col0, ncols, dst_fn):
        sv, sk = slab([(w_in[l][:, col0:col0 + ncols], 0)], 8, ncols)
        for cc in range(ncols // 128):
            for th in range(2):
                b, bk = bank()
                for kc in range(8):
                    mm(b[:], sv[:, kc, cc * 128:(cc + 1) * 128], hT[:, kc, th * 512:(th + 1) * 512], kc == 0, kc == 7,
                       [sk, hk(kc, th)], [bk], inc=(kc == 7))
                dst_fn(cc, th, b, bk)

    def gla(l, pcfg):
        is_prompt, seqs, p = pcfg
        pmemset(AT[32:33, :], 1.0, ["AT1"])
        pmemset(WDEC, 0.0, ["WDEC"])
        dma_cast(SLA[:], w_in[l][:, O_A:O_A + 32].rearrange("(k p) c -> p k c", p=128), [], ["SLA"])
        for th in range(2):
            b, bk = bank()
            for kc in range(8):
                mm(b[0:32, :], SLA[:, kc, :], hT[:, kc, th * 512:(th + 1) * 512], kc == 0, kc == 7,
                   ["SLA", hk(kc, th)], [bk], inc=(kc == 7))
            acp(AT[0:32, th * 512:(th + 1) * 512], b[0:32, :], [bk], ["AT%d" % th])
        dma_sp(WDEC[0:16, 0, :], w_decay[l, 0], [], ["WDEC"])
        dma_sp(WDEC[16:32, 1, :], w_decay[l, 1], [], ["WDEC"])
        for d in range(2):
            dma_sp(WDEC[32:33, d, :], b_decay[l, d:d + 1, :], [], ["WDEC"])
        dma_sp(GNB[:], gla_ng[l].partition_broadcast(128), [], ["GNB"])
        ck("glaA")

        for h in range(4):
            ada_chunk()
            for d in range(2):
                for half in range(2):
                    b, bk = bank()
                    for j in range(4):
                        tile = half * 4 + j
                        mm(b[:, j * 128:(j + 1) * 128], AT[0:33, tile * 128:(tile + 1) * 128],
                           WDEC[0:33, d, h * 128:(h + 1) * 128], True, True,
                           ["AT%d" % half, "AT1", "WDEC"], [bk], inc=(j == 3))
                    t, tk = T5()
                    act(t[:], b[:], AF.Exp, [bk], [tk], scale=-1.0)
                    act(g_SP[d][:, half * 512:(half + 1) * 512], t[:], AF.Ln, [tk], ["gSP%d" % d], bias=1.0)
            def put_q(cc, th, b, bk):
                acp(TA[:, th * 512:(th + 1) * 512], b[:], [bk], ["TA"])

            def put_k(cc, th, b, bk):
                acp(TB[:, th * 512:(th + 1) * 512], b[:], [bk], ["TB"])

            feat_proj(l, O_QG + h * 128, 128, put_q)
            feat_proj(l, O_KG + h * 128, 128, put_k)

            def post_v(b3, bk, out3, dkey):
                vcp(out3, b3, [bk], [dkey])

            def post_r(b3, bk, out3, dkey):
                t, tk = T5()
                act(t[:].rearrange("p (a v) -> p a v", a=2), b3, AF.Silu, [bk], [tk])
                tt(out3, t[:].rearrange("p (a v) -> p a v", a=2), GNB[:].unsqueeze(1).to_broadcast([128, 2, 256]),
                   ALU.mult, [tk, "GNB"], [dkey])

            tok_proj(l, O_VG + h * 256, g_V, "gV", post_v)
            tok_proj(l, O_RG + h * 256, g_RG, "gRG", post_r)
            ck("glaB")

            for d in range(2):
                lastc = 127 if d == 0 else 0
                for half in range(2):
                    hs = slice(half * 512, (half + 1) * 512)
                    b, bk = bank()
                    for j in range(4):
                        tile = half * 4 + j
                        mm(b[:, j * 128:(j + 1) * 128], g_SP[d][:, tile * 128:(tile + 1) * 128], tric[d][:], True, True,
                           ["gSP%d" % d, "tric%d" % d], [bk], inc=(j == 3))
                    t, tk = T5()
                    act(t[:], b[:], AF.Exp, [bk], [tk])
                    vcp(DEC[d][:, half * 4:(half + 1) * 4], t[:, lastc::128], [tk], ["DEC%d" % d])
                    stt(g_QD[d][:, hs], TA[:, hs], 128.0 ** -0.5, t[:], ALU.mult, ALU.mult, ["TA", tk], ["gQD%d" % d])
                    t2, t2k = T5()
                    act(t2[:], b[:], AF.Exp, [bk], [t2k], scale=-1.0)
                    tt(g_KD[d][:, hs], TB[:, hs], t2[:], ALU.mult, ["TB", t2k], ["gKD%d" % d])
            for half in range(2):
                hs = slice(half * 512, (half + 1) * 512)
                bkT, bkTk = bank()
                for j in range(4):
                    tile = half * 4 + j
                    tr(bkT[:, j * 128:(j + 1) * 128], TB[:, tile * 128:(tile + 1) * 128], identf[:], ["TB", "identf"],
                       [bkTk], inc=(j == 3))
                for d in range(2):
                    b, bk = bank()
                    for j in range(4):
                        tile = half * 4 + j
                        mm(b[:, j * 128:(j + 1) * 128], trir[d][:], g_SP[d][:, tile * 128:(tile + 1) * 128], True, True,
                           ["gSP%d" % d, "trir%d" % d], [bk], inc=(j == 3))
                    t, tk = T5()
                    act(t[:], b[:], AF.Exp, [bk], [tk])
                    tt(g_KW[d][:, hs], bkT[:], t[:], ALU.mult, [bkTk, tk], ["gKW%d" % d])

            ck("glaC")
            written = set()
            for si, (t0, n) in enumerate(seqs):
                have = [False, False]
                g_SST = g_SST2[si % 2]
                sq_ = "q%d" % (si % 2)
                if not is_prompt:
                    for d in range(2):
                        dma_sp(g_SST[d][:], gs0[l, d, h], [], ["gSST%d" % d + sq_])
                        acp(g_SBF[d][:], g_SST[d][:], ["gSST%d" % d + sq_], ["gSBF%d" % d])
                        have[d] = True
                tl = lambda i, d: (t0 + i) if d == 0 else (t0 + n - 1 - i)
                need_upd = lambda i: (i < n - 1) or is_prompt
                stA, stB, stK, stO = {}, {}, {}, {}

                def A(i):
                    b, bk = bank()
                    for d in range(2):
                        tsl = slice(tl(i, d) * 128, (tl(i, d) + 1) * 128)
                        mm(b[:, d * 128:(d + 1) * 128], g_KD[d][:, tsl], g_QD[d][:, tsl], True, True,
                           ["gKD%d" % d, "gQD%d" % d], [bk], inc=(d == 1))
                    stA[i] = (b, bk)

                def B(i):
                    b, bk = stA[i]
                    for d in range(2):
                        am, amk = g_ATT.get()
                        tt(am[:], b[:, d * 128:(d + 1) * 128], mask[d][:], ALU.mult, [bk, "mask%d" % d], [amk])
                        stB[(i, d)] = (am, amk)

                def K(i):
                    if not need_upd(i):
                        return
                    b, bk = bank()
                    for d in range(2):
                        tile = tl(i, d)
                        tsl = slice(tile * 128, (tile + 1) * 128)
                        mm(b[:, d * 256:(d + 1) * 256], g_KW[d][:, tsl], g_V[:, tile, :], True, True,
                           ["gKW%d" % d, "gV"], [bk], inc=(d == 1))
                    stK[i] = (b, bk)

                def O(i):
                    b, bk = bank()
                    for d in range(2):
                        tile = tl(i, d)
                        tsl = slice(tile * 128, (tile + 1) * 128)
                        first = (i == 0 and not have[d])
                        am, amk = stB[(i, d)]
                        mm(b[:, d * 256:(d + 1) * 256], am[:], g_V[:, tile, :], True, first, [amk, "gV"], [bk],
                           inc=(first and d == 1))
                        if not first:
                            mm(b[:, d * 256:(d + 1) * 256], g_QD[d][:, tsl], g_SBF[d][:], False, True,
                               ["gQD%d" % d, "gSBF%d" % d], [bk], inc=(d == 1))
                    stO[i] = (b, bk)

                def E(i):
                    if not need_upd(i):
                        return
                    b, bk = stK[i]
                    for d in range(2):
                        tile = tl(i, d)
                        first = (i == 0 and not have[d])
                        if first:
                            acp(g_SST[d][:], b[:, d * 256:(d + 1) * 256], [bk], ["gSST%d" % d + sq_])
                        else:
                            stt(g_SST[d][:], g_SST[d][:], DEC[d][:, tile:tile + 1], b[:, d * 256:(d + 1) * 256],
                                ALU.mult, ALU.add, [bk, "gSST%d" % d + sq_, "DEC%d" % d], ["gSST%d" % d + sq_])
                    if i < n - 1:
                        for d in range(2):
                            acp(g_SBF[d][:], g_SST[d][:], ["gSST%d" % d + sq_], ["gSBF%d" % d])
                    if i == n - 1 and is_prompt:
                        for d in range(2):
                            dma_sp(ogs[si, l, d, h], g_SST[d][:], ["gSST%d" % d + sq_], [])

                def F(i):
                    b, bk = stO[i]
                    for d in range(2):
                        tile = tl(i, d)
                        if tile not in written:
                            written.add(tile)
                            acp(OSUM[:, tile, :], b[:, d * 256:(d + 1) * 256], [bk], [osk(tile)])
                        else:
                            tt(OSUM[:, tile, :], OSUM[:, tile, :], b[:, d * 256:(d + 1) * 256], ALU.add,
                               [bk, osk(tile)], [osk(tile)])

                A(0)
                B(0)
                K(0)
                for i in range(n):
                    if i + 1 < n:
                        A(i + 1)
                    O(i)
                    E(i)
                    if i + 1 < n:
                        B(i + 1)
                        K(i + 1)
                    F(i)
            ck("glaD")
            head_epilogue(g_RG, ogT, h, "gRG")
            ck("glaE")

    R4ALL = ["R4"] + ["R4t%d" % t for t in range(8)]
    R1ALL = ["R1"] + ["R1t%d" % t for t in range(8)]

    def mlstm(l, pcfg):
        is_prompt, seqs, p = pcfg
        L = seqs[0][1] * 128
        NS = len(seqs)
        dma_cast(SLA[:, :, 0:16], w_in[l][:, O_IF:O_IF + 16].rearrange("(k p) c -> p k c", p=128), [], ["SLA"])
        for (dst, dk_, c0, o) in ((SLF, "SLF", 0, 4), (SLF, "SLF", 32, 12), (SLI, "SLI", 0, 0), (SLI, "SLI", 32, 8)):
            vcp(dst[:, :, c0:c0 + 4], SLA[:, :, o:o + 4], ["SLA"], [dk_])
        for d in range(2):
            dma_sp(FB[d * 32:d * 32 + 4, 0:1], f_bias[l, d].rearrange("(p o) -> p o", o=1), [], ["FB"])
        ts(NFB[:], FB[:], -1.0, None, ALU.mult, None, ["FB"], ["NFB"])
        dma_sp(MNB[:], ml_ng[l].partition_broadcast(128), [], ["MNB"])
        for th in range(2):
            hs = slice(th * 512, (th + 1) * 512)
            bF, bFk = bank()
            for kc in range(8):
                mm(bF[0:64, :], SLF[:, kc, :], hT[:, kc, hs], kc == 0, kc == 7, ["SLF", hk(kc, th)], [bFk], inc=(kc == 7))
            act(R1[:, hs], bF[0:64, :], AF.Exp, [bFk, "NFB"], ["R1"], scale=-1.0, bias=NFB[:, 0:1])
            act(R1[:, hs], R1[:, hs], AF.Ln, ["R1"], ["R1"], bias=1.0)
            bI, bIk = bank()
            for kc in range(8):
                mm(bI[0:64, :], SLI[:, kc, :], hT[:, kc, hs], kc == 0, kc == 7, ["SLI", hk(kc, th)], [bIk], inc=(kc == 7))
            vcp(R3[:, hs], bI[0:64, :], [bIk], ["R3"])
        for tile in range(8):
            tsl = slice(tile * 128, (tile + 1) * 128)
            scan(R2[:, tsl], ones[0:64, :], R1[:, tsl], 0.0, ALU.mult, ALU.add, ["R1", "ones"], ["R2"])
        vcp(TOT[32:64, :], R2[32:64, 127::128], ["R2"], ["TOT"])
        tt(R4[32:64, :], R1[32:64, :], R2[32:64, :], ALU.subtract, ["R1", "R2"], ["R4"])
        tt(R2[32:64, :].rearrange("p (t s) -> p t s", t=8), R4[32:64, :].rearrange("p (t s) -> p t s", t=8),
           TOT[32:64, :].unsqueeze(2).to_broadcast([32, 8, 128]), ALU.add, ["R4", "TOT"], ["R2"])
        tt(R3[:], R3[:], R2[:], ALU.add, ["R3", "R2"], ["R3"])
        for tile in range(8):
            tsl = slice(tile * 128, (tile + 1) * 128)
            scan(R4[0:32, tsl], ones[0:32, :], R3[0:32, tsl], -1e30, ALU.mult, ALU.max, ["R3", "ones"], ["R4"])
            rsl = slice(tile * 128 + 127, tile * 128 - 1 if tile > 0 else None, -1)
            scan(R4[32:64, rsl], ones[32:64, :], R3[32:64, rsl], -1e30, ALU.mult, ALU.max, ["R3", "ones"], ["R4"])
        vmemset(MM[:], 0.0, ["MM"])
        col = 0
        cols = []
        for si, (t0, n) in enumerate(seqs):
            if not is_prompt:
                for d in range(2):
                    dma_sp(MM[d * 32:d * 32 + 4, col:col + 1], mm0[l, d].rearrange("(p o) -> p o", o=1), [], ["MM"])
            cols.append(col)
            col += n + 1
        nmax = max(n for (_, n) in seqs)
        for i in range(nmax):
            for si, (t0, n) in enumerate(seqs):
                if i >= n:
                    continue
                for d in range(2):
                    rs = slice(d * 32, d * 32 + 32)
                    c = cols[si] + i
                    tile = t0 + i if d == 0 else t0 + n - 1 - i
                    tsl = slice(tile * 128, (tile + 1) * 128)
                    lc = tile * 128 + (127 if d == 0 else 0)
                    mkey = "MM%d_%d" % (si, d)
                    ts(R4[rs, tsl], R4[rs, tsl], MM[rs, c:c + 1], None, ALU.max, None, ["R4", "R4t%d" % tile, "MM", mkey], ["R4t%d" % tile])
                    act(R1[rs, tsl], R4[rs, tsl], AF.Exp, ["R4t%d" % tile, "MM", mkey], ["R1t%d" % tile], scale=-1.0, bias=MM[rs, c:c + 1])
                    tt(MM[rs, c + 1:c + 2], R4[rs, lc:lc + 1], R2[rs, lc:lc + 1], ALU.subtract, ["R4t%d" % tile, "R2"], [mkey])
        for si, (t0, n) in enumerate(seqs):
            if is_prompt:
                for d in range(2):
                    c = cols[si] + n
                    dma_sp(omm[si, l, d].rearrange("(p o) -> p o", o=1), MM[d * 32:d * 32 + 4, c:c + 1],
                           ["MM", "MM%d_%d" % (si, d)], [])
        ck("mlA")
        tt(R2[:], R4[:], R2[:], ALU.subtract, R4ALL + ["R2"], ["R2"])
        act(R2[:], R2[:], AF.Exp, ["R2"], ["R2"], scale=-1.0)
        for (src, skey, dst, dkey) in ((R3, "R3", UCOL, "UCOL"), (R2, "R2", ECOL, "ECOL")):
            for half in range(2):
                b, bk = bank()
                for j in range(4):
                    tile = half * 4 + j
                    tr(b[:, j * 64:(j + 1) * 64], src[0:64, tile * 128:(tile + 1) * 128], identf[0:64, 0:64],
                       [skey, "identf"], [bk], inc=(j == 3))
                vcp(dst[:, half * 4:(half + 1) * 4, :], b[:, 0:256].rearrange("p (a c) -> p a c", a=4)[:, :, 0:36],
                    [bk], [dkey])

        ck("mlB")
        RAWv = TA.rearrange("p (s l) -> p s l", s=NS)
        ACCv = TB.rearrange("p (s l) -> p s l", s=NS)
        for h in range(4):
            ada_chunk()
            scr = [(TA, TB, "TA", "TB"),
                   (m_QS[0][:].rearrange("p c t -> p (c t)").bitcast(F32), m_QS[1][:].rearrange("p c t -> p (c t)").bitcast(F32),
                    "mQS0", "mQS1")]
            units = [(which, cc) for which in range(2) for cc in range(2)]
            slabs_qk = {}

            def s1(u):
                which, cc = units[u]
                RAW, ACC, rk_, ak_ = scr[u % 2]
                if which not in slabs_qk:
                    slabs_qk[which] = slab([(w_in[l][:, O_QM + which * 1024 + h * 256:O_QM + which * 1024 + (h + 1) * 256], 0)],
                                           8, 256)
                sv, sk = slabs_qk[which]
                for th in range(2):
                    b, bk = bank()
                    for kc in range(8):
                        mm(b[:], sv[:, kc, cc * 128:(cc + 1) * 128], hT[:, kc, th * 512:(th + 1) * 512], kc == 0,
                           kc == 7, [sk, hk(kc, th)], [bk], inc=(kc == 7))
                    acp(RAW[:, th * 512:(th + 1) * 512], b[:], [bk], [rk_])

            def s2(u):
                which, cc = units[u]
                RAW, ACC, rk_, ak_ = scr[u % 2]
                RAWv = RAW.rearrange("p (s l) -> p s l", s=NS)
                ACCv = ACC.rearrange("p (s l) -> p s l", s=NS)
                ch = l * 48 + which * 8 + h * 2 + cc
                bch = l * 16 + which * 8 + h * 2 + cc
                ts(ACC, RAW, WC[:, ch + 16:ch + 17], BC[:, bch:bch + 1], ALU.mult, ALU.add, [rk_, "WC", "BC"], [ak_])
                stt(ACCv[:, :, 1:L], RAWv[:, :, 0:L - 1], WC[:, ch:ch + 1], ACCv[:, :, 1:L], ALU.mult, ALU.add,
                    [rk_, ak_, "WC"], [ak_])
                stt(ACCv[:, :, 0:L - 1], RAWv[:, :, 1:L], WC[:, ch + 32:ch + 33], ACCv[:, :, 0:L - 1], ALU.mult, ALU.add,
                    [rk_, ak_, "WC"], [ak_])

            def s3(u):
                which, cc = units[u]
                RAW, ACC, rk_, ak_ = scr[u % 2]
                act(m_QK[which][:, cc, :], ACC, AF.Silu, [ak_], ["mQK%d" % which])
                if which == 1:
                    act(RAW, ACC, AF.Silu, [ak_], [rk_])
                    for half in range(2):
                        b, bk = bank()
                        for j in range(4):
                            tile = half * 4 + j
                            tr(b[:, j * 128:(j + 1) * 128], RAW[:, tile * 128:(tile + 1) * 128], identf[:],
                               [rk_, "identf"], [bk], inc=(j == 3))
                        vcp(m_KTOK[:, half * 4:(half + 1) * 4, cc * 128:(cc + 1) * 128],
                            b[:].rearrange("p (a c) -> p a c", a=4), [bk], ["mKTOK"])

            def post_v(b3, bk, out3, dkey):
                vcp(out3, b3, [bk], [dkey])

            def post_o(b3, bk, out3, dkey):
                t, tk = T5()
                act(t[:].rearrange("p (a v) -> p a v", a=2), b3, AF.Sigmoid, [bk], [tk])
                tt(out3, t[:].rearrange("p (a v) -> p a v", a=2), MNB[:].unsqueeze(1).to_broadcast([128, 2, 256]),
                   ALU.mult, [tk, "MNB"], [dkey])

            s1(0)
            s1(1)
            s2(0)
            s2(1)
            tok_proj(l, O_VM + h * 256, VAUG, "VAUG", post_v)
            s3(0)
            s1(2)
            s3(1)
            s1(3)
            s2(2)
            s2(3)
            tok_proj(l, O_OM + h * 256, m_OG, "mOG", post_o)
            s3(2)
            s3(3)

            ck("mlC")
            for d in range(2):
                r = d * 32 + h
                rs4 = slice(d * 32, d * 32 + 4)
                lastc = 127 if d == 0 else 0
                for th in range(2):
                    hs = slice(th * 512, (th + 1) * 512)
                    bg, bgk = bank()
                    mm(bg[:], sel[rs4, h, :], R4[rs4, hs], True, True, ["sel%d" % (d * 32)] + R4ALL, [bgk], inc=True)
                    t, tk = T5()
                    for j in range(4):
                        tile = th * 4 + j
                        act(t[:, j * 128:(j + 1) * 128], bg[:, j * 128:(j + 1) * 128], AF.Exp, [bgk, "UCOL"], [tk],
                            scale=-1.0, bias=UCOL[:, tile, r:r + 1])
                    tt(m_DTM[d][:, hs].rearrange("p (a s) -> p a s", a=4), t[:].rearrange("p (a s) -> p a s", a=4),
                       mask[d][:].unsqueeze(1).to_broadcast([128, 4, 128]), ALU.mult, [tk, "mask%d" % d], ["mDTM%d" % d])
                    bi, bik = bank()
                    mm(bi[:], sel[rs4, h, :], R1[rs4, hs], True, True, ["sel%d" % (d * 32)] + R1ALL, [bik], inc=True)
                    vcp(DEC[d][:, th * 4:(th + 1) * 4], bi[:, lastc::128], [bik], ["DEC%d" % d])
                    for cc in range(2):
                        stt(m_QS[d][:, cc, hs], m_QK[0][:, cc, hs], 1.0 / 16.0, bi[:], ALU.mult, ALU.mult,
                            ["mQK0", bik], ["mQS%d" % d])

            ck("mlD")
            written = set()
            for si, (t0, n) in enumerate(seqs):
                have = [False, False]
                CST = CST2[si % 2]
                sq_ = "q%d" % (si % 2)
                if not is_prompt:
                    for d in range(2):
                        dma_sp(CST[d][:, :, 0:256], mc0[l, d, h].rearrange("(c p) v -> p c v", p=128), [], ["CST%d" % d + sq_])
                        for cc in range(2):
                            dma_sp(CST[d][:, cc, 256:257],
                                   mn0[l, d, h, cc * 128:(cc + 1) * 128].rearrange("(p o) -> p o", o=1), [], ["CST%d" % d + sq_])
                        acp(CBF[d][:], CST[d][:], ["CST%d" % d + sq_], ["CBF%d" % d])
                        have[d] = True
                tl = lambda i, d: (t0 + i) if d == 0 else (t0 + n - 1 - i)
                need_upd = lambda i: (i < n - 1) or is_prompt
                stA, stB, stC = {}, {}, {}

                def A(i):
                    b, bk = bank()
                    for d in range(2):
                        tsl = slice(tl(i, d) * 128, (tl(i, d) + 1) * 128)
                        for cc in range(2):
                            mm(b[:, d * 128:(d + 1) * 128], m_QK[1][:, cc, tsl], m_QK[0][:, cc, tsl], cc == 0, cc == 1,
                               ["mQK0", "mQK1"], [bk], inc=(cc == 1 and d == 1))
                    stA[i] = (b, bk)

                def B(i):
                    b, bk = stA[i]
                    for d in range(2):
                        tile = tl(i, d)
                        tsl = slice(tile * 128, (tile + 1) * 128)
                        pt, ptk = m_PT.get()
                        stt(pt[:], b[:, d * 128:(d + 1) * 128], 1.0 / 16.0, m_DTM[d][:, tsl], ALU.mult, ALU.mult,
                            [bk, "mDTM%d" % d], [ptk])
                        stB[(i, d)] = (pt, ptk)
                    if need_upd(i):
                        for d in range(2):
                            tile = tl(i, d)
                            lc = tile * 128 + (127 if d == 0 else 0)
                            kw, kwk = m_KWT.get()
                            ts(kw[:], m_KTOK[:, tile, :], m_DTM[d][:, lc:lc + 1], None, ALU.mult, None,
                               ["mKTOK", "mDTM%d" % d], [kwk])
                            stB[(i, d, "kw")] = (kw, kwk)

                def C(i):
                    if not need_upd(i):
                        return
                    for d in range(2):
                        tile = tl(i, d)
                        kw, kwk = stB[(i, d, "kw")]
                        for cc in range(2):
                            bc, bck = bank()
                            mm(bc[:, 0:257], kw[:, cc * 128:(cc + 1) * 128], VAUG[:, tile, :], True, True,
                               [kwk, "VAUG", "VAUGo"], [bck], inc=True)
                            stC[(i, d, cc)] = (bc, bck)

                def Dn(i):
                    for d in range(2):
                        tile = tl(i, d)
                        tsl = slice(tile * 128, (tile + 1) * 128)
                        first = (i == 0 and not have[d])
                        pt, ptk = stB[(i, d)]
                        bn, bnk = bank()
                        mm(bn[:, 0:257], pt[:], VAUG[:, tile, :], True, first, [ptk, "VAUG", "VAUGo"], [bnk], inc=first)
                        if not first:
                            for cc in range(2):
                                mm(bn[:, 0:257], m_QS[d][:, cc, tsl], CBF[d][:, cc, :], False, cc == 1,
                                   ["mQS%d" % d, "CBF%d" % d], [bnk], inc=(cc == 1))
                        stB[(i, d, "bn")] = (bn, bnk)

                def E(i):
                    if not need_upd(i):
                        return
                    for d in range(2):
                        tile = tl(i, d)
                        first = (i == 0 and not have[d])
                        for cc in range(2):
                            bc, bck = stC[(i, d, cc)]
                            if first:
                                acp(CST[d][:, cc, :], bc[:, 0:257], [bck], ["CST%d" % d + sq_])
                            else:
                                stt(CST[d][:, cc, :], CST[d][:, cc, :], DEC[d][:, tile:tile + 1], bc[:, 0:257],
                                    ALU.mult, ALU.add, [bck, "CST%d" % d + sq_, "DEC%d" % d], ["CST%d" % d + sq_])
                    if i < n - 1:
                        for d in range(2):
                            acp(CBF[d][:], CST[d][:], ["CST%d" % d + sq_], ["CBF%d" % d])
                    if i == n - 1 and is_prompt:
                        for d in range(2):
                            dma_sp(omc[si, l, d, h].rearrange("(c p) v -> p c v", p=128), CST[d][:, :, 0:256],
                                   ["CST%d" % d + sq_], [])
                            for cc in range(2):
                                dma_sp(omn[si, l, d, h, cc * 128:(cc + 1) * 128].rearrange("(p o) -> p o", o=1),
                                       CST[d][:, cc, 256:257], ["CST%d" % d + sq_], [])

                def F(i):
                    ds_ = []
                    for d in range(2):
                        r = d * 32 + h
                        tile = tl(i, d)
                        bn, bnk = stB[(i, d, "bn")]
                        d1, d1k = SM.get()
                        tt(d1[:], bn[:, 256:257], ECOL[:, tile, r:r + 1], ALU.max, [bnk, "ECOL"], [d1k])
                        ds_.append((d1, d1k, bn, bnk, tile))
                    for (d1, d1k, bn, bnk, tile) in ds_:
                        stt(d1[:], bn[:, 256:257], -1.0, d1[:], ALU.mult, ALU.max, [bnk, d1k], [d1k])
                    for (d1, d1k, bn, bnk, tile) in ds_:
                        recip(d1[:], d1[:], [d1k], [d1k])
                    for (d1, d1k, bn, bnk, tile) in ds_:
                        if tile not in written:
                            written.add(tile)
                            act(OSUM[:, tile, :], bn[:, 0:256], AF.Identity, [bnk, d1k], [osk(tile)], scale=d1[:, 0:1])
                        else:
                            stt(OSUM[:, tile, :], bn[:, 0:256], d1[:, 0:1], OSUM[:, tile, :], ALU.mult, ALU.add,
                                [bnk, d1k, osk(tile)], [osk(tile)])

                A(0)
                B(0)
                C(0)
                for i in range(n):
                    if i + 1 < n:
                        A(i + 1)
                    Dn(i)
                    E(i)
                    if i + 1 < n:
                        B(i + 1)
                        C(i + 1)
                    F(i)
            ck("mlE")
            head_epilogue(m_OG, hmT, h, "mOG")

    def merge_out(l, p):
        ada_chunk()
        for dc in range(8):
            cs = slice(dc * 128, (dc + 1) * 128)
            sA, sAk = slab([(w_brg[l][:, cs], 0), (w_brm[l][:, cs], 128)], 8, 256)
            sB, sBk = slab([(w_in[l][:, O_GG + dc * 128:O_GG + (dc + 1) * 128], 0),
                            (w_in[l][:, O_GM + dc * 128:O_GM + (dc + 1) * 128], 128)], 8, 256)
            for th in range(2):
                hs = slice(th * 512, (th + 1) * 512)
                byg, bygk = bank()
                bym, bymk = bank()
                bgg, bggk = bank()
                bgm, bgmk = bank()
                for kc in range(8):
                    mm(byg[:], sA[:, kc, 0:128], ogT[:, kc, hs], kc == 0, kc == 7, [sAk, "mixT"], [bygk], inc=(kc == 7))
                for kc in range(8):
                    mm(bym[:], sA[:, kc, 128:256], hmT[:, kc, hs], kc == 0, kc == 7, [sAk, "mixT"], [bymk], inc=(kc == 7))
                for kc in range(8):
                    mm(bgg[:], sB[:, kc, 0:128], hT[:, kc, hs], kc == 0, kc == 7, [sBk, hk(kc, th)], [bggk], inc=(kc == 7))
                for kc in range(8):
                    mm(bgm[:], sB[:, kc, 128:256], hT[:, kc, hs], kc == 0, kc == 7, [sBk, hk(kc, th)], [bgmk], inc=(kc == 7))
                t1, t1k = T5()
                act(t1[:], bgg[:], AF.Sigmoid, [bggk], [t1k])
                tt(t1[:], byg[:], t1[:], ALU.mult, [bygk, t1k], [t1k])
                t2, t2k = T5()
                act(t2[:], bgm[:], AF.Sigmoid, [bgmk], [t2k])
                tt(t2[:], bym[:], t2[:], ALU.mult, [bymk, t2k], [t2k])
                tt(MT[:, dc, hs], t1[:], t2[:], ALU.add, [t1k, t2k], ["MT%d" % th])
        for dc in range(8):
            so, sok = slab([(w_out[l][:, dc * 128:(dc + 1) * 128], 0)], 8, 128)
            for th in range(2):
                b, bk = bank()
                for kc in range(8):
                    mm(b[:], so[:, kc, :], MT[:, kc, th * 512:(th + 1) * 512], kc == 0, kc == 7, [sok, "MT%d" % th], [bk],
                       inc=(kc == 7))
                resid_update(b, bk, dc, th, adac(ADA, l, 5, dc, p), "ADA%d_5" % l)

    try:
        ck("setup", locals())
        S.barrier()
        for p in range(2):
            is_prompt = (p == 0)
            seqs = [(0, 2), (2, 2), (4, 2), (6, 2)] if is_prompt else [(0, 8)]
            pcfg = (is_prompt, seqs, p)
            for tile in range(8):
                stg = TA if tile % 2 == 0 else TB
                sk_ = "TA" if tile % 2 == 0 else "TB"
                dma_sp(stg, x_in[p][tile * 128:(tile + 1) * 128, :], [], [sk_])
                for half in range(2):
                    b, bk = bank()
                    for j in range(4):
                        dc = half * 4 + j
                        tr(b[:, j * 128:(j + 1) * 128], stg[:, dc * 128:(dc + 1) * 128], identf[:], [sk_, "identf"], [bk],
                           inc=(j == 3))
                    acp(xT[:, half * 4:(half + 1) * 4, tile * 128:(tile + 1) * 128],
                        b[:].rearrange("p (a t) -> p a t", a=4), [bk],
                        [xk(dc_, tile // 4) for dc_ in range(half * 4, half * 4 + 4)])
            if not is_prompt:
                for dc in range(8):
                    v = dc % 4
                    if dc < 4:
                        in1 = S4[:, v, 0:16].unsqueeze(2).to_broadcast([128, 16, 64])
                    else:
                        in1 = S4[:, v, :].unsqueeze(1).to_broadcast([128, 16, 64])
                    xv = xT[:, dc, :].rearrange("p (r c) -> p r c", r=16)
                    tt(xv, xv, in1, ALU.add, [xk(dc, 0), xk(dc, 1), "S4"], [xk(dc, 0), xk(dc, 1)])
            for dc in range(8):
                for th in range(2):
                    ts(hT[:, dc, th * 512:(th + 1) * 512], xT[:, dc, th * 512:(th + 1) * 512], adac(ONEP, 0, 1, dc, p),
                       adac(ADA, 0, 0, dc, p), ALU.mult, ALU.add, [xk(dc, th), "ONEP0_1", "ADA0_0"], [hk(dc, th)])
            ck("loadx%d" % p, locals())
            for l in range(2):
                S.barrier()
                ffn(l, 0, p)
                ck("ffn1_%d_%d" % (p, l), locals())
                layernorm(l, 0, (l, 4, 3, p))
                ck("ln0_%d_%d" % (p, l), locals())
                S.barrier()
                gla(l, pcfg)
                ck("gla_%d_%d" % (p, l), locals())
                S.barrier()
                mlstm(l, pcfg)
                ck("mlstm_%d_%d" % (p, l), locals())
                S.barrier()
                merge_out(l, p)
                ck("merge_%d_%d" % (p, l), locals())
                layernorm(l, 1, (l, 7, 6, p))
                ck("ln1_%d_%d" % (p, l), locals())
                S.barrier()
                ffn(l, 1, p)
                layernorm(l, 2, (1, 1, 0, p) if l == 0 else None)
                ck("ln2_%d_%d" % (p, l), locals())
            S.barrier()
            for tile in range(8):
                stg = TA if tile % 2 == 0 else TB
                sk_ = "TA" if tile % 2 == 0 else "TB"
                for half in range(2):
                    b, bk = bank()
                    for j in range(4):
                        dc = half * 4 + j
                        tr(b[:, j * 128:(j + 1) * 128], xT[:, dc, tile * 128:(tile + 1) * 128], identf[:],
                           [xk(dc, tile // 4), "identf"], [bk], inc=(j == 3))
                    acp(stg[:, half * 512:(half + 1) * 512], b[:], [bk], [sk_])
                dma_sp(y_out[p][tile * 128:(tile + 1) * 128, :], stg, [sk_], [])
    except _Stop:
        pass

    S.finish()
    S.run()
    sbuf_left = nc.sbuf_bytes_remaining
    st.close()
    S.sbuf_left = sbuf_left
    return nc, S, dump_aps


_CACHE = {}


def kernel(x_prompt, x_sample, c, state_gla_s, state_mlstm_c, state_mlstm_n, state_mlstm_m, c_ctx,
           w_ada, b_ada, ffn1_w_gate, ffn1_w_up, ffn1_w_down, w_in, w_decay, b_decay, w_conv, b_conv,
           f_bias, gla_norm_g, mlstm_norm_g, w_br_gla, w_br_mlstm, w_out,
           ffn2_w_gate, ffn2_w_up, ffn2_w_down, ln_g, ln_b):
    f = lambda a: np.ascontiguousarray(np.asarray(a, dtype=np.float32))
    if "nc" not in _CACHE:
        _CACHE["nc"] = build_program()[0]
    nc = _CACHE["nc"]
    shared = {
        "w_ada": f(w_ada), "b_ada": f(b_ada).reshape(2, 72, 128),
        "ffn1_w_gate": f(ffn1_w_gate), "ffn1_w_up": f(ffn1_w_up), "ffn1_w_down": f(ffn1_w_down),
        "ffn2_w_gate": f(ffn2_w_gate), "ffn2_w_up": f(ffn2_w_up), "ffn2_w_down": f(ffn2_w_down),
        "w_in": f(w_in), "w_decay": f(w_decay), "b_decay": f(b_decay),
        "w_conv": f(w_conv).reshape(96, 128), "b_conv": f(b_conv).reshape(32, 128),
        "f_bias": f(f_bias), "gla_norm_g": f(gla_norm_g), "mlstm_norm_g": f(mlstm_norm_g),
        "w_br_gla": f(w_br_gla), "w_br_mlstm": f(w_br_mlstm), "w_out": f(w_out),
        "ln_g": f(ln_g).reshape(48, 128), "ln_b": f(ln_b).reshape(48, 128),
    }
    xp = f(x_prompt)
    xs = f(x_sample)
    cc = f(c)
    cctx = f(c_ctx)
    in_maps = []
    for i in range(8):
        m = dict(shared)
        m["xp"] = xp[4 * i:4 * i + 4].reshape(T, D)
        m["xs"] = xs[i]
        m["cond"] = np.ascontiguousarray(np.concatenate([cctx.reshape(8, 128), cc[i].reshape(8, 128)], axis=0))
        m["gs0"] = f(state_gla_s[i])
        m["mc0"] = f(state_mlstm_c[i])
        m["mn0"] = f(state_mlstm_n[i])
        m["mm0"] = f(state_mlstm_m[i])
        in_maps.append(m)
    res = run_bass_kernel_spmd(nc, in_maps, core_ids=list(range(8)))
    r = res.results
    y_prompt = np.concatenate([r[i]["yp"].reshape(4, 256, D) for i in range(8)], axis=0)
    y_sample = np.stack([r[i]["ys"] for i in range(8)], axis=0)
    new_gla_s = np.concatenate([r[i]["ogs"] for i in range(8)], axis=0)
    new_mlstm_c = np.concatenate([r[i]["omc"] for i in range(8)], axis=0)
    new_mlstm_n = np.concatenate([r[i]["omn"] for i in range(8)], axis=0)
    new_mlstm_m = np.concatenate([r[i]["omm"] for i in range(8)], axis=0)
    return (y_prompt.astype(np.float32), y_sample.astype(np.float32), new_gla_s.astype(np.float32),
            new_mlstm_c.astype(np.float32), new_mlstm_n.astype(np.float32), new_mlstm_m.astype(np.float32))
```
